# Optimizing a Trainium2 kernel written in Bass

```python
import jax, jax.numpy as jnp
from jax import lax
import numpy as np

D_MODEL = 2048
BATCH = 4
SEQ = 2048
DEPTH = 4
DEC_BATCH = 128
DEC_SEQ = 4
PAST_LEN = 16384
PAGE_SIZE = 128

CHUNK = 128
D_A = D_MODEL // 2
D_B = D_MODEL // 2
D_C = D_MODEL // 2
G_A = D_A // CHUNK
K_CONV = 31
POOL_WINDOWS = (2, 4, 8, 16)
N_POOL_GROUPS = len(POOL_WINDOWS)
D_CG = D_C // N_POOL_GROUPS
D_CG_OUT = D_MODEL // N_POOL_GROUPS
POOL_BUF = max(POOL_WINDOWS) - 1
K_FFN = 3
D_FF = (((8 * D_MODEL) // 3 + 127) // 128) * 128
N_BRANCH = 3
N_IN = 2 * D_A + 2 * D_B + D_C + N_BRANCH * D_MODEL
EPS = 1e-6

kernel_name = 'hybrid_gmlp_conformer_pool_decoder_step'


def rmsnorm(x, g):
    x32 = x.astype(jnp.float32)
    y = x32 * lax.rsqrt(jnp.mean(x32 * x32, axis=-1, keepdims=True) + EPS)
    return (y * g.astype(jnp.float32)).astype(x.dtype)


def layernorm(x, g, b):
    x32 = x.astype(jnp.float32)
    mu = jnp.mean(x32, axis=-1, keepdims=True)
    var = jnp.mean(jnp.square(x32 - mu), axis=-1, keepdims=True)
    y = (x32 - mu) * lax.rsqrt(var + EPS)
    return (y * g.astype(jnp.float32) + b.astype(jnp.float32)).astype(x.dtype)


def causal_dwconv(x, buf, w, b):
    k = w.shape[0]
    xcat = jnp.concatenate([buf.astype(x.dtype), x], axis=1)
    y = lax.conv_general_dilated(xcat, w.astype(x.dtype)[:, None, :], window_strides=(1,),
                                 padding='VALID', dimension_numbers=('NWC', 'WIO', 'NWC'),
                                 feature_group_count=x.shape[-1])
    return y + b.astype(x.dtype), xcat[:, xcat.shape[1] - (k - 1):]


def chunk_spatial_gate(u, v, w_s, b_s):
    n, t, c = v.shape
    n_chunks = -(-t // CHUNK)
    pad = n_chunks * CHUNK - t
    vp = jnp.pad(v, ((0, 0), (0, pad), (0, 0))).reshape(n, n_chunks, CHUNK, G_A, c // G_A)
    mask = jnp.tril(jnp.ones((CHUNK, CHUNK), dtype=bool))
    w = jnp.where(mask[None], w_s, jnp.zeros_like(w_s))
    mixed = jnp.einsum('gts,bnsgc->bntgc', w, vp) + b_s.T[None, None, :, :, None]
    mixed = mixed.reshape(n, n_chunks * CHUNK, c)[:, :t]
    return u * mixed


def multiscale_pool(p, buf, start):
    n, t, c = p.shape
    L = buf.shape[1]
    xcat = jnp.concatenate([buf.astype(p.dtype), p], axis=1)
    cs = jnp.concatenate([jnp.zeros((n, 1, c), jnp.float32),
                          jnp.cumsum(xcat.astype(jnp.float32), axis=1)], axis=1)
    pos = start + jnp.arange(t, dtype=jnp.int32)
    outs = []
    for gi, w in enumerate(POOL_WINDOWS):
        lo, hi = gi * D_CG, (gi + 1) * D_CG
        s = cs[:, L + 1:L + 1 + t, lo:hi] - cs[:, L + 1 - w:L + 1 - w + t, lo:hi]
        cnt = jnp.minimum(pos + 1, w).astype(jnp.float32)[None, :, None]
        outs.append(s / cnt)
    pooled = jnp.concatenate(outs, axis=-1) - p.astype(jnp.float32)
    return pooled.astype(p.dtype), xcat[:, xcat.shape[1] - L:]


def hybrid_layer(x, c, conv_buf, pool_buf, ffn_buf, start, ada_w, ada_b, g_pre_mix, g_post_mix,
                 g_pre_ffn, g_post_ffn, w_in, b_gate, ln_v_g, ln_v_b, w_spatial, b_spatial, w_a_out,
                 w_dwconv, b_dwconv, ln_conv_g, ln_conv_b, w_b_out, w_pool_grp, pool_scale, w_o,
                 w_up, w_ffn_conv, b_ffn_conv, w_down):
    n, t, _ = x.shape
    mod = (jax.nn.silu(c) @ ada_w + ada_b)[:, None, :]
    shift_m, scale_m, gate_m, shift_f, scale_f, gate_f = jnp.split(mod, 6, axis=-1)

    h = rmsnorm(x, g_pre_mix) * (1 + scale_m) + shift_m
    proj = h @ w_in
    u, v, glu, pin, gates_pre = jnp.split(
        proj, [D_A, 2 * D_A, 2 * D_A + 2 * D_B, 2 * D_A + 2 * D_B + D_C], axis=-1)

    v_n = layernorm(jax.nn.gelu(v), ln_v_g, ln_v_b)
    y_a = chunk_spatial_gate(jax.nn.gelu(u), v_n, w_spatial, b_spatial) @ w_a_out

    ga, gb = jnp.split(glu, 2, axis=-1)
    xb = ga * jax.nn.sigmoid(gb)
    yb, new_conv = causal_dwconv(xb, conv_buf, w_dwconv, b_dwconv)
    y_b = jax.nn.silu(layernorm(yb, ln_conv_g, ln_conv_b)) @ w_b_out

    pooled, new_pool = multiscale_pool(pin, pool_buf, start)
    y_c = jnp.einsum('btgc,gcd->btgd', pooled.reshape(n, t, N_POOL_GROUPS, D_CG), w_pool_grp)
    y_c = y_c.reshape(n, t, D_MODEL) * pool_scale

    g_a, g_b, g_c = jnp.split(jax.nn.sigmoid(gates_pre + b_gate), N_BRANCH, axis=-1)
    mix = (g_a * y_a + g_b * y_b + g_c * y_c) @ w_o
    x = x + gate_m * rmsnorm(mix, g_post_mix)

    h = rmsnorm(x, g_pre_ffn) * (1 + scale_f) + shift_f
    gp, val = jnp.split(h @ w_up, 2, axis=-1)
    gc, new_ffn = causal_dwconv(gp, ffn_buf, w_ffn_conv, b_ffn_conv)
    y = (jax.nn.gelu(gc) * val) @ w_down
    x = x + gate_f * rmsnorm(y, g_post_ffn)
    return x, new_conv, new_pool, new_ffn, v_n


def setup_inputs(seed: int = 0) -> dict:
    key = jax.random.key(seed)
    ks = jax.random.split(key, 40)

    def nrm(k, shape, scale):
        return jax.random.normal(k, shape, jnp.float32) * scale

    def gain(k, shape):
        return 1.0 + nrm(k, shape, 0.05)

    return {
        'x_prompt': nrm(ks[0], (BATCH, SEQ, D_MODEL), 1.0),
        'x_sample': nrm(ks[1], (DEC_BATCH, DEC_SEQ, D_MODEL), 1.0),
        'c_prompt': nrm(ks[2], (BATCH, D_MODEL), 1.0),
        'c_sample': nrm(ks[3], (DEC_BATCH, D_MODEL), 1.0),
        'state_conv': nrm(ks[4], (DEPTH, DEC_BATCH, K_CONV - 1, D_B), 0.5),
        'state_pool': nrm(ks[5], (DEPTH, DEC_BATCH, POOL_BUF, D_C), 1.0),
        'state_ffn_conv': nrm(ks[6], (DEPTH, DEC_BATCH, K_FFN - 1, D_FF), 1.0),
        'ada_w': nrm(ks[7], (DEPTH, D_MODEL, 6 * D_MODEL), 0.5 * D_MODEL ** -0.5),
        'ada_b': nrm(ks[8], (DEPTH, 6 * D_MODEL), 0.02),
        'g_pre_mix': gain(ks[9], (DEPTH, D_MODEL)),
        'g_post_mix': gain(ks[10], (DEPTH, D_MODEL)),
        'g_pre_ffn': gain(ks[11], (DEPTH, D_MODEL)),
        'g_post_ffn': gain(ks[12], (DEPTH, D_MODEL)),
        'w_in': nrm(ks[13], (DEPTH, D_MODEL, N_IN), D_MODEL ** -0.5),
        'b_gate': nrm(ks[14], (DEPTH, N_BRANCH * D_MODEL), 0.02),
        'ln_v_g': gain(ks[15], (DEPTH, D_A)),
        'ln_v_b': nrm(ks[16], (DEPTH, D_A), 0.02),
        'w_spatial': nrm(ks[17], (DEPTH, G_A, CHUNK, CHUNK), CHUNK ** -0.5),
        'b_spatial': 1.0 + nrm(ks[18], (DEPTH, G_A, CHUNK), 0.1),
        'w_a_out': nrm(ks[19], (DEPTH, D_A, D_MODEL), D_A ** -0.5),
        'w_dwconv': nrm(ks[20], (DEPTH, K_CONV, D_B), K_CONV ** -0.5),
        'b_dwconv': nrm(ks[21], (DEPTH, D_B), 0.02),
        'ln_conv_g': gain(ks[22], (DEPTH, D_B)),
        'ln_conv_b': nrm(ks[23], (DEPTH, D_B), 0.02),
        'w_b_out': nrm(ks[24], (DEPTH, D_B, D_MODEL), D_B ** -0.5),
        'w_pool_grp': nrm(ks[25], (DEPTH, N_POOL_GROUPS, D_CG, D_CG_OUT), D_CG ** -0.5),
        'pool_scale': gain(ks[26], (DEPTH, D_MODEL)),
        'w_o': nrm(ks[27], (DEPTH, D_MODEL, D_MODEL), D_MODEL ** -0.5),
        'w_up': nrm(ks[28], (DEPTH, D_MODEL, 2 * D_FF), D_MODEL ** -0.5),
        'w_ffn_conv': nrm(ks[29], (DEPTH, K_FFN, D_FF), K_FFN ** -0.5),
        'b_ffn_conv': nrm(ks[30], (DEPTH, D_FF), 0.02),
        'w_down': nrm(ks[31], (DEPTH, D_FF, D_MODEL), D_FF ** -0.5),
    }


def reference(x_prompt, x_sample, c_prompt, c_sample, state_conv, state_pool, state_ffn_conv,
              ada_w, ada_b, g_pre_mix, g_post_mix, g_pre_ffn, g_post_ffn, w_in, b_gate,
              ln_v_g, ln_v_b, w_spatial, b_spatial, w_a_out, w_dwconv, b_dwconv, ln_conv_g,
              ln_conv_b, w_b_out, w_pool_grp, pool_scale, w_o, w_up, w_ffn_conv, b_ffn_conv,
              w_down):
    x_p, x_s = x_prompt, x_sample
    conv_p, conv_s, pool_p, pool_s, ffn_p, ffn_s, v_s = [], [], [], [], [], [], []
    for l in range(DEPTH):
        lw = dict(ada_w=ada_w[l], ada_b=ada_b[l], g_pre_mix=g_pre_mix[l], g_post_mix=g_post_mix[l],
                  g_pre_ffn=g_pre_ffn[l], g_post_ffn=g_post_ffn[l], w_in=w_in[l], b_gate=b_gate[l],
                  ln_v_g=ln_v_g[l], ln_v_b=ln_v_b[l], w_spatial=w_spatial[l], b_spatial=b_spatial[l],
                  w_a_out=w_a_out[l], w_dwconv=w_dwconv[l], b_dwconv=b_dwconv[l],
                  ln_conv_g=ln_conv_g[l], ln_conv_b=ln_conv_b[l], w_b_out=w_b_out[l],
                  w_pool_grp=w_pool_grp[l], pool_scale=pool_scale[l], w_o=w_o[l], w_up=w_up[l],
                  w_ffn_conv=w_ffn_conv[l], b_ffn_conv=b_ffn_conv[l], w_down=w_down[l])
        zc = jnp.zeros((x_p.shape[0], K_CONV - 1, D_B), x_p.dtype)
        zp = jnp.zeros((x_p.shape[0], POOL_BUF, D_C), x_p.dtype)
        zf = jnp.zeros((x_p.shape[0], K_FFN - 1, D_FF), x_p.dtype)
        x_p, cp, pp, fp, _ = hybrid_layer(x_p, c_prompt, zc, zp, zf, 0, **lw)
        x_s, cs_, ps_, fs_, vs_ = hybrid_layer(x_s, c_sample, state_conv[l], state_pool[l],
                                               state_ffn_conv[l], PAST_LEN, **lw)
        conv_p.append(cp); pool_p.append(pp); ffn_p.append(fp)
        conv_s.append(cs_); pool_s.append(ps_); ffn_s.append(fs_); v_s.append(vs_)
    return (x_p, x_s, jnp.stack(conv_p), jnp.stack(conv_s), jnp.stack(pool_p), jnp.stack(pool_s),
            jnp.stack(ffn_p), jnp.stack(ffn_s), jnp.stack(v_s))
```

```python
import numpy as np
import concourse.bass as bass
import concourse.mybir as mybir
from concourse.bass_utils import run_bass_kernel_spmd

F32 = mybir.dt.float32
BF16 = mybir.dt.bfloat16
AF = mybir.ActivationFunctionType
ALU = mybir.AluOpType

D = 2048; DA = 1024; DFF = 5504; NIN = 11264; DEPTH = 4
KC = 16; NFC = 43
EPS = 1e-6
NCORES = 8
HALO = 128
NPROMPT = 1152
NS = 64
SUPERS = [(0, 384, True), (384, 384, False), (768, 384, False)]
NPMAX = 384
TTMAX = NPMAX + NS
GATE0 = 2 * DA + 2 * DA + DA

O_GPM = 0; O_GQM = 16; O_GPF = 32; O_GQF = 48; O_BG = 64; O_PS = 112; O_BDW = 128
O_LCG = 136; O_LCB = 144; O_WDW = 152; O_WFC = 400; O_BFC = 529; O_ADB = 572; NPP = 668

ENGS = ("pe", "act", "dve", "pool", "sp")


class Plan:
    def __init__(self):
        self.ops = {e: [] for e in ENGS}
        self.last_write = {}
        self.readers = {}
        self.lane_count = {}
        self.lane_last = {}
        self.inherit = {}
        self.carry = {}

    @staticmethod
    def _merge(dst, tok):
        ch = tok[:2]
        old = dst.get(ch)
        if old is None or old[2] < tok[2]:
            dst[ch] = tok

    def _init_key(self, k):
        if k not in self.readers:
            name = k[0] if isinstance(k, tuple) else k
            self.readers[k] = dict(self.inherit.get(name, {}))

    def fence(self, dying, newnames):
        merged = {}
        for k, t in self.last_write.items():
            name = k[0] if isinstance(k, tuple) else k
            if name in dying:
                self._merge(merged, t)
        for k, rd in self.readers.items():
            name = k[0] if isinstance(k, tuple) else k
            if name in dying:
                for t in rd.values():
                    self._merge(merged, t)
        for t in merged.values():
            self._merge(self.carry, t)
        merged = self.carry
        for n in newnames:
            self.inherit[n] = dict(merged)
        for k in list(self.readers.keys()):
            name = k[0] if isinstance(k, tuple) else k
            if name in newnames:
                for t in merged.values():
                    self._merge(self.readers[k], t)

    def add(self, eng, emit, reads=(), writes=(), lane=None, serialize=True):
        idx = len(self.ops[eng])
        if lane is not None:
            cnt = self.lane_count.get(lane, 0) + 1
            self.lane_count[lane] = cnt
            tok = ("d", lane, cnt)
        else:
            tok = ("c", eng, idx)
        deps = set()
        if lane is not None and serialize and lane in self.lane_last:
            deps.add(self.lane_last[lane])
        for k in reads:
            self._init_key(k)
            t = self.last_write.get(k)
            if t is not None:
                deps.add(t)
        for k in writes:
            self._init_key(k)
            t = self.last_write.get(k)
            if t is not None:
                deps.add(t)
            for t in self.readers[k].values():
                deps.add(t)
        final = []
        for t in deps:
            if t == tok:
                continue
            if t[0] == "c" and lane is None and t[1] == eng:
                if eng == "pe":
                    continue
                is_raw = any(self.last_write.get(k) == t for k in reads)
                if not is_raw:
                    continue
            final.append(t)
            if t[0] == "c":
                self.ops[t[1]][t[2]]["inc"] = True
        self.ops[eng].append(dict(emit=emit, deps=final, inc=False, lane=lane))
        if lane is not None:
            self.lane_last[lane] = tok
        for k in reads:
            self._merge(self.readers[k], tok)
        for k in writes:
            self.last_write[k] = tok
            self.readers[k] = {}
        return tok

    def make_runner(self, sems, lane_sems, final_waits=()):
        counts = {}
        for e in ENGS:
            c = 0
            lst = []
            for op in self.ops[e]:
                if op["inc"] and op["lane"] is None:
                    c += 1
                lst.append(c)
            counts[e] = lst

        def tokval(t):
            if t[0] == "c":
                return sems[t[1]], counts[t[1]][t[2]]
            return lane_sems[t[1]], 16 * t[2]

        def run(e, engine):
            known = {}
            for op in self.ops[e]:
                for t in op["deps"]:
                    s, v = tokval(t)
                    if known.get(id(s), -1) >= v:
                        continue
                    known[id(s)] = v
                    engine.wait_ge(s, v)
                ins = op["emit"](engine)
                if op["lane"] is not None:
                    ins.then_inc(lane_sems[op["lane"]], 16)
                elif op["inc"]:
                    ins.then_inc(sems[e], 1)
            if e == "sp":
                for t in final_waits:
                    s, v = tokval(t)
                    engine.wait_ge(s, v)
        return run


class Arena:
    def __init__(self, nc):
        self.nc = nc
        self.off = (nc.sbuf_base + 63) // 64 * 64
        self.top = nc.sbuf_top
        self.live = []
        self.uid = 0
        self.peak = self.off

    def alloc(self, name, shape, dt):
        nb = int(np.prod(shape[1:])) * (4 if dt == F32 else 2)
        nb = (nb + 63) // 64 * 64
        o = self.off
        self.off += nb
        self.peak = max(self.peak, self.off)
        assert self.off <= self.top, f"SBUF overflow at {name}: {self.off} > {self.top}"
        self.uid += 1
        t = self.nc.alloc_sbuf_tensor_at(f"{name}_{self.uid}", shape, dt, offset=o)
        self.live.append(name)
        return t

    def mark(self):
        return (self.off, len(self.live))

    def reset(self, m):
        dying = set(self.live[m[1]:])
        self.off = m[0]
        del self.live[m[1]:]
        return dying


def build_program():
    nc = bass.Bass("TRN2", target_bir_lowering=False)

    def din(name, shape):
        return nc.dram_tensor(name, list(shape), F32, kind="ExternalInput").ap()

    def dout(name, shape):
        return nc.dram_tensor(name, list(shape), F32, kind="ExternalOutput").ap()

    xp = din("xp", [D, NPROMPT]); xs = din("xs", [D, NS]); cT = din("cT", [D, 17])
    hmask = din("hmask", [128, 1]); pcnt = din("pcnt", [128, 64]); tril = din("tril", [128, 128])
    bdmask = din("bdmask", [64, 64])
    st_conv = din("st_conv", [DEPTH, DA, 16 * 30]); st_pool = din("st_pool", [DEPTH, DA, 16 * 15])
    st_ffn = din("st_ffn", [DEPTH, DFF, 16 * 2])
    pp = din("pp", [DEPTH, 128, NPP])
    lnvg = din("lnvg", [DEPTH, DA]); lnvb = din("lnvb", [DEPTH, DA])
    wsT = din("wsT", [DEPTH, 128, 8 * 128]); bsp = din("bsp", [DEPTH, 8 * 128])
    ws4 = din("ws4", [DEPTH, 64, 32]); bs4 = din("bs4", [DEPTH, 8 * 64])
    ada_w = din("ada_w", [DEPTH, D, 6 * D]); w_in = din("w_in", [DEPTH, D, NIN])
    w_a_out = din("w_a_out", [DEPTH, DA, D]); w_b_out = din("w_b_out", [DEPTH, DA, D])
    w_pool = din("w_pool", [DEPTH, 4, 256, 512]); w_o = din("w_o", [DEPTH, D, D])
    w_up = din("w_up", [DEPTH, D, 2 * DFF]); w_down = din("w_down", [DEPTH, DFF, D])

    yp = dout("yp", [D, 1024]); ys = dout("ys", [D, NS])
    o_conv_p = dout("o_conv_p", [DEPTH, DA, 30]); o_conv_s = dout("o_conv_s", [DEPTH, DA, 16 * 30])
    o_pool_p = dout("o_pool_p", [DEPTH, DA, 15]); o_pool_s = dout("o_pool_s", [DEPTH, DA, 16 * 15])
    o_ffn_p = dout("o_ffn_p", [DEPTH, DFF, 2]); o_ffn_s = dout("o_ffn_s", [DEPTH, DFF, 16 * 2])
    o_v_s = dout("o_v_s", [DEPTH, NS, DA])

    modsc = nc.dram_tensor("modsc", [DEPTH, 128, 96 * 17], F32).ap()
    P = Plan()
    A = Arena(nc)
    out_toks = []

    X = A.alloc("X", [128, KC, TTMAX], F32)
    H = A.alloc("H", [128, KC, TTMAX], BF16)
    MODC = A.alloc("MODC", [128, 96, 17], F32)
    PPt = A.alloc("PP", [128, DEPTH, NPP], F32)
    NSLOT = 3
    SLOT_ELEMS = 8192
    WS = [A.alloc(f"WS{i}", [128, SLOT_ELEMS], BF16) for i in range(NSLOT)]
    RSTD = A.alloc("RSTD", [128, TTMAX], F32)
    NSQ = 3
    SQ = [A.alloc(f"SQ{i}", [128, 512], BF16) for i in range(NSQ)]
    NTMP = 2
    TMPF = [A.alloc(f"TMPF{i}", [128, 512], F32) for i in range(NTMP)]
    NSIG = 2
    SIG = [A.alloc(f"SIG{i}", [128, 512], F32) for i in range(NSIG)]
    HCONV = A.alloc("HCONV", [128, DEPTH, 8, 30], F32)
    HPOOL = A.alloc("HPOOL", [128, DEPTH, 8, 15], F32)
    HFFN = A.alloc("HFFN", [128, DEPTH, NFC, 2], F32)
    ONES = A.alloc("ONES", [128, 128], BF16)
    TRIL = A.alloc("TRIL", [128, 128], F32)
    BDM = A.alloc("BDM", [64, 64], F32)
    HMASK = A.alloc("HMASK", [128, 1], F32)
    PCNT = A.alloc("PCNT", [128, 64], F32)
    MV = A.alloc("MV", [128, 32], F32)
    ada_mark = A.mark()
    CT = A.alloc("CT", [128, KC, 17], F32)
    SC = A.alloc("SC", [128, KC, 17], BF16)

    from contextlib import ExitStack
    es = ExitStack()
    PSB = [es.enter_context(nc.psum_tensor(f"psb{i}", [128, 512], F32)) for i in range(8)]
    sems = {e: es.enter_context(nc.semaphore(f"s_{e}")) for e in ENGS}
    NMISC = 8
    lanes = {}
    for i in range(NSLOT):
        lanes[f"w{i}"] = es.enter_context(nc.semaphore(f"l_w{i}"))
    for i in range(NMISC):
        lanes[f"m{i}"] = es.enter_context(nc.semaphore(f"l_m{i}"))

    st = dict(misc=0, slot=0, bank=0, sq=0, tmp=0, sig=0)

    def misc_lane():
        st["misc"] = (st["misc"] + 1) % NMISC
        return f"m{st['misc']}"

    def dma(out, in_, reads=(), writes=(), eng="sp"):
        return P.add(eng, lambda e: e.dma_start(out=out, in_=in_), reads=reads, writes=writes,
                     lane=misc_lane())

    def next_bank():
        b = st["bank"]
        st["bank"] = (b + 1) % 6
        return b

    def next_sq():
        st["sq"] = (st["sq"] + 1) % NSQ
        return st["sq"]

    def next_tmp():
        st["tmp"] = (st["tmp"] + 1) % NTMP
        return st["tmp"]

    def next_sig():
        st["sig"] = (st["sig"] + 1) % NSIG
        return st["sig"]

    def act(out, in_, func, reads, writes, bias=0.0, scale=1.0):
        return P.add("act", lambda e: e.activation(out=out, in_=in_, func=func, bias=bias, scale=scale),
                     reads=reads, writes=writes)

    def tt(out, in0, in1, op, reads, writes, eng="dve"):
        return P.add(eng, lambda e: e.tensor_tensor(out=out, in0=in0, in1=in1, op=op),
                     reads=reads, writes=writes)

    def ts(out, in0, s1, s2, op0, op1, reads, writes, eng="dve"):
        if s2 is None:
            return P.add(eng, lambda e: e.tensor_scalar(out=out, in0=in0, scalar1=s1, scalar2=None, op0=op0),
                         reads=reads, writes=writes)
        return P.add(eng, lambda e: e.tensor_scalar(out=out, in0=in0, scalar1=s1, scalar2=s2, op0=op0, op1=op1),
                     reads=reads, writes=writes)

    def stt(out, in0, scalar, in1, op0, op1, reads, writes, eng="dve"):
        return P.add(eng, lambda e: e.scalar_tensor_tensor(out=out, in0=in0, scalar=scalar, in1=in1,
                                                            op0=op0, op1=op1), reads=reads, writes=writes)

    def copy(out, in_, reads, writes, eng="dve"):
        return P.add(eng, lambda e: e.tensor_copy(out, in_), reads=reads, writes=writes)

    def memset(ap, val, writes, eng="dve"):
        return P.add(eng, lambda e: e.memset(ap, val), writes=writes)

    def mm(ps_ap, lhsT, rhs, start, stop, reads, bank):
        return P.add("pe", lambda e: e.matmul(ps_ap, lhsT=lhsT, rhs=rhs, start=start, stop=stop),
                     reads=reads, writes=[("ps", bank)])

    def load_unit(pieces):
        s = st["slot"]
        st["slot"] = (s + 1) % NSLOT
        views = []
        off = 0
        for pi, (src, kc) in enumerate(pieces):
            ncols = src.shape[1]
            v = WS[s][:, off:off + kc * ncols].rearrange("p (k c) -> p k c", c=ncols)
            srcv = src.rearrange("(k p) c -> p k c", p=128)
            P.add("pool", lambda e, v=v, srcv=srcv: e.dma_start(out=v, in_=srcv), writes=[("WS", s, pi)],
                  lane=f"w{s}", serialize=False)
            views.append(v)
            off += kc * ncols
        assert off <= SLOT_ELEMS
        return s, views

    def chain(ps_ap, bank, lhs_list, rhs_list, reads, first=True, last=True):
        n = len(lhs_list)
        for k in range(n):
            mm(ps_ap, lhs_list[k], rhs_list[k], start=(first and k == 0), stop=(last and k == n - 1),
               reads=reads, bank=bank)

    dma(PPt[:], pp.rearrange("l p c -> p l c"), writes=["PP"])
    dma(TRIL[:], tril, writes=["TRIL"])
    dma(BDM[:], bdmask, writes=["BDM"])
    dma(HMASK[:], hmask, writes=["HMASK"])
    dma(PCNT[:], pcnt, writes=["PCNT"])
    dma(CT[:], cT.rearrange("(k p) j -> p k j", p=128), writes=["CT"])
    memset(ONES[:], 1.0, ["ONES"])
    memset(HCONV[:], 0.0, ["HCONV"])
    memset(HPOOL[:], 0.0, ["HPOOL"])
    memset(HFFN[:], 0.0, ["HFFN"])
    act(SC[:], CT[:], AF.Silu, ["CT"], ["SC"])

    for l in range(DEPTH):
        for u in range(24):
            s, (wv,) = load_unit([(ada_w[l][:, u * 512:(u + 1) * 512], KC)])
            for m in range(4):
                mi = u * 4 + m
                b = next_bank()
                chain(PSB[b][:, 0:17], b, [wv[:, k, m * 128:(m + 1) * 128] for k in range(KC)],
                      [SC[:, k, :] for k in range(KC)], reads=[("WS", s, 0), ("WS", s, 1), "SC"])
                act(MODC[:, mi, :], PSB[b][:, 0:17], AF.Identity, [("ps", b), "PP"], ["MODC"],
                    bias=PPt[:, l, O_ADB + mi:O_ADB + mi + 1])
        for (c0, goff) in ((16, O_GPM), (64, O_GPF)):
            ts(MODC[:, c0:c0 + 16, :], MODC[:, c0:c0 + 16, :], 1.0, None, ALU.add, None,
               ["MODC"], ["MODC"])
        for (c0, goff) in ((16, O_GPM), (64, O_GPF), (32, O_GQM), (80, O_GQF)):
            tt(MODC[:, c0:c0 + 16, :], MODC[:, c0:c0 + 16, :],
               PPt[:, l, goff:goff + 16].unsqueeze(2).to_broadcast([128, 16, 17]), ALU.mult,
               ["MODC", "PP"], ["MODC"])
        dma(modsc[l], MODC[:].rearrange("p a b -> p (a b)"), reads=["MODC"], writes=[("MODSC", l)])

    dying = A.reset(ada_mark)
    P.fence(dying, [])
    base_mark = A.mark()

    for si, (P0, NP, HAS_S) in enumerate(SUPERS):
        TT = NP + (NS if HAS_S else 0)
        first_super = (si == 0)
        last_super = (si == len(SUPERS) - 1)
        pblocks = []
        o = 0
        while o < NP:
            n = min(512, NP - o)
            pblocks.append((o, n))
            o += n
        blocks = [(o_, n_, False) for (o_, n_) in pblocks] + ([(NP, NS, True)] if HAS_S else [])
        mblocks = [[o_, n_, [(o_, n_, False)]] for (o_, n_) in pblocks]
        if HAS_S:
            if mblocks[-1][1] + NS <= 512:
                mblocks[-1][1] += NS
                mblocks[-1][2].append((NP, NS, True))
            else:
                mblocks.append([NP, NS, [(NP, NS, True)]])
        assert len(mblocks) <= 2
        ntile = NP // 128

        for k in range(KC):
            dma(X[:, k, 0:NP], xp[k * 128:(k + 1) * 128, P0:P0 + NP], writes=[("X", k)])
            if HAS_S:
                dma(X[:, k, NP:NP + NS], xs[k * 128:(k + 1) * 128, :], writes=[("X", k)])

        def s3(ap):
            return ap.rearrange("p (s t) -> p s t", t=4)

        def bc_s(ap16):
            return ap16.unsqueeze(2).to_broadcast([128, 16, 4])

        def rstd_from(bank, o_, n_, scale):
            act(RSTD[:, o_:o_ + n_], PSB[bank][:, 0:n_], AF.Sqrt, [("ps", bank)], ["RSTD"],
                bias=EPS, scale=scale)
            P.add("dve", lambda e: e.reciprocal(out=RSTD[:, o_:o_ + n_], in_=RSTD[:, o_:o_ + n_]),
                  reads=["RSTD"], writes=["RSTD"])

        def prenorm(l, a_off, b_off):
            for bi_, (O_, N_, subs_) in enumerate(mblocks):
                b = 6 + bi_
                for k in range(KC):
                    q = next_sq()
                    act(SQ[q][:, 0:N_], X[:, k, O_:O_ + N_], AF.Square, [("X", k)], [("SQ", q)])
                    mm(PSB[b][:, 0:N_], ONES[:], SQ[q][:, 0:N_], start=(k == 0), stop=(k == KC - 1),
                       reads=["ONES", ("SQ", q)], bank=b)
                rstd_from(b, O_, N_, 1.0 / D)
            for (o_, n_, is_s) in blocks:
                for k in range(KC):
                    t_ = next_tmp()
                    tt(TMPF[t_][:, 0:n_], X[:, k, o_:o_ + n_], RSTD[:, o_:o_ + n_], ALU.mult,
                       [("X", k), "RSTD"], [("TMPF", t_)])
                    if not is_s:
                        act(H[:, k, o_:o_ + n_], TMPF[t_][:, 0:n_], AF.Identity,
                            [("TMPF", t_), "MODC"], [("H", k)],
                            bias=MODC[:, b_off + k, 0:1], scale=MODC[:, a_off + k, 0:1])
                    else:
                        tt(s3(TMPF[t_][:, 0:n_]), s3(TMPF[t_][:, 0:n_]), bc_s(MODC[:, a_off + k, 1:17]),
                           ALU.mult, [("TMPF", t_), "MODC"], [("TMPF", t_)])
                        tt(s3(H[:, k, o_:o_ + n_]), s3(TMPF[t_][:, 0:n_]), bc_s(MODC[:, b_off + k, 1:17]),
                           ALU.add, [("TMPF", t_), "MODC"], [("H", k)])

        def postnorm(l, g_off, YMO):
            for bi, (o_, n_, is_s) in enumerate(blocks):
                for mi in range(KC):
                    t_ = next_tmp()
                    tt(TMPF[t_][:, 0:n_], YMO[:, mi, o_:o_ + n_], RSTD[:, o_:o_ + n_], ALU.mult,
                       [("YMO", mi), "RSTD"], [("TMPF", t_)])
                    if not is_s:
                        stt(X[:, mi, o_:o_ + n_], TMPF[t_][:, 0:n_], MODC[:, g_off + mi, 0:1],
                            X[:, mi, o_:o_ + n_], ALU.mult, ALU.add,
                            [("TMPF", t_), "MODC", ("X", mi)], [("X", mi)])
                    else:
                        tt(s3(TMPF[t_][:, 0:n_]), s3(TMPF[t_][:, 0:n_]), bc_s(MODC[:, g_off + mi, 1:17]),
                           ALU.mult, [("TMPF", t_), "MODC"], [("TMPF", t_)])
                        tt(X[:, mi, o_:o_ + n_], X[:, mi, o_:o_ + n_], TMPF[t_][:, 0:n_], ALU.add,
                           [("TMPF", t_), ("X", mi)], [("X", mi)])

        def out_stage(l, YMO, units, rhs_fn, nk_total_fn):
            pass

        for l in range(DEPTH):
            m0 = A.mark()
            dma(MODC[:].rearrange("p a b -> p (a b)"), modsc[l], reads=[("MODSC", l)], writes=["MODC"])
            MIXACC = A.alloc("MIXACC", [128, KC, TT], BF16)
            P.fence(set(), ["MIXACC"])
            prenorm(l, 16, 0)
            mA = A.mark()

            def gated_out(l, src_buf, src_key, kc_src, w_src_fn, gate_col0, first, scale_off=None,
                          rhs_chunk_fn=None):
                for u in range(8):
                    c0 = u * 256
                    wsrc, kcs = w_src_fn(c0)
                    s, (wy, wg) = load_unit([(wsrc, kcs),
                                              (w_in[l][:, gate_col0 + c0:gate_col0 + c0 + 256], KC)])
                    for m in range(2):
                        mi = u * 2 + m
                        for (o_, n_, _sb) in mblocks:
                            b1 = next_bank()
                            chain(PSB[b1][:, 0:n_], b1, [wy[:, k, m * 128:(m + 1) * 128] for k in range(kcs)],
                                  [rhs_chunk_fn(mi, k, o_, n_) for k in range(kcs)],
                                  reads=[("WS", s, 0), ("WS", s, 1)] + [(src_key, kk) for kk in range(8)])
                            b2 = next_bank()
                            chain(PSB[b2][:, 0:n_], b2, [wg[:, k, m * 128:(m + 1) * 128] for k in range(KC)],
                                  [H[:, k, o_:o_ + n_] for k in range(KC)],
                                  reads=[("WS", s, 0), ("WS", s, 1)] + [("H", kk) for kk in range(KC)])
                            g = next_sig()
                            act(SIG[g][:, 0:n_], PSB[b2][:, 0:n_], AF.Sigmoid, [("ps", b2), "PP"], [("SIG", g)],
                                bias=PPt[:, l, O_BG + (gate_col0 - GATE0) // 128 + mi:
                                         O_BG + (gate_col0 - GATE0) // 128 + mi + 1])
                            if first:
                                tt(MIXACC[:, mi, o_:o_ + n_], PSB[b1][:, 0:n_], SIG[g][:, 0:n_], ALU.mult,
                                   [("ps", b1), ("SIG", g)], [("MIXACC", mi)])
                            else:
                                t_ = next_tmp()
                                if scale_off is None:
                                    tt(TMPF[t_][:, 0:n_], PSB[b1][:, 0:n_], SIG[g][:, 0:n_], ALU.mult,
                                       [("ps", b1), ("SIG", g)], [("TMPF", t_)])
                                else:
                                    stt(TMPF[t_][:, 0:n_], PSB[b1][:, 0:n_],
                                        PPt[:, l, scale_off + mi:scale_off + mi + 1], SIG[g][:, 0:n_],
                                        ALU.mult, ALU.mult, [("ps", b1), ("SIG", g), "PP"], [("TMPF", t_)])
                                tt(MIXACC[:, mi, o_:o_ + n_], MIXACC[:, mi, o_:o_ + n_], TMPF[t_][:, 0:n_],
                                   ALU.add, [("MIXACC", mi), ("TMPF", t_)], [("MIXACC", mi)])

            NTT = ntile + (1 if HAS_S else 0)
            VN = A.alloc("VN", [128, NTT, DA], BF16)
            U = A.alloc("U", [128, 8, TT], BF16)
            mA1 = A.mark()
            VF = A.alloc("VF", [128, NTT, DA], F32)
            LNG = A.alloc("LNG", [128, DA], F32)
            LNB = A.alloc("LNB", [128, DA], F32)
            P.fence(set(), ["VN", "U", "VF", "LNG", "LNB"])
            dma(LNG[:], lnvg[l:l + 1, :].partition_broadcast(128), writes=["LNG"])
            dma(LNB[:], lnvb[l:l + 1, :].partition_broadcast(128), writes=["LNB"])
            tiles = [(t_ * 128, 128, t_) for t_ in range(ntile)] + ([(NP, NS, ntile)] if HAS_S else [])
            for uu in range(2):
                s, (wv,) = load_unit([(w_in[l][:, DA + uu * 512:DA + (uu + 1) * 512], KC)])
                for (o_, n_, ti) in tiles:
                    b = next_bank()
                    chain(PSB[b][0:n_, 0:512], b, [H[:, k, o_:o_ + n_] for k in range(KC)],
                          [wv[:, k, :] for k in range(KC)], reads=[("WS", s, 0), ("WS", s, 1)] + [("H", kk) for kk in range(KC)])
                    act(VF[0:n_, ti, uu * 512:(uu + 1) * 512], PSB[b][0:n_, 0:512], AF.Gelu_apprx_tanh,
                        [("ps", b)], [("VF", ti)])
            for (o_, n_, ti) in tiles:
                for hh in range(2):
                    P.add("dve", lambda e, n_=n_, ti=ti, hh=hh, VF=VF: e.bn_stats(
                        out=MV[0:n_, hh * 6:(hh + 1) * 6], in_=VF[0:n_, ti, hh * 512:(hh + 1) * 512]),
                        reads=[("VF", ti)], writes=["MV"])
                P.add("dve", lambda e, n_=n_: e.bn_aggr(out=MV[0:n_, 16:18], in_=MV[0:n_, 0:12]),
                      reads=["MV"], writes=["MV"])
                act(MV[0:n_, 18:19], MV[0:n_, 17:18], AF.Sqrt, ["MV"], ["MV"], bias=EPS, scale=1.0)
                P.add("dve", lambda e, n_=n_: e.reciprocal(out=MV[0:n_, 18:19], in_=MV[0:n_, 18:19]),
                      reads=["MV"], writes=["MV"])
                ts(VF[0:n_, ti, :], VF[0:n_, ti, :], MV[0:n_, 16:17], MV[0:n_, 18:19], ALU.subtract, ALU.mult,
                   [("VF", ti), "MV"], [("VF", ti)])
                tt(VF[0:n_, ti, :], VF[0:n_, ti, :], LNG[0:n_, :], ALU.mult, [("VF", ti), "LNG"], [("VF", ti)])
                if ti < ntile:
                    tt(VN[0:n_, ti, :], VF[0:n_, ti, :], LNB[0:n_, :], ALU.add, [("VF", ti), "LNB"], [("VN", ti)])
                else:
                    tt(VF[0:n_, ti, :], VF[0:n_, ti, :], LNB[0:n_, :], ALU.add, [("VF", ti), "LNB"], [("VF", ti)])
                    act(VN[0:n_, ti, :], VF[0:n_, ti, :], AF.Identity, [("VF", ti)], [("VN", ti)])
                    out_toks.append(dma(o_v_s[l], VF[0:NS, ti, :], reads=[("VF", ti)]))
            for uu in range(2):
                s, (wv,) = load_unit([(w_in[l][:, uu * 512:(uu + 1) * 512], KC)])
                for m in range(4):
                    mi = uu * 4 + m
                    for (o_, n_, _sb) in mblocks:
                        b = next_bank()
                        chain(PSB[b][:, 0:n_], b, [wv[:, k, m * 128:(m + 1) * 128] for k in range(KC)],
                              [H[:, k, o_:o_ + n_] for k in range(KC)],
                              reads=[("WS", s, 0), ("WS", s, 1)] + [("H", kk) for kk in range(KC)])
                        act(U[:, mi, o_:o_ + n_], PSB[b][:, 0:n_], AF.Gelu_apprx_tanh, [("ps", b)], [("U", mi)])
            dying = A.reset(mA1)
            WST = A.alloc("WST", [128, 8, 128], F32)
            WSB = A.alloc("WSB", [128, 8, 128], BF16)
            BSF = A.alloc("BSF", [1, 1024], F32)
            BSR = A.alloc("BSR", [1, 1024], BF16)
            WS4 = A.alloc("WS4", [64, 32], F32)
            BDB = A.alloc("BDB", [64, 8, 64], BF16)
            BS4F = A.alloc("BS4F", [1, 512], F32)
            BS4R = A.alloc("BS4R", [1, 512], BF16)
            P.fence(dying, ["WST", "WSB", "BSF", "BSR", "WS4", "BDB", "BS4F", "BS4R"])
            dma(WST[:], wsT[l].rearrange("p (g t) -> p g t", t=128), writes=["WST"])
            tt(WSB[:], WST[:], TRIL[:].unsqueeze(1).to_broadcast([128, 8, 128]), ALU.mult,
               ["WST", "TRIL"], ["WSB"])
            dma(BSF[:], bsp[l:l + 1, :], writes=["BSF"])
            copy(BSR[:], BSF[:], ["BSF"], ["BSR"])
            if HAS_S:
                dma(WS4[:], ws4[l], writes=["WS4"])
                tt(BDB[:].rearrange("p g (s t) -> p g s t", t=4),
                   WS4[:].rearrange("p (g t) -> p g t", t=4).unsqueeze(2).to_broadcast([64, 8, 16, 4]),
                   BDM[:].rearrange("p (s t) -> p s t", t=4).unsqueeze(1).to_broadcast([64, 8, 16, 4]),
                   ALU.mult, ["WS4", "BDM"], ["BDB"])
                dma(BS4F[:], bs4[l:l + 1, :], writes=["BS4F"])
                copy(BS4R[:], BS4F[:], ["BS4F"], ["BS4R"])
            for (o_, n_, ti) in tiles:
                for g in range(8):
                    b = next_bank()
                    if ti < ntile:
                        mm(PSB[b][:, 0:128], VN[:, ti, g * 128:(g + 1) * 128], WSB[:, g, :], True, False,
                           [("VN", ti), "WSB"], b)
                        mm(PSB[b][:, 0:128], ONES[0:1, :], BSR[0:1, g * 128:(g + 1) * 128], False, True,
                           ["ONES", "BSR"], b)
                    else:
                        mm(PSB[b][:, 0:NS], VN[0:NS, ti, g * 128:(g + 1) * 128], BDB[:, g, :], True, False,
                           [("VN", ti), "BDB"], b)
                        mm(PSB[b][:, 0:NS], ONES[0:1, :], BS4R[0:1, g * NS:(g + 1) * NS], False, True,
                           ["ONES", "BS4R"], b)
                    tt(U[:, g, o_:o_ + n_], PSB[b][:, 0:n_], U[:, g, o_:o_ + n_], ALU.mult,
                       [("ps", b), ("U", g)], [("U", g)])
            gated_out(l, U, "U", 8, lambda c0: (w_a_out[l][:, c0:c0 + 256], 8), GATE0, True,
                      rhs_chunk_fn=lambda mi, k, o_, n_: U[:, k, o_:o_ + n_])

            dying = A.reset(mA)
            CACC = A.alloc("CACC", [128, 8, TT], F32)
            YBIN = A.alloc("YBIN", [128, 8, TT], BF16)
            XBUF = [A.alloc(f"XBUF{i}", [128, 30 + NP], F32) for i in range(2)]
            XSC = [A.alloc(f"XSC{i}", [128, 16, 34], F32) for i in range(2)]
            MEAN = A.alloc("MEAN", [128, 512], F32)
            VAR = A.alloc("VAR", [128, 512], F32)
            P.fence(dying, ["CACC", "YBIN", "XBUF0", "XBUF1", "XSC0", "XSC1", "MEAN", "VAR"])
            for uo in range(4):
                c0 = uo * 256
                s, (wa, wb) = load_unit([(w_in[l][:, 2 * DA + c0:2 * DA + c0 + 256], KC),
                                         (w_in[l][:, 3 * DA + c0:3 * DA + c0 + 256], KC)])
                for m in range(2):
                    ci = uo * 2 + m
                    xb_ = XBUF[ci % 2]
                    xk = f"XBUF{ci % 2}"
                    xs_ = XSC[ci % 2]
                    sk = f"XSC{ci % 2}"
                    copy(xb_[:, 0:30], HCONV[:, l, ci, :], ["HCONV"], [xk])
                    if HAS_S:
                        dma(xs_[:, :, 0:30], st_conv[l][ci * 128:(ci + 1) * 128, :].rearrange("p (s r) -> p s r", r=30),
                            writes=[sk])
                    for (O_, N_, subs_) in mblocks:
                        b1 = next_bank()
                        chain(PSB[b1][:, 0:N_], b1, [wa[:, k, m * 128:(m + 1) * 128] for k in range(KC)],
                              [H[:, k, O_:O_ + N_] for k in range(KC)],
                              reads=[("WS", s, 0), ("WS", s, 1)] + [("H", kk) for kk in range(KC)])
                        b2 = next_bank()
                        chain(PSB[b2][:, 0:N_], b2, [wb[:, k, m * 128:(m + 1) * 128] for k in range(KC)],
                              [H[:, k, O_:O_ + N_] for k in range(KC)],
                              reads=[("WS", s, 0), ("WS", s, 1)] + [("H", kk) for kk in range(KC)])
                        g = next_sig()
                        act(SIG[g][:, 0:N_], PSB[b2][:, 0:N_], AF.Sigmoid, [("ps", b2)], [("SIG", g)])
                        for (o_, n_, is_s) in subs_:
                            r0 = o_ - O_
                            if not is_s:
                                tt(xb_[:, 30 + o_:30 + o_ + n_], PSB[b1][:, r0:r0 + n_], SIG[g][:, r0:r0 + n_], ALU.mult,
                                   [("ps", b1), ("SIG", g)], [xk])
                            else:
                                tt(xs_[:, :, 30:34], s3(PSB[b1][:, r0:r0 + n_]), s3(SIG[g][:, r0:r0 + n_]), ALU.mult,
                                   [("ps", b1), ("SIG", g)], [sk])
                    if first_super:
                        ts(xb_[:, 30:30 + HALO], xb_[:, 30:30 + HALO], HMASK[:, 0:1], None, ALU.mult, None,
                           [xk, "HMASK"], [xk])
                    wcol = lambda tap: PPt[:, l, O_WDW + tap * 8 + ci:O_WDW + tap * 8 + ci + 1]
                    ts(CACC[:, ci, 0:NP], xb_[:, 0:NP], wcol(0), PPt[:, l, O_BDW + ci:O_BDW + ci + 1],
                       ALU.mult, ALU.add, [xk, "PP"], [("CACC", ci)])
                    for tap in range(1, 31):
                        stt(CACC[:, ci, 0:NP], xb_[:, tap:tap + NP], wcol(tap), CACC[:, ci, 0:NP], ALU.mult, ALU.add,
                            [xk, "PP", ("CACC", ci)], [("CACC", ci)])
                    copy(HCONV[:, l, ci, :], xb_[:, NP:NP + 30], [xk], ["HCONV"])
                    if last_super:
                        out_toks.append(dma(o_conv_p[l][ci * 128:(ci + 1) * 128, :], xb_[:, NP:NP + 30], reads=[xk]))
                    if HAS_S:
                        cs_ = s3(CACC[:, ci, NP:NP + NS])
                        ts(cs_, xs_[:, :, 0:4], wcol(0), PPt[:, l, O_BDW + ci:O_BDW + ci + 1], ALU.mult, ALU.add,
                           [sk, "PP"], [("CACC", ci)])
                        for tap in range(1, 31):
                            stt(cs_, xs_[:, :, tap:tap + 4], wcol(tap), cs_, ALU.mult, ALU.add,
                                [sk, "PP", ("CACC", ci)], [("CACC", ci)])
                        out_toks.append(dma(o_conv_s[l][ci * 128:(ci + 1) * 128, :].rearrange("p (s r) -> p s r", r=30),
                                            xs_[:, :, 4:34], reads=[sk]))
            for (o_, n_, _sb) in mblocks:
                for ci in range(8):
                    q = next_sq()
                    act(SQ[q][:, 0:n_], CACC[:, ci, o_:o_ + n_], AF.Square, [("CACC", ci)], [("SQ", q)])
                    mm(PSB[7][:, 0:n_], ONES[:], SQ[q][:, 0:n_], ci == 0, ci == 7, ["ONES", ("SQ", q)], 7)
                    q2 = next_sq()
                    copy(SQ[q2][:, 0:n_], CACC[:, ci, o_:o_ + n_], [("CACC", ci)], [("SQ", q2)])
                    mm(PSB[6][:, 0:n_], ONES[:], SQ[q2][:, 0:n_], ci == 0, ci == 7, ["ONES", ("SQ", q2)], 6)
                ts(MEAN[:, 0:n_], PSB[6][:, 0:n_], 1.0 / DA, None, ALU.mult, None, [("ps", 6)], ["MEAN"])
                tt(VAR[:, 0:n_], MEAN[:, 0:n_], MEAN[:, 0:n_], ALU.mult, ["MEAN"], ["VAR"])
                stt(VAR[:, 0:n_], PSB[7][:, 0:n_], 1.0 / DA, VAR[:, 0:n_], ALU.mult, ALU.subtract,
                    [("ps", 7), "VAR"], ["VAR"])
                ts(VAR[:, 0:n_], VAR[:, 0:n_], 0.0, None, ALU.max, None, ["VAR"], ["VAR"])
                act(VAR[:, 0:n_], VAR[:, 0:n_], AF.Sqrt, ["VAR"], ["VAR"], bias=EPS, scale=1.0)
                P.add("dve", lambda e, n_=n_, VAR=VAR: e.reciprocal(out=VAR[:, 0:n_], in_=VAR[:, 0:n_]),
                      reads=["VAR"], writes=["VAR"])
                for ci in range(8):
                    t_ = next_tmp()
                    tt(TMPF[t_][:, 0:n_], CACC[:, ci, o_:o_ + n_], MEAN[:, 0:n_], ALU.subtract,
                       [("CACC", ci), "MEAN"], [("TMPF", t_)])
                    tt(TMPF[t_][:, 0:n_], TMPF[t_][:, 0:n_], VAR[:, 0:n_], ALU.mult, [("TMPF", t_), "VAR"],
                       [("TMPF", t_)])
                    act(YBIN[:, ci, o_:o_ + n_], TMPF[t_][:, 0:n_], AF.Silu, [("TMPF", t_), "PP"], [("YBIN", ci)],
                        bias=PPt[:, l, O_LCB + ci:O_LCB + ci + 1], scale=PPt[:, l, O_LCG + ci:O_LCG + ci + 1])
            gated_out(l, YBIN, "YBIN", 8, lambda c0: (w_b_out[l][:, c0:c0 + 256], 8), GATE0 + D, False,
                      rhs_chunk_fn=lambda mi, k, o_, n_: YBIN[:, k, o_:o_ + n_])

            dying = A.reset(mA)
            LP = 15 + NP
            PBUF = A.alloc("PBUF", [128, 8, LP], F32)
            POOLED = A.alloc("POOLED", [128, 8, TT], BF16)
            PT = [A.alloc(f"PT{i}", [128, 2, LP], F32) for i in range(2)]
            PSC = A.alloc("PSC", [128, 8, 16, 19], F32)
            PTS = [A.alloc(f"PTS{i}", [128, 2, 16, 19], F32) for i in range(2)]
            T16 = A.alloc("T16", [128, 2, 16], F32)
            P.fence(dying, ["PBUF", "POOLED", "PT0", "PT1", "PSC", "PTS0", "PTS1", "T16"])
            for ci in range(8):
                copy(PBUF[:, ci, 0:15], HPOOL[:, l, ci, :], ["HPOOL"], [("PBUF", ci)])
            if HAS_S:
                for ci in range(8):
                    dma(PSC[:, ci, :, 0:15], st_pool[l][ci * 128:(ci + 1) * 128, :].rearrange("p (s r) -> p s r", r=15),
                        writes=[("PSC", ci)])
            for uu in range(2):
                s, (wv,) = load_unit([(w_in[l][:, 4 * DA + uu * 512:4 * DA + (uu + 1) * 512], KC)])
                for m in range(4):
                    ci = uu * 4 + m
                    for (O_, N_, subs_) in mblocks:
                        b = next_bank()
                        chain(PSB[b][:, 0:N_], b, [wv[:, k, m * 128:(m + 1) * 128] for k in range(KC)],
                              [H[:, k, O_:O_ + N_] for k in range(KC)],
                              reads=[("WS", s, 0), ("WS", s, 1)] + [("H", kk) for kk in range(KC)])
                        for (o_, n_, is_s) in subs_:
                            r0 = o_ - O_
                            if not is_s:
                                act(PBUF[:, ci, 15 + o_:15 + o_ + n_], PSB[b][:, r0:r0 + n_], AF.Identity, [("ps", b)],
                                    [("PBUF", ci)])
                            else:
                                act(PSC[:, ci, :, 15:19], s3(PSB[b][:, r0:r0 + n_]), AF.Identity, [("ps", b)],
                                    [("PSC", ci)])
                    if first_super:
                        ts(PBUF[:, ci, 15:15 + HALO], PBUF[:, ci, 15:15 + HALO], HMASK[:, 0:1], None, ALU.mult, None,
                           [("PBUF", ci), "HMASK"], [("PBUF", ci)])
                    copy(HPOOL[:, l, ci, :], PBUF[:, ci, NP:NP + 15], [("PBUF", ci)], ["HPOOL"])
                    if last_super:
                        out_toks.append(dma(o_pool_p[l][ci * 128:(ci + 1) * 128, :], PBUF[:, ci, NP:NP + 15],
                                            reads=[("PBUF", ci)]))
                    if HAS_S:
                        out_toks.append(dma(o_pool_s[l][ci * 128:(ci + 1) * 128, :].rearrange("p (s r) -> p s r", r=15),
                                            PSC[:, ci, :, 4:19], reads=[("PSC", ci)]))
            for gi in range(4):
                w = 2 << gi
                c2 = slice(2 * gi, 2 * gi + 2)
                rk = [("PBUF", 2 * gi), ("PBUF", 2 * gi + 1)]
                src = PBUF[:, c2, :]
                srck = rk
                sh = 1
                lo = 0
                for stp in range(gi + 1):
                    dst = PT[stp % 2]
                    dk = [f"PT{stp % 2}"]
                    nlo = lo + sh
                    tt(dst[:, :, nlo:LP], src[:, :, nlo:LP], src[:, :, lo:LP - sh], ALU.add, srck, dk)
                    src = dst; srck = dk; lo = nlo; sh *= 2
                stt(POOLED[:, c2, 0:NP], src[:, :, 15:15 + NP], 1.0 / w, PBUF[:, c2, 15:15 + NP], ALU.mult,
                    ALU.subtract, srck + rk, [("POOLED", 2 * gi), ("POOLED", 2 * gi + 1)])
                if first_super:
                    a0 = 15 + HALO
                    tt(T16[:], src[:, :, a0:a0 + 16],
                       PCNT[:, gi * 16:(gi + 1) * 16].unsqueeze(1).to_broadcast([128, 2, 16]), ALU.mult,
                       srck + ["PCNT"], ["T16"])
                    tt(POOLED[:, c2, HALO:HALO + 16], T16[:], PBUF[:, c2, a0:a0 + 16], ALU.subtract,
                       ["T16"] + rk, [("POOLED", 2 * gi), ("POOLED", 2 * gi + 1)])
                if HAS_S:
                    rks = [("PSC", 2 * gi), ("PSC", 2 * gi + 1)]
                    src = PSC[:, c2, :, :]
                    srck = rks
                    sh = 1
                    lo = 0
                    for stp in range(gi + 1):
                        dst = PTS[stp % 2]
                        dk = [f"PTS{stp % 2}"]
                        nlo = lo + sh
                        tt(dst[:, :, :, nlo:19], src[:, :, :, nlo:19], src[:, :, :, lo:19 - sh], ALU.add, srck, dk)
                        src = dst; srck = dk; lo = nlo; sh *= 2
                    stt(POOLED[:, c2, NP:NP + NS].rearrange("p c (s t) -> p c s t", t=4), src[:, :, :, 15:19],
                        1.0 / w, PSC[:, c2, :, 15:19], ALU.mult, ALU.subtract, srck + rks,
                        [("POOLED", 2 * gi), ("POOLED", 2 * gi + 1)])
            gated_out(l, POOLED, "POOLED", 2,
                      lambda c0: (w_pool[l][c0 // 512][:, (c0 % 512):(c0 % 512) + 256], 2), GATE0 + 2 * D, False,
                      scale_off=O_PS,
                      rhs_chunk_fn=lambda mi, k, o_, n_: POOLED[:, 2 * (mi // 4) + k, o_:o_ + n_])

            dying = A.reset(mA)
            YMO = A.alloc("YMO", [128, KC, TT], F32)
            P.fence(dying, ["YMO"])

            def proj_out(l, units_fn, nunits, rhs_fn, rhs_keys, kparts):
                for u in range(nunits):
                    banks = {}
                    for kp in range(kparts):
                        src, kc_, koff = units_fn(u, kp)
                        s, (wv,) = load_unit([(src, kc_)])
                        for m in range(2):
                            for bi, (o_, n_, _sb) in enumerate(mblocks):
                                if kp == 0:
                                    banks[(m, bi)] = next_bank()
                                b = banks[(m, bi)]
                                chain(PSB[b][:, 0:n_], b, [wv[:, k, m * 128:(m + 1) * 128] for k in range(kc_)],
                                      [rhs_fn(koff + k, o_, n_) for k in range(kc_)],
                                      reads=[("WS", s, 0), ("WS", s, 1)] + rhs_keys, first=(kp == 0), last=(kp == kparts - 1))
                    for m in range(2):
                        mi = u * 2 + m
                        for bi, (o_, n_, _sb) in enumerate(mblocks):
                            b = banks[(m, bi)]
                            act(YMO[:, mi, o_:o_ + n_], PSB[b][:, 0:n_], AF.Identity, [("ps", b)], [("YMO", mi)])
                            q = next_sq()
                            act(SQ[q][:, 0:n_], PSB[b][:, 0:n_], AF.Square, [("ps", b)], [("SQ", q)])
                            sb_ = 6 + bi
                            mm(PSB[sb_][:, 0:n_], ONES[:],
                               SQ[q][:, 0:n_], mi == 0, mi == KC - 1, ["ONES", ("SQ", q)], sb_)
                for bi, (o_, n_, _sb) in enumerate(mblocks):
                    sb_ = 6 + bi
                    c0_ = 0
                    act(RSTD[:, o_:o_ + n_], PSB[sb_][:, c0_:c0_ + n_], AF.Sqrt, [("ps", sb_)], ["RSTD"],
                        bias=EPS, scale=1.0 / D)
                    P.add("dve", lambda e, o_=o_, n_=n_: e.reciprocal(out=RSTD[:, o_:o_ + n_], in_=RSTD[:, o_:o_ + n_]),
                          reads=["RSTD"], writes=["RSTD"])

            proj_out(l, lambda u, kp: (w_o[l][:, u * 256:(u + 1) * 256], KC, 0), 8,
                     lambda k, o_, n_: MIXACC[:, k, o_:o_ + n_], [("MIXACC", kk) for kk in range(KC)], 1)
            postnorm(l, 32, YMO)

            dying = A.reset(m0)
            ACTB = A.alloc("ACTB", [128, NFC, TT], BF16)
            YMO = A.alloc("YMO", [128, KC, TT], F32)
            GSB = [A.alloc(f"GSB{i}", [128, 2 + NP], F32) for i in range(2)]
            GSS = [A.alloc(f"GSS{i}", [128, 16, 6], F32) for i in range(2)]
            FACC = [A.alloc(f"FACC{i}", [128, TT], F32) for i in range(2)]
            SFF = A.alloc("SFF", [128, NFC, 16, 2], F32)
            OFP = A.alloc("OFP", [128, NFC, 2], F32)
            P.fence(dying, ["ACTB", "YMO", "GSB0", "GSB1", "GSS0", "GSS1", "FACC0", "FACC1", "SFF", "OFP"])
            prenorm(l, 64, 48)
            if HAS_S:
                for j in range(NFC):
                    dma(SFF[:, j, :, :], st_ffn[l][j * 128:(j + 1) * 128, :].rearrange("p (s r) -> p s r", r=2),
                        writes=[("SFF", j)])
            for u in range(22):
                f0 = u * 256
                nf = min(256, DFF - f0)
                s, (wg, wv) = load_unit([(w_up[l][:, f0:f0 + nf], KC), (w_up[l][:, DFF + f0:DFF + f0 + nf], KC)])
                for m in range(nf // 128):
                    j = u * 2 + m
                    gb_ = GSB[j % 2]; gk = f"GSB{j % 2}"
                    gs_ = GSS[j % 2]; gsk = f"GSS{j % 2}"
                    fa_ = FACC[j % 2]; fk = f"FACC{j % 2}"
                    copy(gb_[:, 0:2], HFFN[:, l, j, :], ["HFFN"], [gk])
                    if HAS_S:
                        copy(gs_[:, :, 0:2], SFF[:, j, :, :], [("SFF", j)], [gsk])
                    for (O_, N_, subs_) in mblocks:
                        b1 = next_bank()
                        chain(PSB[b1][:, 0:N_], b1, [wg[:, k, m * 128:(m + 1) * 128] for k in range(KC)],
                              [H[:, k, O_:O_ + N_] for k in range(KC)],
                              reads=[("WS", s, 0), ("WS", s, 1)] + [("H", kk) for kk in range(KC)])
                        for (o_, n_, is_s) in subs_:
                            r0 = o_ - O_
                            if not is_s:
                                act(gb_[:, 2 + o_:2 + o_ + n_], PSB[b1][:, r0:r0 + n_], AF.Identity, [("ps", b1)], [gk])
                            else:
                                act(gs_[:, :, 2:6], s3(PSB[b1][:, r0:r0 + n_]), AF.Identity, [("ps", b1)], [gsk])
                    if first_super:
                        ts(gb_[:, 2:2 + HALO], gb_[:, 2:2 + HALO], HMASK[:, 0:1], None, ALU.mult, None,
                           [gk, "HMASK"], [gk])
                    wc = lambda tap: PPt[:, l, O_WFC + tap * NFC + j:O_WFC + tap * NFC + j + 1]
                    bcol = PPt[:, l, O_BFC + j:O_BFC + j + 1]
                    ts(fa_[:, 0:NP], gb_[:, 0:NP], wc(0), bcol, ALU.mult, ALU.add, [gk, "PP"], [fk])
                    for tap in (1, 2):
                        stt(fa_[:, 0:NP], gb_[:, tap:tap + NP], wc(tap), fa_[:, 0:NP], ALU.mult, ALU.add,
                            [gk, "PP", fk], [fk])
                    copy(HFFN[:, l, j, :], gb_[:, NP:NP + 2], [gk], ["HFFN"])
                    if last_super:
                        copy(OFP[:, j, :], gb_[:, NP:NP + 2], [gk], ["OFP"])
                    if HAS_S:
                        fs_ = s3(fa_[:, NP:NP + NS])
                        ts(fs_, gs_[:, :, 0:4], wc(0), bcol, ALU.mult, ALU.add, [gsk, "PP"], [fk])
                        for tap in (1, 2):
                            stt(fs_, gs_[:, :, tap:tap + 4], wc(tap), fs_, ALU.mult, ALU.add, [gsk, "PP", fk], [fk])
                        copy(SFF[:, j, :, :], gs_[:, :, 4:6], [gsk], [("SFF", j)])
                    act(fa_[:, 0:TT], fa_[:, 0:TT], AF.Gelu_apprx_tanh, [fk], [fk])
                    for (o_, n_, _sb) in mblocks:
                        b2 = next_bank()
                        chain(PSB[b2][:, 0:n_], b2, [wv[:, k, m * 128:(m + 1) * 128] for k in range(KC)],
                              [H[:, k, o_:o_ + n_] for k in range(KC)],
                              reads=[("WS", s, 0), ("WS", s, 1)] + [("H", kk) for kk in range(KC)])
                        tt(ACTB[:, j, o_:o_ + n_], PSB[b2][:, 0:n_], fa_[:, o_:o_ + n_], ALU.mult,
                           [("ps", b2), fk], [("ACTB", j)])
            if last_super:
                out_toks.append(dma(o_ffn_p[l].rearrange("(j p) r -> p j r", p=128), OFP[:], reads=["OFP"]))
            if HAS_S:
                out_toks.append(dma(o_ffn_s[l].rearrange("(j p) (s r) -> p j s r", p=128, r=2), SFF[:],
                                    reads=[("SFF", j) for j in range(NFC)]))
            KH = [(0, 22), (22, 21)]
            proj_out(l, lambda u, kp: (w_down[l][KH[kp][0] * 128:(KH[kp][0] + KH[kp][1]) * 128, u * 256:(u + 1) * 256],
                                       KH[kp][1], KH[kp][0]), 8,
                     lambda k, o_, n_: ACTB[:, k, o_:o_ + n_], [("ACTB", jj) for jj in range(NFC)], 2)
            postnorm(l, 80, YMO)
            dying = A.reset(m0)
            P.fence(dying, ["MIXACC", "VN", "U", "VF", "LNG", "LNB"])

        for k in range(KC):
            c_lo = max(P0, HALO)
            if c_lo < P0 + NP:
                out_toks.append(dma(yp[k * 128:(k + 1) * 128, c_lo - HALO:P0 + NP - HALO],
                                    X[:, k, c_lo - P0:NP], reads=[("X", k)]))
            if HAS_S:
                out_toks.append(dma(ys[k * 128:(k + 1) * 128, :], X[:, k, NP:NP + NS], reads=[("X", k)]))

    run = P.make_runner(sems, lanes, final_waits=out_toks)
    with nc.Block() as block:
        @block.sync
        def _(e):
            run("sp", e)

        @block.tensor
        def _(e):
            run("pe", e)

        @block.scalar
        def _(e):
            run("act", e)

        @block.vector
        def _(e):
            run("dve", e)

        @block.gpsimd
        def _(e):
            run("pool", e)
    es.close()
    return nc, A.peak


def _chunkT(v):
    return np.ascontiguousarray(v.reshape(-1, 128).T)


def kernel(x_prompt, x_sample, c_prompt, c_sample, state_conv, state_pool, state_ffn_conv,
           ada_w, ada_b, g_pre_mix, g_post_mix, g_pre_ffn, g_post_ffn, w_in, b_gate,
           ln_v_g, ln_v_b, w_spatial, b_spatial, w_a_out, w_dwconv, b_dwconv, ln_conv_g,
           ln_conv_b, w_b_out, w_pool_grp, pool_scale, w_o, w_up, w_ffn_conv, b_ffn_conv, w_down):
    f = lambda a: np.ascontiguousarray(np.asarray(a, dtype=np.float32))
    x_prompt, x_sample, c_prompt, c_sample = f(x_prompt), f(x_sample), f(c_prompt), f(c_sample)
    state_conv, state_pool, state_ffn_conv = f(state_conv), f(state_pool), f(state_ffn_conv)
    w_spatial = f(w_spatial); b_spatial = f(b_spatial)

    pp = np.zeros((DEPTH, 128, NPP), np.float32)
    for l in range(DEPTH):
        pp[l, :, O_GPM:O_GPM + 16] = _chunkT(f(g_pre_mix)[l])
        pp[l, :, O_GQM:O_GQM + 16] = _chunkT(f(g_post_mix)[l])
        pp[l, :, O_GPF:O_GPF + 16] = _chunkT(f(g_pre_ffn)[l])
        pp[l, :, O_GQF:O_GQF + 16] = _chunkT(f(g_post_ffn)[l])
        pp[l, :, O_BG:O_BG + 48] = _chunkT(f(b_gate)[l])
        pp[l, :, O_PS:O_PS + 16] = _chunkT(f(pool_scale)[l])
        pp[l, :, O_BDW:O_BDW + 8] = _chunkT(f(b_dwconv)[l])
        pp[l, :, O_LCG:O_LCG + 8] = _chunkT(f(ln_conv_g)[l])
        pp[l, :, O_LCB:O_LCB + 8] = _chunkT(f(ln_conv_b)[l])
        for tap in range(31):
            pp[l, :, O_WDW + tap * 8:O_WDW + tap * 8 + 8] = _chunkT(f(w_dwconv)[l, tap])
        for tap in range(3):
            pp[l, :, O_WFC + tap * NFC:O_WFC + (tap + 1) * NFC] = _chunkT(f(w_ffn_conv)[l, tap])
        pp[l, :, O_BFC:O_BFC + NFC] = _chunkT(f(b_ffn_conv)[l])
        pp[l, :, O_ADB:O_ADB + 96] = _chunkT(f(ada_b)[l])
    wsT = np.ascontiguousarray(w_spatial.transpose(0, 3, 1, 2)).reshape(DEPTH, 128, 8 * 128)
    bsp = np.ascontiguousarray(b_spatial.reshape(DEPTH, 8 * 128))
    w4 = w_spatial[:, :, 0:4, 0:4].transpose(0, 3, 1, 2)
    ws4 = np.ascontiguousarray(np.tile(w4.reshape(DEPTH, 1, 4, 32), (1, 16, 1, 1)).reshape(DEPTH, 64, 32))
    bs4 = np.ascontiguousarray(np.tile(b_spatial[:, :, None, 0:4], (1, 1, 16, 1)).reshape(DEPTH, 8 * 64))
    tril = np.triu(np.ones((128, 128), np.float32))
    bdm = np.zeros((64, 16, 4), np.float32)
    for sq in range(16):
        for s_ in range(4):
            bdm[sq * 4 + s_, sq, s_:] = 1.0
    bdm = bdm.reshape(64, 64)

    shared = dict(pp=pp, lnvg=f(ln_v_g), lnvb=f(ln_v_b), wsT=wsT, bsp=bsp, ws4=ws4, bs4=bs4, tril=tril,
                  bdmask=bdm, ada_w=f(ada_w), w_in=f(w_in), w_a_out=f(w_a_out), w_b_out=f(w_b_out),
                  w_pool=f(w_pool_grp), w_o=f(w_o), w_up=f(w_up), w_down=f(w_down))
    in_maps = []
    for c in range(NCORES):
        seq, half = c // 2, c % 2
        xpc = np.zeros((NPROMPT, D), np.float32)
        if half == 0:
            xpc[HALO:] = x_prompt[seq, 0:1024]
        else:
            xpc = x_prompt[seq, 1024 - HALO:2048]
        ss = slice(c * 16, (c + 1) * 16)
        cT = np.concatenate([c_prompt[seq][None, :], c_sample[ss]], axis=0).T
        pcnt = np.zeros((4, 16), np.float32)
        for gi, w in enumerate((2, 4, 8, 16)):
            for i in range(16):
                pos = half * 1024 + i
                pcnt[gi, i] = 1.0 / min(pos + 1, w)
        m = dict(shared)
        m.update(
            xp=np.ascontiguousarray(xpc.T), xs=np.ascontiguousarray(x_sample[ss].reshape(NS, D).T),
            cT=np.ascontiguousarray(cT), hmask=np.full((128, 1), float(half), np.float32),
            pcnt=np.ascontiguousarray(np.tile(pcnt.reshape(1, 64), (128, 1))),
            st_conv=np.ascontiguousarray(state_conv[:, ss].transpose(0, 3, 1, 2)).reshape(DEPTH, DA, 16 * 30),
            st_pool=np.ascontiguousarray(state_pool[:, ss].transpose(0, 3, 1, 2)).reshape(DEPTH, DA, 16 * 15),
            st_ffn=np.ascontiguousarray(state_ffn_conv[:, ss].transpose(0, 3, 1, 2)).reshape(DEPTH, DFF, 16 * 2),
        )
        in_maps.append(m)

    nc, _ = build_program()
    res = run_bass_kernel_spmd(nc, in_maps, core_ids=list(range(NCORES)))
    R = res.results

    y_prompt = np.zeros((4, 2048, D), np.float32)
    y_sample = np.zeros((128, 4, D), np.float32)
    conv_p = np.zeros((DEPTH, 4, 30, DA), np.float32); conv_s = np.zeros((DEPTH, 128, 30, DA), np.float32)
    pool_p = np.zeros((DEPTH, 4, 15, DA), np.float32); pool_s = np.zeros((DEPTH, 128, 15, DA), np.float32)
    ffn_p = np.zeros((DEPTH, 4, 2, DFF), np.float32); ffn_s = np.zeros((DEPTH, 128, 2, DFF), np.float32)
    v_s = np.zeros((DEPTH, 128, 4, DA), np.float32)
    for c in range(NCORES):
        seq, half = c // 2, c % 2
        ss = slice(c * 16, (c + 1) * 16)
        r = R[c]
        y_prompt[seq, half * 1024:(half + 1) * 1024] = r["yp"].T
        y_sample[ss] = r["ys"].T.reshape(16, 4, D)
        conv_s[:, ss] = r["o_conv_s"].reshape(DEPTH, DA, 16, 30).transpose(0, 2, 3, 1)
        pool_s[:, ss] = r["o_pool_s"].reshape(DEPTH, DA, 16, 15).transpose(0, 2, 3, 1)
        ffn_s[:, ss] = r["o_ffn_s"].reshape(DEPTH, DFF, 16, 2).transpose(0, 2, 3, 1)
        v_s[:, ss] = r["o_v_s"].reshape(DEPTH, 16, 4, DA)
        if half == 1:
            conv_p[:, seq] = r["o_conv_p"].transpose(0, 2, 1)
            pool_p[:, seq] = r["o_pool_p"].transpose(0, 2, 1)
            ffn_p[:, seq] = r["o_ffn_p"].transpose(0, 2, 1)
    return (y_prompt, y_sample, conv_p, conv_s, pool_p, pool_s, ffn_p, ffn_s, v_s)
```

```python
import numpy as np
import concourse.bass as bass
import concourse.mybir as mybir
from concourse.bass_utils import run_bass_kernel_spmd

F32 = mybir.dt.float32
BF16 = mybir.dt.bfloat16
AF = mybir.ActivationFunctionType
ALU = mybir.AluOpType

D = 2048; DA = 1024; DFF = 5504; NIN = 11264; DEPTH = 4
KC = 16; NFC = 43
EPS = 1e-6
NCORES = 8
HALO = 128
NPROMPT = 1152
NS = 64
SUPERS = [(0, 384, True), (384, 384, False), (768, 384, False)]
NPMAX = 384
TTMAX = NPMAX + NS
GATE0 = 2 * DA + 2 * DA + DA

O_GPM = 0; O_GQM = 16; O_GPF = 32; O_GQF = 48; O_BG = 64; O_PS = 112; O_BDW = 128
O_LCG = 136; O_LCB = 144; O_WDW = 152; O_WFC = 400; O_BFC = 529; O_ADB = 572; NPP = 668

ENGS = ("pe", "act", "dve", "pool", "sp")


class Plan:
    def __init__(self):
        self.ops = {e: [] for e in ENGS}
        self.last_write = {}
        self.readers = {}
        self.lane_count = {}
        self.lane_last = {}
        self.inherit = {}
        self.carry = {}

    @staticmethod
    def _merge(dst, tok):
        ch = tok[:2]
        old = dst.get(ch)
        if old is None or old[2] < tok[2]:
            dst[ch] = tok

    def _init_key(self, k):
        if k not in self.readers:
            name = k[0] if isinstance(k, tuple) else k
            self.readers[k] = dict(self.inherit.get(name, {}))

    def fence(self, dying, newnames):
        merged = {}
        for k, t in self.last_write.items():
            name = k[0] if isinstance(k, tuple) else k
            if name in dying:
                self._merge(merged, t)
        for k, rd in self.readers.items():
            name = k[0] if isinstance(k, tuple) else k
            if name in dying:
                for t in rd.values():
                    self._merge(merged, t)
        for t in merged.values():
            self._merge(self.carry, t)
        merged = self.carry
        for n in newnames:
            self.inherit[n] = dict(merged)
        for k in list(self.readers.keys()):
            name = k[0] if isinstance(k, tuple) else k
            if name in newnames:
                for t in merged.values():
                    self._merge(self.readers[k], t)

    def add(self, eng, emit, reads=(), writes=(), lane=None, serialize=True):
        idx = len(self.ops[eng])
        if lane is not None:
            cnt = self.lane_count.get(lane, 0) + 1
            self.lane_count[lane] = cnt
            tok = ("d", lane, cnt)
        else:
            tok = ("c", eng, idx)
        deps = set()
        if lane is not None and serialize and lane in self.lane_last:
            deps.add(self.lane_last[lane])
        for k in reads:
            self._init_key(k)
            t = self.last_write.get(k)
            if t is not None:
                deps.add(t)
        for k in writes:
            self._init_key(k)
            t = self.last_write.get(k)
            if t is not None:
                deps.add(t)
            for t in self.readers[k].values():
                deps.add(t)
        final = []
        for t in deps:
            if t == tok:
                continue
            if t[0] == "c" and lane is None and t[1] == eng:
                if eng == "pe":
                    continue
                is_raw = any(self.last_write.get(k) == t for k in reads)
                if not is_raw:
                    continue
            final.append(t)
            if t[0] == "c":
                self.ops[t[1]][t[2]]["inc"] = True
        self.ops[eng].append(dict(emit=emit, deps=final, inc=False, lane=lane))
        if lane is not None:
            self.lane_last[lane] = tok
        for k in reads:
            self._merge(self.readers[k], tok)
        for k in writes:
            self.last_write[k] = tok
            self.readers[k] = {}
        return tok

    def make_runner(self, sems, lane_sems, final_waits=()):
        counts = {}
        for e in ENGS:
            c = 0
            lst = []
            for op in self.ops[e]:
                if op["inc"] and op["lane"] is None:
                    c += 1
                lst.append(c)
            counts[e] = lst

        def tokval(t):
            if t[0] == "c":
                return sems[t[1]], counts[t[1]][t[2]]
            return lane_sems[t[1]], 16 * t[2]

        def run(e, engine):
            known = {}
            for op in self.ops[e]:
                for t in op["deps"]:
                    s, v = tokval(t)
                    if known.get(id(s), -1) >= v:
                        continue
                    known[id(s)] = v
                    engine.wait_ge(s, v)
                ins = op["emit"](engine)
                if op["lane"] is not None:
                    ins.then_inc(lane_sems[op["lane"]], 16)
                elif op["inc"]:
                    ins.then_inc(sems[e], 1)
            if e == "sp":
                for t in final_waits:
                    s, v = tokval(t)
                    engine.wait_ge(s, v)
        return run


class Arena:
    def __init__(self, nc):
        self.nc = nc
        self.off = (nc.sbuf_base + 63) // 64 * 64
        self.top = nc.sbuf_top
        self.live = []
        self.uid = 0
        self.peak = self.off

    def alloc(self, name, shape, dt):
        nb = int(np.prod(shape[1:])) * (4 if dt == F32 else 2)
        nb = (nb + 63) // 64 * 64
        o = self.off
        self.off += nb
        self.peak = max(self.peak, self.off)
        assert self.off <= self.top, f"SBUF overflow at {name}: {self.off} > {self.top}"
        self.uid += 1
        t = self.nc.alloc_sbuf_tensor_at(f"{name}_{self.uid}", shape, dt, offset=o)
        self.live.append(name)
        return t

    def mark(self):
        return (self.off, len(self.live))

    def reset(self, m):
        dying = set(self.live[m[1]:])
        self.off = m[0]
        del self.live[m[1]:]
        return dying


def build_program():
    nc = bass.Bass("TRN2", target_bir_lowering=False)

    def din(name, shape):
        return nc.dram_tensor(name, list(shape), F32, kind="ExternalInput").ap()

    def dout(name, shape):
        return nc.dram_tensor(name, list(shape), F32, kind="ExternalOutput").ap()

    xp = din("xp", [D, NPROMPT]); xs = din("xs", [D, NS]); cT = din("cT", [D, 17])
    hmask = din("hmask", [128, 1]); pcnt = din("pcnt", [128, 64]); tril = din("tril", [128, 128])
    bdmask = din("bdmask", [64, 64])
    st_conv = din("st_conv", [DEPTH, DA, 16 * 30]); st_pool = din("st_pool", [DEPTH, DA, 16 * 15])
    st_ffn = din("st_ffn", [DEPTH, DFF, 16 * 2])
    pp = din("pp", [DEPTH, 128, NPP])
    lnvg = din("lnvg", [DEPTH, DA]); lnvb = din("lnvb", [DEPTH, DA])
    wsT = din("wsT", [DEPTH, 128, 8 * 128]); bsp = din("bsp", [DEPTH, 8 * 128])
    ws4 = din("ws4", [DEPTH, 64, 32]); bs4 = din("bs4", [DEPTH, 8 * 64])
    ada_w = din("ada_w", [DEPTH, D, 6 * D]); w_in = din("w_in", [DEPTH, D, NIN])
    w_a_out = din("w_a_out", [DEPTH, DA, D]); w_b_out = din("w_b_out", [DEPTH, DA, D])
    w_pool = din("w_pool", [DEPTH, 4, 256, 512]); w_o = din("w_o", [DEPTH, D, D])
    w_up = din("w_up", [DEPTH, D, 2 * DFF]); w_down = din("w_down", [DEPTH, DFF, D])

    yp = dout("yp", [D, 1024]); ys = dout("ys", [D, NS])
    o_conv_p = dout("o_conv_p", [DEPTH, DA, 30]); o_conv_s = dout("o_conv_s", [DEPTH, DA, 16 * 30])
    o_pool_p = dout("o_pool_p", [DEPTH, DA, 15]); o_pool_s = dout("o_pool_s", [DEPTH, DA, 16 * 15])
    o_ffn_p = dout("o_ffn_p", [DEPTH, DFF, 2]); o_ffn_s = dout("o_ffn_s", [DEPTH, DFF, 16 * 2])
    o_v_s = dout("o_v_s", [DEPTH, NS, DA])

    modsc = nc.dram_tensor("modsc", [DEPTH, 128, 96 * 17], F32).ap()
    P = Plan()
    A = Arena(nc)
    out_toks = []

    X = A.alloc("X", [128, KC, TTMAX], F32)
    H = A.alloc("H", [128, KC, TTMAX], BF16)
    MODC = A.alloc("MODC", [128, 96, 17], F32)
    PPt = A.alloc("PP", [128, DEPTH, NPP], F32)
    NSLOT = 3
    SLOT_ELEMS = 8192
    WS = [A.alloc(f"WS{i}", [128, SLOT_ELEMS], BF16) for i in range(NSLOT)]
    RSTD = A.alloc("RSTD", [128, TTMAX], F32)
    NSQ = 3
    SQ = [A.alloc(f"SQ{i}", [128, 512], BF16) for i in range(NSQ)]
    NTMP = 2
    TMPF = [A.alloc(f"TMPF{i}", [128, 512], F32) for i in range(NTMP)]
    NSIG = 2
    SIG = [A.alloc(f"SIG{i}", [128, 512], F32) for i in range(NSIG)]
    HCONV = A.alloc("HCONV", [128, DEPTH, 8, 30], F32)
    HPOOL = A.alloc("HPOOL", [128, DEPTH, 8, 15], F32)
    HFFN = A.alloc("HFFN", [128, DEPTH, NFC, 2], F32)
    ONES = A.alloc("ONES", [128, 128], BF16)
    TRIL = A.alloc("TRIL", [128, 128], F32)
    BDM = A.alloc("BDM", [64, 64], F32)
    HMASK = A.alloc("HMASK", [128, 1], F32)
    PCNT = A.alloc("PCNT", [128, 64], F32)
    MV = A.alloc("MV", [128, 32], F32)
    ada_mark = A.mark()
    CT = A.alloc("CT", [128, KC, 17], F32)
    SC = A.alloc("SC", [128, KC, 17], BF16)

    from contextlib import ExitStack
    es = ExitStack()
    PSB = [es.enter_context(nc.psum_tensor(f"psb{i}", [128, 512], F32)) for i in range(8)]
    sems = {e: es.enter_context(nc.semaphore(f"s_{e}")) for e in ENGS}
    NMISC = 8
    lanes = {}
    for i in range(NSLOT):
        lanes[f"w{i}"] = es.enter_context(nc.semaphore(f"l_w{i}"))
    for i in range(NMISC):
        lanes[f"m{i}"] = es.enter_context(nc.semaphore(f"l_m{i}"))

    st = dict(misc=0, slot=0, bank=0, sq=0, tmp=0, sig=0)

    def misc_lane():
        st["misc"] = (st["misc"] + 1) % NMISC
        return f"m{st['misc']}"

    def dma(out, in_, reads=(), writes=(), eng="sp"):
        return P.add(eng, lambda e: e.dma_start(out=out, in_=in_), reads=reads, writes=writes,
                     lane=misc_lane())

    def next_bank():
        b = st["bank"]
        st["bank"] = (b + 1) % 6
        return b

    def next_sq():
        st["sq"] = (st["sq"] + 1) % NSQ
        return st["sq"]

    def next_tmp():
        st["tmp"] = (st["tmp"] + 1) % NTMP
        return st["tmp"]

    def next_sig():
        st["sig"] = (st["sig"] + 1) % NSIG
        return st["sig"]

    def act(out, in_, func, reads, writes, bias=0.0, scale=1.0):
        return P.add("act", lambda e: e.activation(out=out, in_=in_, func=func, bias=bias, scale=scale),
                     reads=reads, writes=writes)

    def tt(out, in0, in1, op, reads, writes, eng="dve"):
        return P.add(eng, lambda e: e.tensor_tensor(out=out, in0=in0, in1=in1, op=op),
                     reads=reads, writes=writes)

    def ts(out, in0, s1, s2, op0, op1, reads, writes, eng="dve"):
        if s2 is None:
            return P.add(eng, lambda e: e.tensor_scalar(out=out, in0=in0, scalar1=s1, scalar2=None, op0=op0),
                         reads=reads, writes=writes)
        return P.add(eng, lambda e: e.tensor_scalar(out=out, in0=in0, scalar1=s1, scalar2=s2, op0=op0, op1=op1),
                     reads=reads, writes=writes)

    def stt(out, in0, scalar, in1, op0, op1, reads, writes, eng="dve"):
        return P.add(eng, lambda e: e.scalar_tensor_tensor(out=out, in0=in0, scalar=scalar, in1=in1,
                                                            op0=op0, op1=op1), reads=reads, writes=writes)

    def copy(out, in_, reads, writes, eng="dve"):
        return P.add(eng, lambda e: e.tensor_copy(out, in_), reads=reads, writes=writes)

    def memset(ap, val, writes, eng="dve"):
        return P.add(eng, lambda e: e.memset(ap, val), writes=writes)

    def mm(ps_ap, lhsT, rhs, start, stop, reads, bank):
        return P.add("pe", lambda e: e.matmul(ps_ap, lhsT=lhsT, rhs=rhs, start=start, stop=stop),
                     reads=reads, writes=[("ps", bank)])

    def load_unit(pieces):
        s = st["slot"]
        st["slot"] = (s + 1) % NSLOT
        views = []
        off = 0
        for pi, (src, kc) in enumerate(pieces):
            ncols = src.shape[1]
            v = WS[s][:, off:off + kc * ncols].rearrange("p (k c) -> p k c", c=ncols)
            srcv = src.rearrange("(k p) c -> p k c", p=128)
            P.add("pool", lambda e, v=v, srcv=srcv: e.dma_start(out=v, in_=srcv), writes=[("WS", s, pi)],
                  lane=f"w{s}", serialize=False)
            views.append(v)
            off += kc * ncols
        assert off <= SLOT_ELEMS
        return s, views

    def chain(ps_ap, bank, lhs_list, rhs_list, reads, first=True, last=True):
        n = len(lhs_list)
        for k in range(n):
            mm(ps_ap, lhs_list[k], rhs_list[k], start=(first and k == 0), stop=(last and k == n - 1),
               reads=reads, bank=bank)

    dma(PPt[:], pp.rearrange("l p c -> p l c"), writes=["PP"])
    dma(TRIL[:], tril, writes=["TRIL"])
    dma(BDM[:], bdmask, writes=["BDM"])
    dma(HMASK[:], hmask, writes=["HMASK"])
    dma(PCNT[:], pcnt, writes=["PCNT"])
    dma(CT[:], cT.rearrange("(k p) j -> p k j", p=128), writes=["CT"])
    memset(ONES[:], 1.0, ["ONES"])
    memset(HCONV[:], 0.0, ["HCONV"])
    memset(HPOOL[:], 0.0, ["HPOOL"])
    memset(HFFN[:], 0.0, ["HFFN"])
    act(SC[:], CT[:], AF.Silu, ["CT"], ["SC"])

    for l in range(DEPTH):
        for u in range(24):
            s, (wv,) = load_unit([(ada_w[l][:, u * 512:(u + 1) * 512], KC)])
            for m in range(4):
                mi = u * 4 + m
                b = next_bank()
                chain(PSB[b][:, 0:17], b, [wv[:, k, m * 128:(m + 1) * 128] for k in range(KC)],
                      [SC[:, k, :] for k in range(KC)], reads=[("WS", s, 0), ("WS", s, 1), "SC"])
                act(MODC[:, mi, :], PSB[b][:, 0:17], AF.Identity, [("ps", b), "PP"], ["MODC"],
                    bias=PPt[:, l, O_ADB + mi:O_ADB + mi + 1])
        for (c0, goff) in ((16, O_GPM), (64, O_GPF)):
            ts(MODC[:, c0:c0 + 16, :], MODC[:, c0:c0 + 16, :], 1.0, None, ALU.add, None,
               ["MODC"], ["MODC"])
        for (c0, goff) in ((16, O_GPM), (64, O_GPF), (32, O_GQM), (80, O_GQF)):
            tt(MODC[:, c0:c0 + 16, :], MODC[:, c0:c0 + 16, :],
               PPt[:, l, goff:goff + 16].unsqueeze(2).to_broadcast([128, 16, 17]), ALU.mult,
               ["MODC", "PP"], ["MODC"])
        dma(modsc[l], MODC[:].rearrange("p a b -> p (a b)"), reads=["MODC"], writes=[("MODSC", l)])

    dying = A.reset(ada_mark)
    P.fence(dying, [])
    base_mark = A.mark()

    for si, (P0, NP, HAS_S) in enumerate(SUPERS):
        TT = NP + (NS if HAS_S else 0)
        first_super = (si == 0)
        last_super = (si == len(SUPERS) - 1)
        pblocks = []
        o = 0
        while o < NP:
            n = min(512, NP - o)
            pblocks.append((o, n))
            o += n
        blocks = [(o_, n_, False) for (o_, n_) in pblocks] + ([(NP, NS, True)] if HAS_S else [])
        mblocks = [[o_, n_, [(o_, n_, False)]] for (o_, n_) in pblocks]
        if HAS_S:
            if mblocks[-1][1] + NS <= 512:
                mblocks[-1][1] += NS
                mblocks[-1][2].append((NP, NS, True))
            else:
                mblocks.append([NP, NS, [(NP, NS, True)]])
        assert len(mblocks) <= 2
        ntile = NP // 128

        for k in range(KC):
            dma(X[:, k, 0:NP], xp[k * 128:(k + 1) * 128, P0:P0 + NP], writes=[("X", k)])
            if HAS_S:
                dma(X[:, k, NP:NP + NS], xs[k * 128:(k + 1) * 128, :], writes=[("X", k)])

        def s3(ap):
            return ap.rearrange("p (s t) -> p s t", t=4)

        def bc_s(ap16):
            return ap16.unsqueeze(2).to_broadcast([128, 16, 4])

        def rstd_from(bank, o_, n_, scale):
            act(RSTD[:, o_:o_ + n_], PSB[bank][:, 0:n_], AF.Sqrt, [("ps", bank)], ["RSTD"],
                bias=EPS, scale=scale)
            P.add("dve", lambda e: e.reciprocal(out=RSTD[:, o_:o_ + n_], in_=RSTD[:, o_:o_ + n_]),
                  reads=["RSTD"], writes=["RSTD"])

        def prenorm(l, a_off, b_off):
            for bi_, (O_, N_, subs_) in enumerate(mblocks):
                b = 6 + bi_
                for k in range(KC):
                    q = next_sq()
                    act(SQ[q][:, 0:N_], X[:, k, O_:O_ + N_], AF.Square, [("X", k)], [("SQ", q)])
                    mm(PSB[b][:, 0:N_], ONES[:], SQ[q][:, 0:N_], start=(k == 0), stop=(k == KC - 1),
                       reads=["ONES", ("SQ", q)], bank=b)
                rstd_from(b, O_, N_, 1.0 / D)
            for (o_, n_, is_s) in blocks:
                for k in range(KC):
                    t_ = next_tmp()
                    tt(TMPF[t_][:, 0:n_], X[:, k, o_:o_ + n_], RSTD[:, o_:o_ + n_], ALU.mult,
                       [("X", k), "RSTD"], [("TMPF", t_)])
                    if not is_s:
                        act(H[:, k, o_:o_ + n_], TMPF[t_][:, 0:n_], AF.Identity,
                            [("TMPF", t_), "MODC"], [("H", k)],
                            bias=MODC[:, b_off + k, 0:1], scale=MODC[:, a_off + k, 0:1])
                    else:
                        tt(s3(TMPF[t_][:, 0:n_]), s3(TMPF[t_][:, 0:n_]), bc_s(MODC[:, a_off + k, 1:17]),
                           ALU.mult, [("TMPF", t_), "MODC"], [("TMPF", t_)])
                        tt(s3(H[:, k, o_:o_ + n_]), s3(TMPF[t_][:, 0:n_]), bc_s(MODC[:, b_off + k, 1:17]),
                           ALU.add, [("TMPF", t_), "MODC"], [("H", k)])

        def postnorm(l, g_off, YMO):
            for bi, (o_, n_, is_s) in enumerate(blocks):
                for mi in range(KC):
                    t_ = next_tmp()
                    tt(TMPF[t_][:, 0:n_], YMO[:, mi, o_:o_ + n_], RSTD[:, o_:o_ + n_], ALU.mult,
                       [("YMO", mi), "RSTD"], [("TMPF", t_)])
                    if not is_s:
                        stt(X[:, mi, o_:o_ + n_], TMPF[t_][:, 0:n_], MODC[:, g_off + mi, 0:1],
                            X[:, mi, o_:o_ + n_], ALU.mult, ALU.add,
                            [("TMPF", t_), "MODC", ("X", mi)], [("X", mi)])
                    else:
                        tt(s3(TMPF[t_][:, 0:n_]), s3(TMPF[t_][:, 0:n_]), bc_s(MODC[:, g_off + mi, 1:17]),
                           ALU.mult, [("TMPF", t_), "MODC"], [("TMPF", t_)])
                        tt(X[:, mi, o_:o_ + n_], X[:, mi, o_:o_ + n_], TMPF[t_][:, 0:n_], ALU.add,
                           [("TMPF", t_), ("X", mi)], [("X", mi)])

        def out_stage(l, YMO, units, rhs_fn, nk_total_fn):
            pass

        for l in range(DEPTH):
            m0 = A.mark()
            dma(MODC[:].rearrange("p a b -> p (a b)"), modsc[l], reads=[("MODSC", l)], writes=["MODC"])
            MIXACC = A.alloc("MIXACC", [128, KC, TT], BF16)
            CACC = A.alloc("CACC", [128, 8, TT], F32)
            XBUF = [A.alloc(f"XBUF{i}", [128, 30 + NP], F32) for i in range(2)]
            XSC = [A.alloc(f"XSC{i}", [128, 16, 34], F32) for i in range(2)]
            P.fence(set(), ["MIXACC", "CACC", "XBUF0", "XBUF1", "XSC0", "XSC1"])
            prenorm(l, 16, 0)
            mA = A.mark()

            def gated_out(l, src_buf, src_key, kc_src, w_src_fn, gate_col0, first, scale_off=None,
                          rhs_chunk_fn=None, pre_unit=None):
                for u in range(8):
                    if pre_unit is not None:
                        pre_unit(u)
                    c0 = u * 256
                    wsrc, kcs = w_src_fn(c0)
                    s, (wy, wg) = load_unit([(wsrc, kcs),
                                              (w_in[l][:, gate_col0 + c0:gate_col0 + c0 + 256], KC)])
                    for m in range(2):
                        mi = u * 2 + m
                        for (o_, n_, _sb) in mblocks:
                            b1 = next_bank()
                            chain(PSB[b1][:, 0:n_], b1, [wy[:, k, m * 128:(m + 1) * 128] for k in range(kcs)],
                                  [rhs_chunk_fn(mi, k, o_, n_) for k in range(kcs)],
                                  reads=[("WS", s, 0), ("WS", s, 1)] + [(src_key, kk) for kk in range(8)])
                            b2 = next_bank()
                            chain(PSB[b2][:, 0:n_], b2, [wg[:, k, m * 128:(m + 1) * 128] for k in range(KC)],
                                  [H[:, k, o_:o_ + n_] for k in range(KC)],
                                  reads=[("WS", s, 0), ("WS", s, 1)] + [("H", kk) for kk in range(KC)])
                            g = next_sig()
                            act(SIG[g][:, 0:n_], PSB[b2][:, 0:n_], AF.Sigmoid, [("ps", b2), "PP"], [("SIG", g)],
                                bias=PPt[:, l, O_BG + (gate_col0 - GATE0) // 128 + mi:
                                         O_BG + (gate_col0 - GATE0) // 128 + mi + 1])
                            if first:
                                tt(MIXACC[:, mi, o_:o_ + n_], PSB[b1][:, 0:n_], SIG[g][:, 0:n_], ALU.mult,
                                   [("ps", b1), ("SIG", g)], [("MIXACC", mi)])
                            else:
                                t_ = next_tmp()
                                if scale_off is None:
                                    tt(TMPF[t_][:, 0:n_], PSB[b1][:, 0:n_], SIG[g][:, 0:n_], ALU.mult,
                                       [("ps", b1), ("SIG", g)], [("TMPF", t_)])
                                else:
                                    stt(TMPF[t_][:, 0:n_], PSB[b1][:, 0:n_],
                                        PPt[:, l, scale_off + mi:scale_off + mi + 1], SIG[g][:, 0:n_],
                                        ALU.mult, ALU.mult, [("ps", b1), ("SIG", g), "PP"], [("TMPF", t_)])
                                tt(MIXACC[:, mi, o_:o_ + n_], MIXACC[:, mi, o_:o_ + n_], TMPF[t_][:, 0:n_],
                                   ALU.add, [("MIXACC", mi), ("TMPF", t_)], [("MIXACC", mi)])

            NTT = ntile + (1 if HAS_S else 0)
            VN = A.alloc("VN", [128, NTT, DA], BF16)
            U = A.alloc("U", [128, 8, TT], BF16)
            mA1 = A.mark()
            VF = A.alloc("VF", [128, NTT, DA], F32)
            LNG = A.alloc("LNG", [128, DA], F32)
            LNB = A.alloc("LNB", [128, DA], F32)
            P.fence(set(), ["VN", "U", "VF", "LNG", "LNB"])
            dma(LNG[:], lnvg[l:l + 1, :].partition_broadcast(128), writes=["LNG"])
            dma(LNB[:], lnvb[l:l + 1, :].partition_broadcast(128), writes=["LNB"])
            tiles = [(t_ * 128, 128, t_) for t_ in range(ntile)] + ([(NP, NS, ntile)] if HAS_S else [])
            for uu in range(2):
                s, (wv,) = load_unit([(w_in[l][:, DA + uu * 512:DA + (uu + 1) * 512], KC)])
                for (o_, n_, ti) in tiles:
                    b = next_bank()
                    chain(PSB[b][0:n_, 0:512], b, [H[:, k, o_:o_ + n_] for k in range(KC)],
                          [wv[:, k, :] for k in range(KC)], reads=[("WS", s, 0), ("WS", s, 1)] + [("H", kk) for kk in range(KC)])
                    act(VF[0:n_, ti, uu * 512:(uu + 1) * 512], PSB[b][0:n_, 0:512], AF.Gelu_apprx_tanh,
                        [("ps", b)], [("VF", ti)])
            for (o_, n_, ti) in tiles:
                for hh in range(2):
                    P.add("dve", lambda e, n_=n_, ti=ti, hh=hh, VF=VF: e.bn_stats(
                        out=MV[0:n_, hh * 6:(hh + 1) * 6], in_=VF[0:n_, ti, hh * 512:(hh + 1) * 512]),
                        reads=[("VF", ti)], writes=["MV"])
                P.add("dve", lambda e, n_=n_: e.bn_aggr(out=MV[0:n_, 16:18], in_=MV[0:n_, 0:12]),
                      reads=["MV"], writes=["MV"])
                act(MV[0:n_, 18:19], MV[0:n_, 17:18], AF.Sqrt, ["MV"], ["MV"], bias=EPS, scale=1.0)
                P.add("dve", lambda e, n_=n_: e.reciprocal(out=MV[0:n_, 18:19], in_=MV[0:n_, 18:19]),
                      reads=["MV"], writes=["MV"])
                ts(VF[0:n_, ti, :], VF[0:n_, ti, :], MV[0:n_, 16:17], MV[0:n_, 18:19], ALU.subtract, ALU.mult,
                   [("VF", ti), "MV"], [("VF", ti)])
                tt(VF[0:n_, ti, :], VF[0:n_, ti, :], LNG[0:n_, :], ALU.mult, [("VF", ti), "LNG"], [("VF", ti)])
                if ti < ntile:
                    tt(VN[0:n_, ti, :], VF[0:n_, ti, :], LNB[0:n_, :], ALU.add, [("VF", ti), "LNB"], [("VN", ti)])
                else:
                    tt(VF[0:n_, ti, :], VF[0:n_, ti, :], LNB[0:n_, :], ALU.add, [("VF", ti), "LNB"], [("VF", ti)])
                    act(VN[0:n_, ti, :], VF[0:n_, ti, :], AF.Identity, [("VF", ti)], [("VN", ti)])
                    out_toks.append(dma(o_v_s[l], VF[0:NS, ti, :], reads=[("VF", ti)]))
            for uu in range(2):
                s, (wv,) = load_unit([(w_in[l][:, uu * 512:(uu + 1) * 512], KC)])
                for m in range(4):
                    mi = uu * 4 + m
                    for (o_, n_, _sb) in mblocks:
                        b = next_bank()
                        chain(PSB[b][:, 0:n_], b, [wv[:, k, m * 128:(m + 1) * 128] for k in range(KC)],
                              [H[:, k, o_:o_ + n_] for k in range(KC)],
                              reads=[("WS", s, 0), ("WS", s, 1)] + [("H", kk) for kk in range(KC)])
                        act(U[:, mi, o_:o_ + n_], PSB[b][:, 0:n_], AF.Gelu_apprx_tanh, [("ps", b)], [("U", mi)])
            dying = A.reset(mA1)
            WST = A.alloc("WST", [128, 8, 128], F32)
            WSB = A.alloc("WSB", [128, 8, 128], BF16)
            BSF = A.alloc("BSF", [1, 1024], F32)
            BSR = A.alloc("BSR", [1, 1024], BF16)
            WS4 = A.alloc("WS4", [64, 32], F32)
            BDB = A.alloc("BDB", [64, 8, 64], BF16)
            BS4F = A.alloc("BS4F", [1, 512], F32)
            BS4R = A.alloc("BS4R", [1, 512], BF16)
            P.fence(dying, ["WST", "WSB", "BSF", "BSR", "WS4", "BDB", "BS4F", "BS4R"])
            dma(WST[:], wsT[l].rearrange("p (g t) -> p g t", t=128), writes=["WST"])
            tt(WSB[:], WST[:], TRIL[:].unsqueeze(1).to_broadcast([128, 8, 128]), ALU.mult,
               ["WST", "TRIL"], ["WSB"])
            dma(BSF[:], bsp[l:l + 1, :], writes=["BSF"])
            copy(BSR[:], BSF[:], ["BSF"], ["BSR"])
            if HAS_S:
                dma(WS4[:], ws4[l], writes=["WS4"])
                tt(BDB[:].rearrange("p g (s t) -> p g s t", t=4),
                   WS4[:].rearrange("p (g t) -> p g t", t=4).unsqueeze(2).to_broadcast([64, 8, 16, 4]),
                   BDM[:].rearrange("p (s t) -> p s t", t=4).unsqueeze(1).to_broadcast([64, 8, 16, 4]),
                   ALU.mult, ["WS4", "BDM"], ["BDB"])
                dma(BS4F[:], bs4[l:l + 1, :], writes=["BS4F"])
                copy(BS4R[:], BS4F[:], ["BS4F"], ["BS4R"])
            for (o_, n_, ti) in tiles:
                for g in range(8):
                    b = next_bank()
                    if ti < ntile:
                        mm(PSB[b][:, 0:128], VN[:, ti, g * 128:(g + 1) * 128], WSB[:, g, :], True, False,
                           [("VN", ti), "WSB"], b)
                        mm(PSB[b][:, 0:128], ONES[0:1, :], BSR[0:1, g * 128:(g + 1) * 128], False, True,
                           ["ONES", "BSR"], b)
                    else:
                        mm(PSB[b][:, 0:NS], VN[0:NS, ti, g * 128:(g + 1) * 128], BDB[:, g, :], True, False,
                           [("VN", ti), "BDB"], b)
                        mm(PSB[b][:, 0:NS], ONES[0:1, :], BS4R[0:1, g * NS:(g + 1) * NS], False, True,
                           ["ONES", "BS4R"], b)
                    tt(U[:, g, o_:o_ + n_], PSB[b][:, 0:n_], U[:, g, o_:o_ + n_], ALU.mult,
                       [("ps", b), ("U", g)], [("U", g)])

            def glu_unit(uo):
                c0 = uo * 256
                s, (wa, wb) = load_unit([(w_in[l][:, 2 * DA + c0:2 * DA + c0 + 256], KC),
                                         (w_in[l][:, 3 * DA + c0:3 * DA + c0 + 256], KC)])
                for m in range(2):
                    ci = uo * 2 + m
                    xb_ = XBUF[ci % 2]
                    xk = f"XBUF{ci % 2}"
                    xs_ = XSC[ci % 2]
                    sk = f"XSC{ci % 2}"
                    copy(xb_[:, 0:30], HCONV[:, l, ci, :], ["HCONV"], [xk])
                    if HAS_S:
                        dma(xs_[:, :, 0:30], st_conv[l][ci * 128:(ci + 1) * 128, :].rearrange("p (s r) -> p s r", r=30),
                            writes=[sk])
                    for (O_, N_, subs_) in mblocks:
                        b1 = next_bank()
                        chain(PSB[b1][:, 0:N_], b1, [wa[:, k, m * 128:(m + 1) * 128] for k in range(KC)],
                              [H[:, k, O_:O_ + N_] for k in range(KC)],
                              reads=[("WS", s, 0), ("WS", s, 1)] + [("H", kk) for kk in range(KC)])
                        b2 = next_bank()
                        chain(PSB[b2][:, 0:N_], b2, [wb[:, k, m * 128:(m + 1) * 128] for k in range(KC)],
                              [H[:, k, O_:O_ + N_] for k in range(KC)],
                              reads=[("WS", s, 0), ("WS", s, 1)] + [("H", kk) for kk in range(KC)])
                        g = next_sig()
                        act(SIG[g][:, 0:N_], PSB[b2][:, 0:N_], AF.Sigmoid, [("ps", b2)], [("SIG", g)])
                        for (o_, n_, is_s) in subs_:
                            r0 = o_ - O_
                            if not is_s:
                                tt(xb_[:, 30 + o_:30 + o_ + n_], PSB[b1][:, r0:r0 + n_], SIG[g][:, r0:r0 + n_], ALU.mult,
                                   [("ps", b1), ("SIG", g)], [xk])
                            else:
                                tt(xs_[:, :, 30:34], s3(PSB[b1][:, r0:r0 + n_]), s3(SIG[g][:, r0:r0 + n_]), ALU.mult,
                                   [("ps", b1), ("SIG", g)], [sk])
                    if first_super:
                        ts(xb_[:, 30:30 + HALO], xb_[:, 30:30 + HALO], HMASK[:, 0:1], None, ALU.mult, None,
                           [xk, "HMASK"], [xk])
                    wcol = lambda tap: PPt[:, l, O_WDW + tap * 8 + ci:O_WDW + tap * 8 + ci + 1]
                    ts(CACC[:, ci, 0:NP], xb_[:, 0:NP], wcol(0), PPt[:, l, O_BDW + ci:O_BDW + ci + 1],
                       ALU.mult, ALU.add, [xk, "PP"], [("CACC", ci)])
                    for tap in range(1, 31):
                        stt(CACC[:, ci, 0:NP], xb_[:, tap:tap + NP], wcol(tap), CACC[:, ci, 0:NP], ALU.mult, ALU.add,
                            [xk, "PP", ("CACC", ci)], [("CACC", ci)])
                    copy(HCONV[:, l, ci, :], xb_[:, NP:NP + 30], [xk], ["HCONV"])
                    if last_super:
                        out_toks.append(dma(o_conv_p[l][ci * 128:(ci + 1) * 128, :], xb_[:, NP:NP + 30], reads=[xk]))
                    if HAS_S:
                        cs_ = s3(CACC[:, ci, NP:NP + NS])
                        ts(cs_, xs_[:, :, 0:4], wcol(0), PPt[:, l, O_BDW + ci:O_BDW + ci + 1], ALU.mult, ALU.add,
                           [sk, "PP"], [("CACC", ci)])
                        for tap in range(1, 31):
                            stt(cs_, xs_[:, :, tap:tap + 4], wcol(tap), cs_, ALU.mult, ALU.add,
                                [sk, "PP", ("CACC", ci)], [("CACC", ci)])
                        out_toks.append(dma(o_conv_s[l][ci * 128:(ci + 1) * 128, :].rearrange("p (s r) -> p s r", r=30),
                                            xs_[:, :, 4:34], reads=[sk]))
            gated_out(l, U, "U", 8, lambda c0: (w_a_out[l][:, c0:c0 + 256], 8), GATE0, True,
                      rhs_chunk_fn=lambda mi, k, o_, n_: U[:, k, o_:o_ + n_],
                      pre_unit=lambda u: glu_unit(u // 2) if u % 2 == 0 else None)
            dying = A.reset(mA)
            YBIN = A.alloc("YBIN", [128, 8, TT], BF16)
            MEAN = A.alloc("MEAN", [128, 512], F32)
            VAR = A.alloc("VAR", [128, 512], F32)
            P.fence(dying, ["YBIN", "MEAN", "VAR"])
            for (o_, n_, _sb) in mblocks:
                for ci in range(8):
                    q = next_sq()
                    act(SQ[q][:, 0:n_], CACC[:, ci, o_:o_ + n_], AF.Square, [("CACC", ci)], [("SQ", q)])
                    mm(PSB[7][:, 0:n_], ONES[:], SQ[q][:, 0:n_], ci == 0, ci == 7, ["ONES", ("SQ", q)], 7)
                    q2 = next_sq()
                    copy(SQ[q2][:, 0:n_], CACC[:, ci, o_:o_ + n_], [("CACC", ci)], [("SQ", q2)])
                    mm(PSB[6][:, 0:n_], ONES[:], SQ[q2][:, 0:n_], ci == 0, ci == 7, ["ONES", ("SQ", q2)], 6)
                ts(MEAN[:, 0:n_], PSB[6][:, 0:n_], 1.0 / DA, None, ALU.mult, None, [("ps", 6)], ["MEAN"])
                tt(VAR[:, 0:n_], MEAN[:, 0:n_], MEAN[:, 0:n_], ALU.mult, ["MEAN"], ["VAR"])
                stt(VAR[:, 0:n_], PSB[7][:, 0:n_], 1.0 / DA, VAR[:, 0:n_], ALU.mult, ALU.subtract,
                    [("ps", 7), "VAR"], ["VAR"])
                ts(VAR[:, 0:n_], VAR[:, 0:n_], 0.0, None, ALU.max, None, ["VAR"], ["VAR"])
                act(VAR[:, 0:n_], VAR[:, 0:n_], AF.Sqrt, ["VAR"], ["VAR"], bias=EPS, scale=1.0)
                P.add("dve", lambda e, n_=n_, VAR=VAR: e.reciprocal(out=VAR[:, 0:n_], in_=VAR[:, 0:n_]),
                      reads=["VAR"], writes=["VAR"])
                for ci in range(8):
                    t_ = next_tmp()
                    tt(TMPF[t_][:, 0:n_], CACC[:, ci, o_:o_ + n_], MEAN[:, 0:n_], ALU.subtract,
                       [("CACC", ci), "MEAN"], [("TMPF", t_)])
                    tt(TMPF[t_][:, 0:n_], TMPF[t_][:, 0:n_], VAR[:, 0:n_], ALU.mult, [("TMPF", t_), "VAR"],
                       [("TMPF", t_)])
                    act(YBIN[:, ci, o_:o_ + n_], TMPF[t_][:, 0:n_], AF.Silu, [("TMPF", t_), "PP"], [("YBIN", ci)],
                        bias=PPt[:, l, O_LCB + ci:O_LCB + ci + 1], scale=PPt[:, l, O_LCG + ci:O_LCG + ci + 1])
            gated_out(l, YBIN, "YBIN", 8, lambda c0: (w_b_out[l][:, c0:c0 + 256], 8), GATE0 + D, False,
                      rhs_chunk_fn=lambda mi, k, o_, n_: YBIN[:, k, o_:o_ + n_])

            dying = A.reset(mA)
            LP = 15 + NP
            PBUF = A.alloc("PBUF", [128, 8, LP], F32)
            POOLED = A.alloc("POOLED", [128, 8, TT], BF16)
            PT = [A.alloc(f"PT{i}", [128, 2, LP], F32) for i in range(2)]
            PSC = A.alloc("PSC", [128, 8, 16, 19], F32)
            PTS = [A.alloc(f"PTS{i}", [128, 2, 16, 19], F32) for i in range(2)]
            T16 = A.alloc("T16", [128, 2, 16], F32)
            P.fence(dying, ["PBUF", "POOLED", "PT0", "PT1", "PSC", "PTS0", "PTS1", "T16"])
            for ci in range(8):
                copy(PBUF[:, ci, 0:15], HPOOL[:, l, ci, :], ["HPOOL"], [("PBUF", ci)])
            if HAS_S:
                for ci in range(8):
                    dma(PSC[:, ci, :, 0:15], st_pool[l][ci * 128:(ci + 1) * 128, :].rearrange("p (s r) -> p s r", r=15),
                        writes=[("PSC", ci)])
            for uu in range(2):
                s, (wv,) = load_unit([(w_in[l][:, 4 * DA + uu * 512:4 * DA + (uu + 1) * 512], KC)])
                for m in range(4):
                    ci = uu * 4 + m
                    for (O_, N_, subs_) in mblocks:
                        b = next_bank()
                        chain(PSB[b][:, 0:N_], b, [wv[:, k, m * 128:(m + 1) * 128] for k in range(KC)],
                              [H[:, k, O_:O_ + N_] for k in range(KC)],
                              reads=[("WS", s, 0), ("WS", s, 1)] + [("H", kk) for kk in range(KC)])
                        for (o_, n_, is_s) in subs_:
                            r0 = o_ - O_
                            if not is_s:
                                act(PBUF[:, ci, 15 + o_:15 + o_ + n_], PSB[b][:, r0:r0 + n_], AF.Identity, [("ps", b)],
                                    [("PBUF", ci)])
                            else:
                                act(PSC[:, ci, :, 15:19], s3(PSB[b][:, r0:r0 + n_]), AF.Identity, [("ps", b)],
                                    [("PSC", ci)])
                    if first_super:
                        ts(PBUF[:, ci, 15:15 + HALO], PBUF[:, ci, 15:15 + HALO], HMASK[:, 0:1], None, ALU.mult, None,
                           [("PBUF", ci), "HMASK"], [("PBUF", ci)])
                    copy(HPOOL[:, l, ci, :], PBUF[:, ci, NP:NP + 15], [("PBUF", ci)], ["HPOOL"])
                    if last_super:
                        out_toks.append(dma(o_pool_p[l][ci * 128:(ci + 1) * 128, :], PBUF[:, ci, NP:NP + 15],
                                            reads=[("PBUF", ci)]))
                    if HAS_S:
                        out_toks.append(dma(o_pool_s[l][ci * 128:(ci + 1) * 128, :].rearrange("p (s r) -> p s r", r=15),
                                            PSC[:, ci, :, 4:19], reads=[("PSC", ci)]))
            for gi in range(4):
                w = 2 << gi
                c2 = slice(2 * gi, 2 * gi + 2)
                rk = [("PBUF", 2 * gi), ("PBUF", 2 * gi + 1)]
                src = PBUF[:, c2, :]
                srck = rk
                sh = 1
                lo = 0
                for stp in range(gi + 1):
                    dst = PT[stp % 2]
                    dk = [f"PT{stp % 2}"]
                    nlo = lo + sh
                    tt(dst[:, :, nlo:LP], src[:, :, nlo:LP], src[:, :, lo:LP - sh], ALU.add, srck, dk)
                    src = dst; srck = dk; lo = nlo; sh *= 2
                stt(POOLED[:, c2, 0:NP], src[:, :, 15:15 + NP], 1.0 / w, PBUF[:, c2, 15:15 + NP], ALU.mult,
                    ALU.subtract, srck + rk, [("POOLED", 2 * gi), ("POOLED", 2 * gi + 1)])
                if first_super:
                    a0 = 15 + HALO
                    tt(T16[:], src[:, :, a0:a0 + 16],
                       PCNT[:, gi * 16:(gi + 1) * 16].unsqueeze(1).to_broadcast([128, 2, 16]), ALU.mult,
                       srck + ["PCNT"], ["T16"])
                    tt(POOLED[:, c2, HALO:HALO + 16], T16[:], PBUF[:, c2, a0:a0 + 16], ALU.subtract,
                       ["T16"] + rk, [("POOLED", 2 * gi), ("POOLED", 2 * gi + 1)])
                if HAS_S:
                    rks = [("PSC", 2 * gi), ("PSC", 2 * gi + 1)]
                    src = PSC[:, c2, :, :]
                    srck = rks
                    sh = 1
                    lo = 0
                    for stp in range(gi + 1):
                        dst = PTS[stp % 2]
                        dk = [f"PTS{stp % 2}"]
                        nlo = lo + sh
                        tt(dst[:, :, :, nlo:19], src[:, :, :, nlo:19], src[:, :, :, lo:19 - sh], ALU.add, srck, dk)
                        src = dst; srck = dk; lo = nlo; sh *= 2
                    stt(POOLED[:, c2, NP:NP + NS].rearrange("p c (s t) -> p c s t", t=4), src[:, :, :, 15:19],
                        1.0 / w, PSC[:, c2, :, 15:19], ALU.mult, ALU.subtract, srck + rks,
                        [("POOLED", 2 * gi), ("POOLED", 2 * gi + 1)])
            gated_out(l, POOLED, "POOLED", 2,
                      lambda c0: (w_pool[l][c0 // 512][:, (c0 % 512):(c0 % 512) + 256], 2), GATE0 + 2 * D, False,
                      scale_off=O_PS,
                      rhs_chunk_fn=lambda mi, k, o_, n_: POOLED[:, 2 * (mi // 4) + k, o_:o_ + n_])

            dying = A.reset(mA)
            YMO = A.alloc("YMO", [128, KC, TT], F32)
            P.fence(dying, ["YMO"])

            def proj_out(l, units_fn, nunits, rhs_fn, rhs_keys, kparts):
                for u in range(nunits):
                    banks = {}
                    for kp in range(kparts):
                        src, kc_, koff = units_fn(u, kp)
                        s, (wv,) = load_unit([(src, kc_)])
                        for m in range(2):
                            for bi, (o_, n_, _sb) in enumerate(mblocks):
                                if kp == 0:
                                    banks[(m, bi)] = next_bank()
                                b = banks[(m, bi)]
                                chain(PSB[b][:, 0:n_], b, [wv[:, k, m * 128:(m + 1) * 128] for k in range(kc_)],
                                      [rhs_fn(koff + k, o_, n_) for k in range(kc_)],
                                      reads=[("WS", s, 0), ("WS", s, 1)] + rhs_keys, first=(kp == 0), last=(kp == kparts - 1))
                    for m in range(2):
                        mi = u * 2 + m
                        for bi, (o_, n_, _sb) in enumerate(mblocks):
                            b = banks[(m, bi)]
                            act(YMO[:, mi, o_:o_ + n_], PSB[b][:, 0:n_], AF.Identity, [("ps", b)], [("YMO", mi)])
                            q = next_sq()
                            act(SQ[q][:, 0:n_], PSB[b][:, 0:n_], AF.Square, [("ps", b)], [("SQ", q)])
                            sb_ = 6 + bi
                            mm(PSB[sb_][:, 0:n_], ONES[:],
                               SQ[q][:, 0:n_], mi == 0, mi == KC - 1, ["ONES", ("SQ", q)], sb_)
                for bi, (o_, n_, _sb) in enumerate(mblocks):
                    sb_ = 6 + bi
                    c0_ = 0
                    act(RSTD[:, o_:o_ + n_], PSB[sb_][:, c0_:c0_ + n_], AF.Sqrt, [("ps", sb_)], ["RSTD"],
                        bias=EPS, scale=1.0 / D)
                    P.add("dve", lambda e, o_=o_, n_=n_: e.reciprocal(out=RSTD[:, o_:o_ + n_], in_=RSTD[:, o_:o_ + n_]),
                          reads=["RSTD"], writes=["RSTD"])

            proj_out(l, lambda u, kp: (w_o[l][:, u * 256:(u + 1) * 256], KC, 0), 8,
                     lambda k, o_, n_: MIXACC[:, k, o_:o_ + n_], [("MIXACC", kk) for kk in range(KC)], 1)
            postnorm(l, 32, YMO)

            dying = A.reset(m0)
            ACTB = A.alloc("ACTB", [128, NFC, TT], BF16)
            YMO = A.alloc("YMO", [128, KC, TT], F32)
            GSB = [A.alloc(f"GSB{i}", [128, 2 + NP], F32) for i in range(2)]
            GSS = [A.alloc(f"GSS{i}", [128, 16, 6], F32) for i in range(2)]
            FACC = [A.alloc(f"FACC{i}", [128, TT], F32) for i in range(2)]
            SFF = A.alloc("SFF", [128, NFC, 16, 2], F32)
            OFP = A.alloc("OFP", [128, NFC, 2], F32)
            P.fence(dying, ["ACTB", "YMO", "GSB0", "GSB1", "GSS0", "GSS1", "FACC0", "FACC1", "SFF", "OFP"])
            prenorm(l, 64, 48)
            if HAS_S:
                for j in range(NFC):
                    dma(SFF[:, j, :, :], st_ffn[l][j * 128:(j + 1) * 128, :].rearrange("p (s r) -> p s r", r=2),
                        writes=[("SFF", j)])
            for u in range(22):
                f0 = u * 256
                nf = min(256, DFF - f0)
                s, (wg, wv) = load_unit([(w_up[l][:, f0:f0 + nf], KC), (w_up[l][:, DFF + f0:DFF + f0 + nf], KC)])
                for m in range(nf // 128):
                    j = u * 2 + m
                    gb_ = GSB[j % 2]; gk = f"GSB{j % 2}"
                    gs_ = GSS[j % 2]; gsk = f"GSS{j % 2}"
                    fa_ = FACC[j % 2]; fk = f"FACC{j % 2}"
                    copy(gb_[:, 0:2], HFFN[:, l, j, :], ["HFFN"], [gk])
                    if HAS_S:
                        copy(gs_[:, :, 0:2], SFF[:, j, :, :], [("SFF", j)], [gsk])
                    for (O_, N_, subs_) in mblocks:
                        b1 = next_bank()
                        chain(PSB[b1][:, 0:N_], b1, [wg[:, k, m * 128:(m + 1) * 128] for k in range(KC)],
                              [H[:, k, O_:O_ + N_] for k in range(KC)],
                              reads=[("WS", s, 0), ("WS", s, 1)] + [("H", kk) for kk in range(KC)])
                        for (o_, n_, is_s) in subs_:
                            r0 = o_ - O_
                            if not is_s:
                                act(gb_[:, 2 + o_:2 + o_ + n_], PSB[b1][:, r0:r0 + n_], AF.Identity, [("ps", b1)], [gk])
                            else:
                                act(gs_[:, :, 2:6], s3(PSB[b1][:, r0:r0 + n_]), AF.Identity, [("ps", b1)], [gsk])
                    if first_super:
                        ts(gb_[:, 2:2 + HALO], gb_[:, 2:2 + HALO], HMASK[:, 0:1], None, ALU.mult, None,
                           [gk, "HMASK"], [gk])
                    wc = lambda tap: PPt[:, l, O_WFC + tap * NFC + j:O_WFC + tap * NFC + j + 1]
                    bcol = PPt[:, l, O_BFC + j:O_BFC + j + 1]
                    ts(fa_[:, 0:NP], gb_[:, 0:NP], wc(0), bcol, ALU.mult, ALU.add, [gk, "PP"], [fk])
                    for tap in (1, 2):
                        stt(fa_[:, 0:NP], gb_[:, tap:tap + NP], wc(tap), fa_[:, 0:NP], ALU.mult, ALU.add,
                            [gk, "PP", fk], [fk])
                    copy(HFFN[:, l, j, :], gb_[:, NP:NP + 2], [gk], ["HFFN"])
                    if last_super:
                        copy(OFP[:, j, :], gb_[:, NP:NP + 2], [gk], ["OFP"])
                    if HAS_S:
                        fs_ = s3(fa_[:, NP:NP + NS])
                        ts(fs_, gs_[:, :, 0:4], wc(0), bcol, ALU.mult, ALU.add, [gsk, "PP"], [fk])
                        for tap in (1, 2):
                            stt(fs_, gs_[:, :, tap:tap + 4], wc(tap), fs_, ALU.mult, ALU.add, [gsk, "PP", fk], [fk])
                        copy(SFF[:, j, :, :], gs_[:, :, 4:6], [gsk], [("SFF", j)])
                    act(fa_[:, 0:TT], fa_[:, 0:TT], AF.Gelu_apprx_tanh, [fk], [fk])
                    for (o_, n_, _sb) in mblocks:
                        b2 = next_bank()
                        chain(PSB[b2][:, 0:n_], b2, [wv[:, k, m * 128:(m + 1) * 128] for k in range(KC)],
                              [H[:, k, o_:o_ + n_] for k in range(KC)],
                              reads=[("WS", s, 0), ("WS", s, 1)] + [("H", kk) for kk in range(KC)])
                        tt(ACTB[:, j, o_:o_ + n_], PSB[b2][:, 0:n_], fa_[:, o_:o_ + n_], ALU.mult,
                           [("ps", b2), fk], [("ACTB", j)])
            if last_super:
                out_toks.append(dma(o_ffn_p[l].rearrange("(j p) r -> p j r", p=128), OFP[:], reads=["OFP"]))
            if HAS_S:
                out_toks.append(dma(o_ffn_s[l].rearrange("(j p) (s r) -> p j s r", p=128, r=2), SFF[:],
                                    reads=[("SFF", j) for j in range(NFC)]))
            KH = [(0, 22), (22, 21)]
            proj_out(l, lambda u, kp: (w_down[l][KH[kp][0] * 128:(KH[kp][0] + KH[kp][1]) * 128, u * 256:(u + 1) * 256],
                                       KH[kp][1], KH[kp][0]), 8,
                     lambda k, o_, n_: ACTB[:, k, o_:o_ + n_], [("ACTB", jj) for jj in range(NFC)], 2)
            postnorm(l, 80, YMO)
            dying = A.reset(m0)
            P.fence(dying, ["MIXACC", "VN", "U", "VF", "LNG", "LNB"])

        for k in range(KC):
            c_lo = max(P0, HALO)
            if c_lo < P0 + NP:
                out_toks.append(dma(yp[k * 128:(k + 1) * 128, c_lo - HALO:P0 + NP - HALO],
                                    X[:, k, c_lo - P0:NP], reads=[("X", k)]))
            if HAS_S:
                out_toks.append(dma(ys[k * 128:(k + 1) * 128, :], X[:, k, NP:NP + NS], reads=[("X", k)]))

    run = P.make_runner(sems, lanes, final_waits=out_toks)
    with nc.Block() as block:
        @block.sync
        def _(e):
            run("sp", e)

        @block.tensor
        def _(e):
            run("pe", e)

        @block.scalar
        def _(e):
            run("act", e)

        @block.vector
        def _(e):
            run("dve", e)

        @block.gpsimd
        def _(e):
            run("pool", e)
    es.close()
    return nc, A.peak


def _chunkT(v):
    return np.ascontiguousarray(v.reshape(-1, 128).T)


def kernel(x_prompt, x_sample, c_prompt, c_sample, state_conv, state_pool, state_ffn_conv,
           ada_w, ada_b, g_pre_mix, g_post_mix, g_pre_ffn, g_post_ffn, w_in, b_gate,
           ln_v_g, ln_v_b, w_spatial, b_spatial, w_a_out, w_dwconv, b_dwconv, ln_conv_g,
           ln_conv_b, w_b_out, w_pool_grp, pool_scale, w_o, w_up, w_ffn_conv, b_ffn_conv, w_down):
    f = lambda a: np.ascontiguousarray(np.asarray(a, dtype=np.float32))
    x_prompt, x_sample, c_prompt, c_sample = f(x_prompt), f(x_sample), f(c_prompt), f(c_sample)
    state_conv, state_pool, state_ffn_conv = f(state_conv), f(state_pool), f(state_ffn_conv)
    w_spatial = f(w_spatial); b_spatial = f(b_spatial)

    pp = np.zeros((DEPTH, 128, NPP), np.float32)
    for l in range(DEPTH):
        pp[l, :, O_GPM:O_GPM + 16] = _chunkT(f(g_pre_mix)[l])
        pp[l, :, O_GQM:O_GQM + 16] = _chunkT(f(g_post_mix)[l])
        pp[l, :, O_GPF:O_GPF + 16] = _chunkT(f(g_pre_ffn)[l])
        pp[l, :, O_GQF:O_GQF + 16] = _chunkT(f(g_post_ffn)[l])
        pp[l, :, O_BG:O_BG + 48] = _chunkT(f(b_gate)[l])
        pp[l, :, O_PS:O_PS + 16] = _chunkT(f(pool_scale)[l])
        pp[l, :, O_BDW:O_BDW + 8] = _chunkT(f(b_dwconv)[l])
        pp[l, :, O_LCG:O_LCG + 8] = _chunkT(f(ln_conv_g)[l])
        pp[l, :, O_LCB:O_LCB + 8] = _chunkT(f(ln_conv_b)[l])
        for tap in range(31):
            pp[l, :, O_WDW + tap * 8:O_WDW + tap * 8 + 8] = _chunkT(f(w_dwconv)[l, tap])
        for tap in range(3):
            pp[l, :, O_WFC + tap * NFC:O_WFC + (tap + 1) * NFC] = _chunkT(f(w_ffn_conv)[l, tap])
        pp[l, :, O_BFC:O_BFC + NFC] = _chunkT(f(b_ffn_conv)[l])
        pp[l, :, O_ADB:O_ADB + 96] = _chunkT(f(ada_b)[l])
    wsT = np.ascontiguousarray(w_spatial.transpose(0, 3, 1, 2)).reshape(DEPTH, 128, 8 * 128)
    bsp = np.ascontiguousarray(b_spatial.reshape(DEPTH, 8 * 128))
    w4 = w_spatial[:, :, 0:4, 0:4].transpose(0, 3, 1, 2)
    ws4 = np.ascontiguousarray(np.tile(w4.reshape(DEPTH, 1, 4, 32), (1, 16, 1, 1)).reshape(DEPTH, 64, 32))
    bs4 = np.ascontiguousarray(np.tile(b_spatial[:, :, None, 0:4], (1, 1, 16, 1)).reshape(DEPTH, 8 * 64))
    tril = np.triu(np.ones((128, 128), np.float32))
    bdm = np.zeros((64, 16, 4), np.float32)
    for sq in range(16):
        for s_ in range(4):
            bdm[sq * 4 + s_, sq, s_:] = 1.0
    bdm = bdm.reshape(64, 64)

    shared = dict(pp=pp, lnvg=f(ln_v_g), lnvb=f(ln_v_b), wsT=wsT, bsp=bsp, ws4=ws4, bs4=bs4, tril=tril,
                  bdmask=bdm, ada_w=f(ada_w), w_in=f(w_in), w_a_out=f(w_a_out), w_b_out=f(w_b_out),
                  w_pool=f(w_pool_grp), w_o=f(w_o), w_up=f(w_up), w_down=f(w_down))
    in_maps = []
    for c in range(NCORES):
        seq, half = c // 2, c % 2
        xpc = np.zeros((NPROMPT, D), np.float32)
        if half == 0:
            xpc[HALO:] = x_prompt[seq, 0:1024]
        else:
            xpc = x_prompt[seq, 1024 - HALO:2048]
        ss = slice(c * 16, (c + 1) * 16)
        cT = np.concatenate([c_prompt[seq][None, :], c_sample[ss]], axis=0).T
        pcnt = np.zeros((4, 16), np.float32)
        for gi, w in enumerate((2, 4, 8, 16)):
            for i in range(16):
                pos = half * 1024 + i
                pcnt[gi, i] = 1.0 / min(pos + 1, w)
        m = dict(shared)
        m.update(
            xp=np.ascontiguousarray(xpc.T), xs=np.ascontiguousarray(x_sample[ss].reshape(NS, D).T),
            cT=np.ascontiguousarray(cT), hmask=np.full((128, 1), float(half), np.float32),
            pcnt=np.ascontiguousarray(np.tile(pcnt.reshape(1, 64), (128, 1))),
            st_conv=np.ascontiguousarray(state_conv[:, ss].transpose(0, 3, 1, 2)).reshape(DEPTH, DA, 16 * 30),
            st_pool=np.ascontiguousarray(state_pool[:, ss].transpose(0, 3, 1, 2)).reshape(DEPTH, DA, 16 * 15),
            st_ffn=np.ascontiguousarray(state_ffn_conv[:, ss].transpose(0, 3, 1, 2)).reshape(DEPTH, DFF, 16 * 2),
        )
        in_maps.append(m)

    nc, _ = build_program()
    res = run_bass_kernel_spmd(nc, in_maps, core_ids=list(range(NCORES)))
    R = res.results

    y_prompt = np.zeros((4, 2048, D), np.float32)
    y_sample = np.zeros((128, 4, D), np.float32)
    conv_p = np.zeros((DEPTH, 4, 30, DA), np.float32); conv_s = np.zeros((DEPTH, 128, 30, DA), np.float32)
    pool_p = np.zeros((DEPTH, 4, 15, DA), np.float32); pool_s = np.zeros((DEPTH, 128, 15, DA), np.float32)
    ffn_p = np.zeros((DEPTH, 4, 2, DFF), np.float32); ffn_s = np.zeros((DEPTH, 128, 2, DFF), np.float32)
    v_s = np.zeros((DEPTH, 128, 4, DA), np.float32)
    for c in range(NCORES):
        seq, half = c // 2, c % 2
        ss = slice(c * 16, (c + 1) * 16)
        r = R[c]
        y_prompt[seq, half * 1024:(half + 1) * 1024] = r["yp"].T
        y_sample[ss] = r["ys"].T.reshape(16, 4, D)
        conv_s[:, ss] = r["o_conv_s"].reshape(DEPTH, DA, 16, 30).transpose(0, 2, 3, 1)
        pool_s[:, ss] = r["o_pool_s"].reshape(DEPTH, DA, 16, 15).transpose(0, 2, 3, 1)
        ffn_s[:, ss] = r["o_ffn_s"].reshape(DEPTH, DFF, 16, 2).transpose(0, 2, 3, 1)
        v_s[:, ss] = r["o_v_s"].reshape(DEPTH, 16, 4, DA)
        if half == 1:
            conv_p[:, seq] = r["o_conv_p"].transpose(0, 2, 1)
            pool_p[:, seq] = r["o_pool_p"].transpose(0, 2, 1)
            ffn_p[:, seq] = r["o_ffn_p"].transpose(0, 2, 1)
    return (y_prompt, y_sample, conv_p, conv_s, pool_p, pool_s, ffn_p, ffn_s, v_s)
```

```python
import numpy as np
import concourse.bass as bass
import concourse.mybir as mybir
from concourse.bass_utils import run_bass_kernel_spmd
from concourse.ap import AP as RawAP

F32 = mybir.dt.float32
BF16 = mybir.dt.bfloat16
AF = mybir.ActivationFunctionType
ALU = mybir.AluOpType

D = 2048; DA = 1024; DFF = 5504; NIN = 11264; DEPTH = 4
KC = 16; NFC = 43
EPS = 1e-6
NCORES = 8
HALO = 128
NPROMPT = 1152
NS = 64
SUPERS = [(0, 384, True), (384, 384, False), (768, 384, False)]
NPMAX = 384
TTMAX = NPMAX + NS
GATE0 = 2 * DA + 2 * DA + DA

O_GPM = 0; O_GQM = 16; O_GPF = 32; O_GQF = 48; O_BG = 64; O_PS = 112; O_BDW = 128
O_LCG = 136; O_LCB = 144; O_WDW = 152; O_WFC = 400; O_BFC = 529; O_ADB = 572; NPP = 668

ENGS = ("pe", "act", "dve", "pool", "sp")


class Plan:
    def __init__(self):
        self.ops = {e: [] for e in ENGS}
        self.last_write = {}
        self.readers = {}
        self.lane_count = {}
        self.lane_last = {}
        self.inherit = {}
        self.carry = {}

    @staticmethod
    def _merge(dst, tok):
        ch = tok[:2]
        old = dst.get(ch)
        if old is None or old[2] < tok[2]:
            dst[ch] = tok

    def _init_key(self, k):
        if k not in self.readers:
            name = k[0] if isinstance(k, tuple) else k
            self.readers[k] = dict(self.inherit.get(name, {}))

    def fence(self, dying, newnames):
        merged = {}
        for k, t in self.last_write.items():
            name = k[0] if isinstance(k, tuple) else k
            if name in dying:
                self._merge(merged, t)
        for k, rd in self.readers.items():
            name = k[0] if isinstance(k, tuple) else k
            if name in dying:
                for t in rd.values():
                    self._merge(merged, t)
        for t in merged.values():
            self._merge(self.carry, t)
        merged = self.carry
        for n in newnames:
            self.inherit[n] = dict(merged)
        for k in list(self.readers.keys()):
            name = k[0] if isinstance(k, tuple) else k
            if name in newnames:
                for t in merged.values():
                    self._merge(self.readers[k], t)

    def add(self, eng, emit, reads=(), writes=(), lane=None, serialize=True):
        idx = len(self.ops[eng])
        if lane is not None:
            cnt = self.lane_count.get(lane, 0) + 1
            self.lane_count[lane] = cnt
            tok = ("d", lane, cnt)
        else:
            tok = ("c", eng, idx)
        deps = set()
        if lane is not None and serialize and lane in self.lane_last:
            deps.add(self.lane_last[lane])
        for k in reads:
            self._init_key(k)
            t = self.last_write.get(k)
            if t is not None:
                deps.add(t)
        for k in writes:
            self._init_key(k)
            t = self.last_write.get(k)
            if t is not None:
                deps.add(t)
            for t in self.readers[k].values():
                deps.add(t)
        final = []
        for t in deps:
            if t == tok:
                continue
            if t[0] == "c" and lane is None and t[1] == eng:
                if eng == "pe":
                    continue
                is_raw = any(self.last_write.get(k) == t for k in reads)
                if not is_raw:
                    continue
            final.append(t)
            if t[0] == "c":
                self.ops[t[1]][t[2]]["inc"] = True
        self.ops[eng].append(dict(emit=emit, deps=final, inc=False, lane=lane))
        if lane is not None:
            self.lane_last[lane] = tok
        for k in reads:
            self._merge(self.readers[k], tok)
        for k in writes:
            self.last_write[k] = tok
            self.readers[k] = {}
        return tok

    def make_runner(self, sems, lane_sems, final_waits=()):
        counts = {}
        for e in ENGS:
            c = 0
            lst = []
            for op in self.ops[e]:
                if op["inc"] and op["lane"] is None:
                    c += 1
                lst.append(c)
            counts[e] = lst

        def tokval(t):
            if t[0] == "c":
                return sems[t[1]], counts[t[1]][t[2]]
            return lane_sems[t[1]], 16 * t[2]

        def run(e, engine):
            known = {}
            for op in self.ops[e]:
                for t in op["deps"]:
                    s, v = tokval(t)
                    if known.get(id(s), -1) >= v:
                        continue
                    known[id(s)] = v
                    engine.wait_ge(s, v)
                ins = op["emit"](engine)
                if op["lane"] is not None:
                    ins.then_inc(lane_sems[op["lane"]], 16)
                elif op["inc"]:
                    ins.then_inc(sems[e], 1)
            if e == "sp":
                for t in final_waits:
                    s, v = tokval(t)
                    engine.wait_ge(s, v)
        return run


class Arena:
    def __init__(self, nc):
        self.nc = nc
        self.off = (nc.sbuf_base + 63) // 64 * 64
        self.top = nc.sbuf_top
        self.live = []
        self.uid = 0
        self.peak = self.off

    def alloc(self, name, shape, dt):
        nb = int(np.prod(shape[1:])) * (4 if dt == F32 else 2)
        nb = (nb + 63) // 64 * 64
        o = self.off
        self.off += nb
        self.peak = max(self.peak, self.off)
        assert self.off <= self.top, f"SBUF overflow at {name}: {self.off} > {self.top}"
        self.uid += 1
        t = self.nc.alloc_sbuf_tensor_at(f"{name}_{self.uid}", shape, dt, offset=o)
        self.live.append(name)
        return t

    def mark(self):
        return (self.off, len(self.live))

    def reset(self, m):
        dying = set(self.live[m[1]:])
        self.off = m[0]
        del self.live[m[1]:]
        return dying


def build_program():
    nc = bass.Bass("TRN2", target_bir_lowering=False)

    def din(name, shape):
        return nc.dram_tensor(name, list(shape), F32, kind="ExternalInput").ap()

    def dout(name, shape):
        return nc.dram_tensor(name, list(shape), F32, kind="ExternalOutput").ap()

    xp = din("xp", [D, NPROMPT]); xs = din("xs", [D, NS]); cT = din("cT", [D, 17])
    hmask = din("hmask", [128, 1]); pcnt = din("pcnt", [128, 64]); tril = din("tril", [128, 128])
    bdmask = din("bdmask", [64, 64])
    st_conv = din("st_conv", [DEPTH, DA, 16 * 30]); st_pool = din("st_pool", [DEPTH, DA, 16 * 15])
    st_ffn = din("st_ffn", [DEPTH, DFF, 16 * 2])
    pp = din("pp", [DEPTH, 128, NPP])
    lnvg = din("lnvg", [DEPTH, DA]); lnvb = din("lnvb", [DEPTH, DA])
    wsT = din("wsT", [DEPTH, 128, 8 * 128]); bsp = din("bsp", [DEPTH, 8 * 128])
    ws4 = din("ws4", [DEPTH, 64, 32]); bs4 = din("bs4", [DEPTH, 8 * 64])
    ada_w = din("ada_w", [DEPTH, D, 6 * D]); w_in = din("w_in", [DEPTH, D, NIN])
    w_a_out = din("w_a_out", [DEPTH, DA, D]); w_b_out = din("w_b_out", [DEPTH, DA, D])
    w_pool = din("w_pool", [DEPTH, 4, 256, 512]); w_o = din("w_o", [DEPTH, D, D])
    w_up = din("w_up", [DEPTH, D, 2 * DFF]); w_down = din("w_down", [DEPTH, DFF, D])

    yp = dout("yp", [D, 1024]); ys = dout("ys", [D, NS])
    o_conv_p = dout("o_conv_p", [DEPTH, DA, 30]); o_conv_s = dout("o_conv_s", [DEPTH, DA, 16 * 30])
    o_pool_p = dout("o_pool_p", [DEPTH, DA, 15]); o_pool_s = dout("o_pool_s", [DEPTH, DA, 16 * 15])
    o_ffn_p = dout("o_ffn_p", [DEPTH, DFF, 2]); o_ffn_s = dout("o_ffn_s", [DEPTH, DFF, 16 * 2])
    o_v_s = dout("o_v_s", [DEPTH, NS, DA])

    modsc = nc.dram_tensor("modsc", [DEPTH, 128, 96 * 17], F32).ap()
    P = Plan()
    A = Arena(nc)
    out_toks = []

    X = A.alloc("X", [128, KC, TTMAX], F32)
    H = A.alloc("H", [128, KC, TTMAX], BF16)
    MODC = A.alloc("MODC", [128, 96, 17], F32)
    PPt = A.alloc("PP", [128, DEPTH, NPP], F32)
    NSLOT = 3
    SLOT_ELEMS = 8192
    WS = [A.alloc(f"WS{i}", [128, SLOT_ELEMS], BF16) for i in range(NSLOT)]
    RSTD = A.alloc("RSTD", [128, TTMAX], F32)
    NSQ = 3
    SQ = [A.alloc(f"SQ{i}", [128, 512], BF16) for i in range(NSQ)]
    NTMP = 2
    TMPF = [A.alloc(f"TMPF{i}", [128, 512], F32) for i in range(NTMP)]
    NSIG = 2
    SIG = [A.alloc(f"SIG{i}", [128, 512], F32) for i in range(NSIG)]
    HCONV = A.alloc("HCONV", [128, DEPTH, 8, 30], F32)
    HPOOL = A.alloc("HPOOL", [128, DEPTH, 8, 15], F32)
    HFFN = A.alloc("HFFN", [128, DEPTH, NFC, 2], F32)
    ONES = A.alloc("ONES", [128, 128], BF16)
    TRIL = A.alloc("TRIL", [128, 128], F32)
    BDM = A.alloc("BDM", [64, 64], F32)
    HMASK = A.alloc("HMASK", [128, 1], F32)
    PCNT = A.alloc("PCNT", [128, 64], F32)
    MV = A.alloc("MV", [128, 32], F32)
    ada_mark = A.mark()
    CT = A.alloc("CT", [128, KC, 17], F32)
    SC = A.alloc("SC", [128, KC, 17], BF16)

    from contextlib import ExitStack
    es = ExitStack()
    PSB = [es.enter_context(nc.psum_tensor(f"psb{i}", [128, 512], F32)) for i in range(8)]
    sems = {e: es.enter_context(nc.semaphore(f"s_{e}")) for e in ENGS}
    NMISC = 8
    lanes = {}
    for i in range(NSLOT):
        lanes[f"w{i}"] = es.enter_context(nc.semaphore(f"l_w{i}"))
    for i in range(NMISC):
        lanes[f"m{i}"] = es.enter_context(nc.semaphore(f"l_m{i}"))

    st = dict(misc=0, slot=0, bank=0, sq=0, tmp=0, sig=0)

    def misc_lane():
        st["misc"] = (st["misc"] + 1) % NMISC
        return f"m{st['misc']}"

    def dma(out, in_, reads=(), writes=(), eng="sp"):
        return P.add(eng, lambda e: e.dma_start(out=out, in_=in_), reads=reads, writes=writes,
                     lane=misc_lane())

    def next_bank():
        b = st["bank"]
        st["bank"] = (b + 1) % 6
        return b

    def next_sq():
        st["sq"] = (st["sq"] + 1) % NSQ
        return st["sq"]

    def next_tmp():
        st["tmp"] = (st["tmp"] + 1) % NTMP
        return st["tmp"]

    def next_sig():
        st["sig"] = (st["sig"] + 1) % NSIG
        return st["sig"]

    def act(out, in_, func, reads, writes, bias=0.0, scale=1.0):
        return P.add("act", lambda e: e.activation(out=out, in_=in_, func=func, bias=bias, scale=scale),
                     reads=reads, writes=writes)

    def tt(out, in0, in1, op, reads, writes, eng="dve"):
        return P.add(eng, lambda e: e.tensor_tensor(out=out, in0=in0, in1=in1, op=op),
                     reads=reads, writes=writes)

    def ts(out, in0, s1, s2, op0, op1, reads, writes, eng="dve"):
        if s2 is None:
            return P.add(eng, lambda e: e.tensor_scalar(out=out, in0=in0, scalar1=s1, scalar2=None, op0=op0),
                         reads=reads, writes=writes)
        return P.add(eng, lambda e: e.tensor_scalar(out=out, in0=in0, scalar1=s1, scalar2=s2, op0=op0, op1=op1),
                     reads=reads, writes=writes)

    def stt(out, in0, scalar, in1, op0, op1, reads, writes, eng="dve"):
        return P.add(eng, lambda e: e.scalar_tensor_tensor(out=out, in0=in0, scalar=scalar, in1=in1,
                                                            op0=op0, op1=op1), reads=reads, writes=writes)

    def copy(out, in_, reads, writes, eng="dve"):
        return P.add(eng, lambda e: e.tensor_copy(out, in_), reads=reads, writes=writes)

    def memset(ap, val, writes, eng="dve"):
        return P.add(eng, lambda e: e.memset(ap, val), writes=writes)

    def mm(ps_ap, lhsT, rhs, start, stop, reads, bank):
        return P.add("pe", lambda e: e.matmul(ps_ap, lhsT=lhsT, rhs=rhs, start=start, stop=stop),
                     reads=reads, writes=[("ps", bank)])

    def load_unit(pieces):
        s = st["slot"]
        st["slot"] = (s + 1) % NSLOT
        views = []
        off = 0
        for pi, (src, kc) in enumerate(pieces):
            ncols = src.shape[1]
            v = WS[s][:, off:off + kc * ncols].rearrange("p (k c) -> p k c", c=ncols)
            srcv = src.rearrange("(k p) c -> p k c", p=128)
            P.add("pool", lambda e, v=v, srcv=srcv: e.dma_start(out=v, in_=srcv), writes=[("WS", s, pi)],
                  lane=f"w{s}", serialize=False)
            views.append(v)
            off += kc * ncols
        assert off <= SLOT_ELEMS
        return s, views

    def chain(ps_ap, bank, lhs_list, rhs_list, reads, first=True, last=True):
        n = len(lhs_list)
        for k in range(n):
            mm(ps_ap, lhs_list[k], rhs_list[k], start=(first and k == 0), stop=(last and k == n - 1),
               reads=reads, bank=bank)

    dma(PPt[:], pp.rearrange("l p c -> p l c"), writes=["PP"])
    dma(TRIL[:], tril, writes=["TRIL"])
    dma(BDM[:], bdmask, writes=["BDM"])
    dma(HMASK[:], hmask, writes=["HMASK"])
    dma(PCNT[:], pcnt, writes=["PCNT"])
    dma(CT[:], cT.rearrange("(k p) j -> p k j", p=128), writes=["CT"])
    memset(ONES[:], 1.0, ["ONES"])
    memset(HCONV[:], 0.0, ["HCONV"])
    memset(HPOOL[:], 0.0, ["HPOOL"])
    memset(HFFN[:], 0.0, ["HFFN"])
    act(SC[:], CT[:], AF.Silu, ["CT"], ["SC"])

    for l in range(DEPTH):
        for u in range(24):
            s, (wv,) = load_unit([(ada_w[l][:, u * 512:(u + 1) * 512], KC)])
            for m in range(4):
                mi = u * 4 + m
                b = next_bank()
                chain(PSB[b][:, 0:17], b, [wv[:, k, m * 128:(m + 1) * 128] for k in range(KC)],
                      [SC[:, k, :] for k in range(KC)], reads=[("WS", s, 0), ("WS", s, 1), "SC"])
                act(MODC[:, mi, :], PSB[b][:, 0:17], AF.Identity, [("ps", b), "PP"], ["MODC"],
                    bias=PPt[:, l, O_ADB + mi:O_ADB + mi + 1])
        for (c0, goff) in ((16, O_GPM), (64, O_GPF)):
            ts(MODC[:, c0:c0 + 16, :], MODC[:, c0:c0 + 16, :], 1.0, None, ALU.add, None,
               ["MODC"], ["MODC"])
        for (c0, goff) in ((16, O_GPM), (64, O_GPF), (32, O_GQM), (80, O_GQF)):
            tt(MODC[:, c0:c0 + 16, :], MODC[:, c0:c0 + 16, :],
               PPt[:, l, goff:goff + 16].unsqueeze(2).to_broadcast([128, 16, 17]), ALU.mult,
               ["MODC", "PP"], ["MODC"])
        dma(modsc[l], MODC[:].rearrange("p a b -> p (a b)"), reads=["MODC"], writes=[("MODSC", l)])

    dying = A.reset(ada_mark)
    P.fence(dying, [])
    base_mark = A.mark()

    for si, (P0, NP, HAS_S) in enumerate(SUPERS):
        TT = NP + (NS if HAS_S else 0)
        first_super = (si == 0)
        last_super = (si == len(SUPERS) - 1)
        pblocks = []
        o = 0
        while o < NP:
            n = min(512, NP - o)
            pblocks.append((o, n))
            o += n
        blocks = [(o_, n_, False) for (o_, n_) in pblocks] + ([(NP, NS, True)] if HAS_S else [])
        mblocks = [[o_, n_, [(o_, n_, False)]] for (o_, n_) in pblocks]
        if HAS_S:
            if mblocks[-1][1] + NS <= 512:
                mblocks[-1][1] += NS
                mblocks[-1][2].append((NP, NS, True))
            else:
                mblocks.append([NP, NS, [(NP, NS, True)]])
        assert len(mblocks) <= 2
        ntile = NP // 128

        for k in range(KC):
            dma(X[:, k, 0:NP], xp[k * 128:(k + 1) * 128, P0:P0 + NP], writes=[("X", k)])
            if HAS_S:
                dma(X[:, k, NP:NP + NS], xs[k * 128:(k + 1) * 128, :], writes=[("X", k)])

        def s3(ap):
            return ap.rearrange("p (s t) -> p s t", t=4)

        def bc_s(ap16):
            return ap16.unsqueeze(2).to_broadcast([128, 16, 4])

        def rstd_from(bank, o_, n_, scale):
            act(RSTD[:, o_:o_ + n_], PSB[bank][:, 0:n_], AF.Sqrt, [("ps", bank)], ["RSTD"],
                bias=EPS, scale=scale)
            P.add("dve", lambda e: e.reciprocal(out=RSTD[:, o_:o_ + n_], in_=RSTD[:, o_:o_ + n_]),
                  reads=["RSTD"], writes=["RSTD"])

        def prenorm(l, a_off, b_off):
            for bi_, (O_, N_, subs_) in enumerate(mblocks):
                b = 6 + bi_
                for k in range(KC):
                    q = next_sq()
                    act(SQ[q][:, 0:N_], X[:, k, O_:O_ + N_], AF.Square, [("X", k)], [("SQ", q)])
                    mm(PSB[b][:, 0:N_], ONES[:], SQ[q][:, 0:N_], start=(k == 0), stop=(k == KC - 1),
                       reads=["ONES", ("SQ", q)], bank=b)
                rstd_from(b, O_, N_, 1.0 / D)
            for (o_, n_, is_s) in blocks:
                for k in range(KC):
                    t_ = next_tmp()
                    tt(TMPF[t_][:, 0:n_], X[:, k, o_:o_ + n_], RSTD[:, o_:o_ + n_], ALU.mult,
                       [("X", k), "RSTD"], [("TMPF", t_)])
                    if not is_s:
                        act(H[:, k, o_:o_ + n_], TMPF[t_][:, 0:n_], AF.Identity,
                            [("TMPF", t_), "MODC"], [("H", k)],
                            bias=MODC[:, b_off + k, 0:1], scale=MODC[:, a_off + k, 0:1])
                    else:
                        tt(s3(TMPF[t_][:, 0:n_]), s3(TMPF[t_][:, 0:n_]), bc_s(MODC[:, a_off + k, 1:17]),
                           ALU.mult, [("TMPF", t_), "MODC"], [("TMPF", t_)])
                        tt(s3(H[:, k, o_:o_ + n_]), s3(TMPF[t_][:, 0:n_]), bc_s(MODC[:, b_off + k, 1:17]),
                           ALU.add, [("TMPF", t_), "MODC"], [("H", k)])

        def postnorm(l, g_off, YMO):
            for bi, (o_, n_, is_s) in enumerate(blocks):
                for mi in range(KC):
                    t_ = next_tmp()
                    tt(TMPF[t_][:, 0:n_], YMO[:, mi, o_:o_ + n_], RSTD[:, o_:o_ + n_], ALU.mult,
                       [("YMO", mi), "RSTD"], [("TMPF", t_)])
                    if not is_s:
                        stt(X[:, mi, o_:o_ + n_], TMPF[t_][:, 0:n_], MODC[:, g_off + mi, 0:1],
                            X[:, mi, o_:o_ + n_], ALU.mult, ALU.add,
                            [("TMPF", t_), "MODC", ("X", mi)], [("X", mi)])
                    else:
                        tt(s3(TMPF[t_][:, 0:n_]), s3(TMPF[t_][:, 0:n_]), bc_s(MODC[:, g_off + mi, 1:17]),
                           ALU.mult, [("TMPF", t_), "MODC"], [("TMPF", t_)])
                        tt(X[:, mi, o_:o_ + n_], X[:, mi, o_:o_ + n_], TMPF[t_][:, 0:n_], ALU.add,
                           [("TMPF", t_), ("X", mi)], [("X", mi)])

        def out_stage(l, YMO, units, rhs_fn, nk_total_fn):
            pass

        for l in range(DEPTH):
            m0 = A.mark()
            dma(MODC[:].rearrange("p a b -> p (a b)"), modsc[l], reads=[("MODSC", l)], writes=["MODC"])
            MIXACC = A.alloc("MIXACC", [128, KC, TT], BF16)
            CACC = A.alloc("CACC", [128, 8, TT], F32)
            XBUF = [A.alloc(f"XBUF{i}", [128, 30 + NP], F32) for i in range(2)]
            XSC = [A.alloc(f"XSC{i}", [128, 16, 34], F32) for i in range(2)]
            P.fence(set(), ["MIXACC", "CACC", "XBUF0", "XBUF1", "XSC0", "XSC1"])
            prenorm(l, 16, 0)
            mA = A.mark()

            def gated_out(l, src_buf, src_key, kc_src, w_src_fn, gate_col0, first, scale_off=None,
                          rhs_chunk_fn=None, pre_unit=None):
                for u in range(8):
                    if pre_unit is not None:
                        pre_unit(u)
                    c0 = u * 256
                    wsrc, kcs = w_src_fn(c0)
                    s, (wy, wg) = load_unit([(wsrc, kcs),
                                              (w_in[l][:, gate_col0 + c0:gate_col0 + c0 + 256], KC)])
                    for m in range(2):
                        mi = u * 2 + m
                        for (o_, n_, _sb) in mblocks:
                            b1 = next_bank()
                            chain(PSB[b1][:, 0:n_], b1, [wy[:, k, m * 128:(m + 1) * 128] for k in range(kcs)],
                                  [rhs_chunk_fn(mi, k, o_, n_) for k in range(kcs)],
                                  reads=[("WS", s, 0), ("WS", s, 1)] + [(src_key, kk) for kk in range(8)])
                            b2 = next_bank()
                            chain(PSB[b2][:, 0:n_], b2, [wg[:, k, m * 128:(m + 1) * 128] for k in range(KC)],
                                  [H[:, k, o_:o_ + n_] for k in range(KC)],
                                  reads=[("WS", s, 0), ("WS", s, 1)] + [("H", kk) for kk in range(KC)])
                            g = next_sig()
                            act(SIG[g][:, 0:n_], PSB[b2][:, 0:n_], AF.Sigmoid, [("ps", b2), "PP"], [("SIG", g)],
                                bias=PPt[:, l, O_BG + (gate_col0 - GATE0) // 128 + mi:
                                         O_BG + (gate_col0 - GATE0) // 128 + mi + 1])
                            if first:
                                tt(MIXACC[:, mi, o_:o_ + n_], PSB[b1][:, 0:n_], SIG[g][:, 0:n_], ALU.mult,
                                   [("ps", b1), ("SIG", g)], [("MIXACC", mi)])
                            else:
                                t_ = next_tmp()
                                if scale_off is None:
                                    tt(TMPF[t_][:, 0:n_], PSB[b1][:, 0:n_], SIG[g][:, 0:n_], ALU.mult,
                                       [("ps", b1), ("SIG", g)], [("TMPF", t_)])
                                else:
                                    stt(TMPF[t_][:, 0:n_], PSB[b1][:, 0:n_],
                                        PPt[:, l, scale_off + mi:scale_off + mi + 1], SIG[g][:, 0:n_],
                                        ALU.mult, ALU.mult, [("ps", b1), ("SIG", g), "PP"], [("TMPF", t_)])
                                tt(MIXACC[:, mi, o_:o_ + n_], MIXACC[:, mi, o_:o_ + n_], TMPF[t_][:, 0:n_],
                                   ALU.add, [("MIXACC", mi), ("TMPF", t_)], [("MIXACC", mi)])

            NTT = ntile + (1 if HAS_S else 0)
            VN = A.alloc("VN", [128, NTT, DA], BF16)
            U = A.alloc("U", [128, 8, TT], BF16)
            mA1 = A.mark()
            VF = A.alloc("VF", [128, NTT, DA], F32)
            LNG = A.alloc("LNG", [128, DA], F32)
            LNB = A.alloc("LNB", [128, DA], F32)
            P.fence(set(), ["VN", "U", "VF", "LNG", "LNB"])
            dma(LNG[:], lnvg[l:l + 1, :].partition_broadcast(128), writes=["LNG"])
            dma(LNB[:], lnvb[l:l + 1, :].partition_broadcast(128), writes=["LNB"])
            tiles = [(t_ * 128, 128, t_) for t_ in range(ntile)] + ([(NP, NS, ntile)] if HAS_S else [])
            for uu in range(2):
                s, (wv,) = load_unit([(w_in[l][:, DA + uu * 512:DA + (uu + 1) * 512], KC)])
                for (o_, n_, ti) in tiles:
                    b = next_bank()
                    chain(PSB[b][0:n_, 0:512], b, [H[:, k, o_:o_ + n_] for k in range(KC)],
                          [wv[:, k, :] for k in range(KC)], reads=[("WS", s, 0), ("WS", s, 1)] + [("H", kk) for kk in range(KC)])
                    act(VF[0:n_, ti, uu * 512:(uu + 1) * 512], PSB[b][0:n_, 0:512], AF.Gelu_apprx_tanh,
                        [("ps", b)], [("VF", ti)])
            for (o_, n_, ti) in tiles:
                for hh in range(2):
                    P.add("dve", lambda e, n_=n_, ti=ti, hh=hh, VF=VF: e.bn_stats(
                        out=MV[0:n_, hh * 6:(hh + 1) * 6], in_=VF[0:n_, ti, hh * 512:(hh + 1) * 512]),
                        reads=[("VF", ti)], writes=["MV"])
                P.add("dve", lambda e, n_=n_: e.bn_aggr(out=MV[0:n_, 16:18], in_=MV[0:n_, 0:12]),
                      reads=["MV"], writes=["MV"])
                act(MV[0:n_, 18:19], MV[0:n_, 17:18], AF.Sqrt, ["MV"], ["MV"], bias=EPS, scale=1.0)
                P.add("dve", lambda e, n_=n_: e.reciprocal(out=MV[0:n_, 18:19], in_=MV[0:n_, 18:19]),
                      reads=["MV"], writes=["MV"])
                ts(VF[0:n_, ti, :], VF[0:n_, ti, :], MV[0:n_, 16:17], MV[0:n_, 18:19], ALU.subtract, ALU.mult,
                   [("VF", ti), "MV"], [("VF", ti)])
                tt(VF[0:n_, ti, :], VF[0:n_, ti, :], LNG[0:n_, :], ALU.mult, [("VF", ti), "LNG"], [("VF", ti)])
                if ti < ntile:
                    tt(VN[0:n_, ti, :], VF[0:n_, ti, :], LNB[0:n_, :], ALU.add, [("VF", ti), "LNB"], [("VN", ti)])
                else:
                    tt(VF[0:n_, ti, :], VF[0:n_, ti, :], LNB[0:n_, :], ALU.add, [("VF", ti), "LNB"], [("VF", ti)])
                    act(VN[0:n_, ti, :], VF[0:n_, ti, :], AF.Identity, [("VF", ti)], [("VN", ti)])
                    out_toks.append(dma(o_v_s[l], VF[0:NS, ti, :], reads=[("VF", ti)]))
            for uu in range(2):
                s, (wv,) = load_unit([(w_in[l][:, uu * 512:(uu + 1) * 512], KC)])
                for m in range(4):
                    mi = uu * 4 + m
                    for (o_, n_, _sb) in mblocks:
                        b = next_bank()
                        chain(PSB[b][:, 0:n_], b, [wv[:, k, m * 128:(m + 1) * 128] for k in range(KC)],
                              [H[:, k, o_:o_ + n_] for k in range(KC)],
                              reads=[("WS", s, 0), ("WS", s, 1)] + [("H", kk) for kk in range(KC)])
                        act(U[:, mi, o_:o_ + n_], PSB[b][:, 0:n_], AF.Gelu_apprx_tanh, [("ps", b)], [("U", mi)])
            dying = A.reset(mA1)
            WST = A.alloc("WST", [128, 8, 128], F32)
            WSB = A.alloc("WSB", [128, 8, 128], BF16)
            BSF = A.alloc("BSF", [1, 1024], F32)
            BSR = A.alloc("BSR", [1, 1024], BF16)
            WS4 = A.alloc("WS4", [64, 32], F32)
            BDB = A.alloc("BDB", [64, 8, 64], BF16)
            BS4F = A.alloc("BS4F", [1, 512], F32)
            BS4R = A.alloc("BS4R", [1, 512], BF16)
            PW = A.alloc("PW", [128, 16 * 4 * 31], F32) if HAS_S else None
            P.fence(dying, ["WST", "WSB", "BSF", "BSR", "WS4", "BDB", "BS4F", "BS4R", "PW"])
            dma(WST[:], wsT[l].rearrange("p (g t) -> p g t", t=128), writes=["WST"])
            tt(WSB[:], WST[:], TRIL[:].unsqueeze(1).to_broadcast([128, 8, 128]), ALU.mult,
               ["WST", "TRIL"], ["WSB"])
            dma(BSF[:], bsp[l:l + 1, :], writes=["BSF"])
            copy(BSR[:], BSF[:], ["BSF"], ["BSR"])
            if HAS_S:
                dma(WS4[:], ws4[l], writes=["WS4"])
                tt(BDB[:].rearrange("p g (s t) -> p g s t", t=4),
                   WS4[:].rearrange("p (g t) -> p g t", t=4).unsqueeze(2).to_broadcast([64, 8, 16, 4]),
                   BDM[:].rearrange("p (s t) -> p s t", t=4).unsqueeze(1).to_broadcast([64, 8, 16, 4]),
                   ALU.mult, ["WS4", "BDM"], ["BDB"])
                dma(BS4F[:], bs4[l:l + 1, :], writes=["BS4F"])
                copy(BS4R[:], BS4F[:], ["BS4F"], ["BS4R"])
            for (o_, n_, ti) in tiles:
                for g in range(8):
                    b = next_bank()
                    if ti < ntile:
                        mm(PSB[b][:, 0:128], VN[:, ti, g * 128:(g + 1) * 128], WSB[:, g, :], True, False,
                           [("VN", ti), "WSB"], b)
                        mm(PSB[b][:, 0:128], ONES[0:1, :], BSR[0:1, g * 128:(g + 1) * 128], False, True,
                           ["ONES", "BSR"], b)
                    else:
                        mm(PSB[b][:, 0:NS], VN[0:NS, ti, g * 128:(g + 1) * 128], BDB[:, g, :], True, False,
                           [("VN", ti), "BDB"], b)
                        mm(PSB[b][:, 0:NS], ONES[0:1, :], BS4R[0:1, g * NS:(g + 1) * NS], False, True,
                           ["ONES", "BS4R"], b)
                    tt(U[:, g, o_:o_ + n_], PSB[b][:, 0:n_], U[:, g, o_:o_ + n_], ALU.mult,
                       [("ps", b), ("U", g)], [("U", g)])

            def glu_unit(uo):
                c0 = uo * 256
                s, (wa, wb) = load_unit([(w_in[l][:, 2 * DA + c0:2 * DA + c0 + 256], KC),
                                         (w_in[l][:, 3 * DA + c0:3 * DA + c0 + 256], KC)])
                for m in range(2):
                    ci = uo * 2 + m
                    xb_ = XBUF[ci % 2]
                    xk = f"XBUF{ci % 2}"
                    xs_ = XSC[ci % 2]
                    sk = f"XSC{ci % 2}"
                    copy(xb_[:, 0:30], HCONV[:, l, ci, :], ["HCONV"], [xk])
                    if HAS_S:
                        dma(xs_[:, :, 0:30], st_conv[l][ci * 128:(ci + 1) * 128, :].rearrange("p (s r) -> p s r", r=30),
                            writes=[sk])
                    for (O_, N_, subs_) in mblocks:
                        b1 = next_bank()
                        chain(PSB[b1][:, 0:N_], b1, [wa[:, k, m * 128:(m + 1) * 128] for k in range(KC)],
                              [H[:, k, O_:O_ + N_] for k in range(KC)],
                              reads=[("WS", s, 0), ("WS", s, 1)] + [("H", kk) for kk in range(KC)])
                        b2 = next_bank()
                        chain(PSB[b2][:, 0:N_], b2, [wb[:, k, m * 128:(m + 1) * 128] for k in range(KC)],
                              [H[:, k, O_:O_ + N_] for k in range(KC)],
                              reads=[("WS", s, 0), ("WS", s, 1)] + [("H", kk) for kk in range(KC)])
                        g = next_sig()
                        act(SIG[g][:, 0:N_], PSB[b2][:, 0:N_], AF.Sigmoid, [("ps", b2)], [("SIG", g)])
                        for (o_, n_, is_s) in subs_:
                            r0 = o_ - O_
                            if not is_s:
                                tt(xb_[:, 30 + o_:30 + o_ + n_], PSB[b1][:, r0:r0 + n_], SIG[g][:, r0:r0 + n_], ALU.mult,
                                   [("ps", b1), ("SIG", g)], [xk])
                            else:
                                tt(xs_[:, :, 30:34], s3(PSB[b1][:, r0:r0 + n_]), s3(SIG[g][:, r0:r0 + n_]), ALU.mult,
                                   [("ps", b1), ("SIG", g)], [sk])
                    if first_super:
                        ts(xb_[:, 30:30 + HALO], xb_[:, 30:30 + HALO], HMASK[:, 0:1], None, ALU.mult, None,
                           [xk, "HMASK"], [xk])
                    wcol = lambda tap: PPt[:, l, O_WDW + tap * 8 + ci:O_WDW + tap * 8 + ci + 1]
                    ts(CACC[:, ci, 0:NP], xb_[:, 0:NP], wcol(0), PPt[:, l, O_BDW + ci:O_BDW + ci + 1],
                       ALU.mult, ALU.add, [xk, "PP"], [("CACC", ci)])
                    for tap in range(1, 31):
                        stt(CACC[:, ci, 0:NP], xb_[:, tap:tap + NP], wcol(tap), CACC[:, ci, 0:NP], ALU.mult, ALU.add,
                            [xk, "PP", ("CACC", ci)], [("CACC", ci)])
                    copy(HCONV[:, l, ci, :], xb_[:, NP:NP + 30], [xk], ["HCONV"])
                    if last_super:
                        out_toks.append(dma(o_conv_p[l][ci * 128:(ci + 1) * 128, :], xb_[:, NP:NP + 30], reads=[xk]))
                    if HAS_S:
                        cs_ = s3(CACC[:, ci, NP:NP + NS])
                        w0_ = xs_[:, :, 0:4]
                        win_ = RawAP(w0_.tensor, w0_.offset, [[w0_.ap[0][0], 128], [34, 16], [1, 4], [1, 31]])
                        c0_ = wcol(0)
                        wbc_ = RawAP(c0_.tensor, c0_.offset, [[c0_.ap[0][0], 128], [0, 16], [0, 4], [8, 31]])
                        pw4_ = PW[:, :].rearrange("p (s t k) -> p s t k", t=4, k=31)
                        tt(pw4_, win_, wbc_, ALU.mult, [sk, "PP"], ["PW"])
                        P.add("dve", lambda e, cs_=cs_, pw4_=pw4_: e.tensor_reduce(
                            out=cs_, in_=pw4_, axis=mybir.AxisListType.X, op=ALU.add),
                            reads=["PW"], writes=[("CACC", ci)])
                        ts(cs_, cs_, PPt[:, l, O_BDW + ci:O_BDW + ci + 1], None, ALU.add, None,
                           [("CACC", ci), "PP"], [("CACC", ci)])
                        out_toks.append(dma(o_conv_s[l][ci * 128:(ci + 1) * 128, :].rearrange("p (s r) -> p s r", r=30),
                                            xs_[:, :, 4:34], reads=[sk]))
            gated_out(l, U, "U", 8, lambda c0: (w_a_out[l][:, c0:c0 + 256], 8), GATE0, True,
                      rhs_chunk_fn=lambda mi, k, o_, n_: U[:, k, o_:o_ + n_],
                      pre_unit=lambda u: glu_unit(u // 2) if u % 2 == 0 else None)
            dying = A.reset(mA)
            YBIN = A.alloc("YBIN", [128, 8, TT], BF16)
            MEAN = A.alloc("MEAN", [128, 512], F32)
            VAR = A.alloc("VAR", [128, 512], F32)
            P.fence(dying, ["YBIN", "MEAN", "VAR"])
            for (o_, n_, _sb) in mblocks:
                for ci in range(8):
                    q = next_sq()
                    act(SQ[q][:, 0:n_], CACC[:, ci, o_:o_ + n_], AF.Square, [("CACC", ci)], [("SQ", q)])
                    mm(PSB[7][:, 0:n_], ONES[:], SQ[q][:, 0:n_], ci == 0, ci == 7, ["ONES", ("SQ", q)], 7)
                    q2 = next_sq()
                    copy(SQ[q2][:, 0:n_], CACC[:, ci, o_:o_ + n_], [("CACC", ci)], [("SQ", q2)])
                    mm(PSB[6][:, 0:n_], ONES[:], SQ[q2][:, 0:n_], ci == 0, ci == 7, ["ONES", ("SQ", q2)], 6)
                ts(MEAN[:, 0:n_], PSB[6][:, 0:n_], 1.0 / DA, None, ALU.mult, None, [("ps", 6)], ["MEAN"])
                tt(VAR[:, 0:n_], MEAN[:, 0:n_], MEAN[:, 0:n_], ALU.mult, ["MEAN"], ["VAR"])
                stt(VAR[:, 0:n_], PSB[7][:, 0:n_], 1.0 / DA, VAR[:, 0:n_], ALU.mult, ALU.subtract,
                    [("ps", 7), "VAR"], ["VAR"])
                ts(VAR[:, 0:n_], VAR[:, 0:n_], 0.0, None, ALU.max, None, ["VAR"], ["VAR"])
                act(VAR[:, 0:n_], VAR[:, 0:n_], AF.Sqrt, ["VAR"], ["VAR"], bias=EPS, scale=1.0)
                P.add("dve", lambda e, n_=n_, VAR=VAR: e.reciprocal(out=VAR[:, 0:n_], in_=VAR[:, 0:n_]),
                      reads=["VAR"], writes=["VAR"])
                for ci in range(8):
                    t_ = next_tmp()
                    tt(TMPF[t_][:, 0:n_], CACC[:, ci, o_:o_ + n_], MEAN[:, 0:n_], ALU.subtract,
                       [("CACC", ci), "MEAN"], [("TMPF", t_)])
                    tt(TMPF[t_][:, 0:n_], TMPF[t_][:, 0:n_], VAR[:, 0:n_], ALU.mult, [("TMPF", t_), "VAR"],
                       [("TMPF", t_)])
                    act(YBIN[:, ci, o_:o_ + n_], TMPF[t_][:, 0:n_], AF.Silu, [("TMPF", t_), "PP"], [("YBIN", ci)],
                        bias=PPt[:, l, O_LCB + ci:O_LCB + ci + 1], scale=PPt[:, l, O_LCG + ci:O_LCG + ci + 1])
            gated_out(l, YBIN, "YBIN", 8, lambda c0: (w_b_out[l][:, c0:c0 + 256], 8), GATE0 + D, False,
                      rhs_chunk_fn=lambda mi, k, o_, n_: YBIN[:, k, o_:o_ + n_])

            dying = A.reset(mA)
            LP = 15 + NP
            PBUF = A.alloc("PBUF", [128, 8, LP], F32)
            POOLED = A.alloc("POOLED", [128, 8, TT], BF16)
            PT = [A.alloc(f"PT{i}", [128, 2, LP], F32) for i in range(2)]
            PSC = A.alloc("PSC", [128, 8, 16, 19], F32)
            PTS = [A.alloc(f"PTS{i}", [128, 2, 16, 19], F32) for i in range(2)]
            T16 = A.alloc("T16", [128, 2, 16], F32)
            P.fence(dying, ["PBUF", "POOLED", "PT0", "PT1", "PSC", "PTS0", "PTS1", "T16"])
            for ci in range(8):
                copy(PBUF[:, ci, 0:15], HPOOL[:, l, ci, :], ["HPOOL"], [("PBUF", ci)])
            if HAS_S:
                for ci in range(8):
                    dma(PSC[:, ci, :, 0:15], st_pool[l][ci * 128:(ci + 1) * 128, :].rearrange("p (s r) -> p s r", r=15),
                        writes=[("PSC", ci)])
            for uu in range(2):
                s, (wv,) = load_unit([(w_in[l][:, 4 * DA + uu * 512:4 * DA + (uu + 1) * 512], KC)])
                for m in range(4):
                    ci = uu * 4 + m
                    for (O_, N_, subs_) in mblocks:
                        b = next_bank()
                        chain(PSB[b][:, 0:N_], b, [wv[:, k, m * 128:(m + 1) * 128] for k in range(KC)],
                              [H[:, k, O_:O_ + N_] for k in range(KC)],
                              reads=[("WS", s, 0), ("WS", s, 1)] + [("H", kk) for kk in range(KC)])
                        for (o_, n_, is_s) in subs_:
                            r0 = o_ - O_
                            if not is_s:
                                act(PBUF[:, ci, 15 + o_:15 + o_ + n_], PSB[b][:, r0:r0 + n_], AF.Identity, [("ps", b)],
                                    [("PBUF", ci)])
                            else:
                                act(PSC[:, ci, :, 15:19], s3(PSB[b][:, r0:r0 + n_]), AF.Identity, [("ps", b)],
                                    [("PSC", ci)])
                    if first_super:
                        ts(PBUF[:, ci, 15:15 + HALO], PBUF[:, ci, 15:15 + HALO], HMASK[:, 0:1], None, ALU.mult, None,
                           [("PBUF", ci), "HMASK"], [("PBUF", ci)])
                    copy(HPOOL[:, l, ci, :], PBUF[:, ci, NP:NP + 15], [("PBUF", ci)], ["HPOOL"])
                    if last_super:
                        out_toks.append(dma(o_pool_p[l][ci * 128:(ci + 1) * 128, :], PBUF[:, ci, NP:NP + 15],
                                            reads=[("PBUF", ci)]))
                    if HAS_S:
                        out_toks.append(dma(o_pool_s[l][ci * 128:(ci + 1) * 128, :].rearrange("p (s r) -> p s r", r=15),
                                            PSC[:, ci, :, 4:19], reads=[("PSC", ci)]))
            for gi in range(4):
                w = 2 << gi
                c2 = slice(2 * gi, 2 * gi + 2)
                rk = [("PBUF", 2 * gi), ("PBUF", 2 * gi + 1)]
                src = PBUF[:, c2, :]
                srck = rk
                sh = 1
                lo = 0
                for stp in range(gi + 1):
                    dst = PT[stp % 2]
                    dk = [f"PT{stp % 2}"]
                    nlo = lo + sh
                    tt(dst[:, :, nlo:LP], src[:, :, nlo:LP], src[:, :, lo:LP - sh], ALU.add, srck, dk)
                    src = dst; srck = dk; lo = nlo; sh *= 2
                stt(POOLED[:, c2, 0:NP], src[:, :, 15:15 + NP], 1.0 / w, PBUF[:, c2, 15:15 + NP], ALU.mult,
                    ALU.subtract, srck + rk, [("POOLED", 2 * gi), ("POOLED", 2 * gi + 1)])
                if first_super:
                    a0 = 15 + HALO
                    tt(T16[:], src[:, :, a0:a0 + 16],
                       PCNT[:, gi * 16:(gi + 1) * 16].unsqueeze(1).to_broadcast([128, 2, 16]), ALU.mult,
                       srck + ["PCNT"], ["T16"])
                    tt(POOLED[:, c2, HALO:HALO + 16], T16[:], PBUF[:, c2, a0:a0 + 16], ALU.subtract,
                       ["T16"] + rk, [("POOLED", 2 * gi), ("POOLED", 2 * gi + 1)])
                if HAS_S:
                    rks = [("PSC", 2 * gi), ("PSC", 2 * gi + 1)]
                    src = PSC[:, c2, :, :]
                    srck = rks
                    sh = 1
                    lo = 0
                    for stp in range(gi + 1):
                        dst = PTS[stp % 2]
                        dk = [f"PTS{stp % 2}"]
                        nlo = lo + sh
                        tt(dst[:, :, :, nlo:19], src[:, :, :, nlo:19], src[:, :, :, lo:19 - sh], ALU.add, srck, dk)
                        src = dst; srck = dk; lo = nlo; sh *= 2
                    stt(POOLED[:, c2, NP:NP + NS].rearrange("p c (s t) -> p c s t", t=4), src[:, :, :, 15:19],
                        1.0 / w, PSC[:, c2, :, 15:19], ALU.mult, ALU.subtract, srck + rks,
                        [("POOLED", 2 * gi), ("POOLED", 2 * gi + 1)])
            gated_out(l, POOLED, "POOLED", 2,
                      lambda c0: (w_pool[l][c0 // 512][:, (c0 % 512):(c0 % 512) + 256], 2), GATE0 + 2 * D, False,
                      scale_off=O_PS,
                      rhs_chunk_fn=lambda mi, k, o_, n_: POOLED[:, 2 * (mi // 4) + k, o_:o_ + n_])

            dying = A.reset(mA)
            YMO = A.alloc("YMO", [128, KC, TT], F32)
            P.fence(dying, ["YMO"])

            def proj_out(l, units_fn, nunits, rhs_fn, rhs_keys, kparts):
                for u in range(nunits):
                    banks = {}
                    for kp in range(kparts):
                        src, kc_, koff = units_fn(u, kp)
                        s, (wv,) = load_unit([(src, kc_)])
                        for m in range(2):
                            for bi, (o_, n_, _sb) in enumerate(mblocks):
                                if kp == 0:
                                    banks[(m, bi)] = next_bank()
                                b = banks[(m, bi)]
                                chain(PSB[b][:, 0:n_], b, [wv[:, k, m * 128:(m + 1) * 128] for k in range(kc_)],
                                      [rhs_fn(koff + k, o_, n_) for k in range(kc_)],
                                      reads=[("WS", s, 0), ("WS", s, 1)] + rhs_keys, first=(kp == 0), last=(kp == kparts - 1))
                    for m in range(2):
                        mi = u * 2 + m
                        for bi, (o_, n_, _sb) in enumerate(mblocks):
                            b = banks[(m, bi)]
                            act(YMO[:, mi, o_:o_ + n_], PSB[b][:, 0:n_], AF.Identity, [("ps", b)], [("YMO", mi)])
                            q = next_sq()
                            act(SQ[q][:, 0:n_], PSB[b][:, 0:n_], AF.Square, [("ps", b)], [("SQ", q)])
                            sb_ = 6 + bi
                            mm(PSB[sb_][:, 0:n_], ONES[:],
                               SQ[q][:, 0:n_], mi == 0, mi == KC - 1, ["ONES", ("SQ", q)], sb_)
                for bi, (o_, n_, _sb) in enumerate(mblocks):
                    sb_ = 6 + bi
                    c0_ = 0
                    act(RSTD[:, o_:o_ + n_], PSB[sb_][:, c0_:c0_ + n_], AF.Sqrt, [("ps", sb_)], ["RSTD"],
                        bias=EPS, scale=1.0 / D)
                    P.add("dve", lambda e, o_=o_, n_=n_: e.reciprocal(out=RSTD[:, o_:o_ + n_], in_=RSTD[:, o_:o_ + n_]),
                          reads=["RSTD"], writes=["RSTD"])

            proj_out(l, lambda u, kp: (w_o[l][:, u * 256:(u + 1) * 256], KC, 0), 8,
                     lambda k, o_, n_: MIXACC[:, k, o_:o_ + n_], [("MIXACC", kk) for kk in range(KC)], 1)
            postnorm(l, 32, YMO)

            dying = A.reset(m0)
            ACTB = A.alloc("ACTB", [128, NFC, TT], BF16)
            YMO = A.alloc("YMO", [128, KC, TT], F32)
            GSB = [A.alloc(f"GSB{i}", [128, 2 + NP], F32) for i in range(2)]
            GSS = [A.alloc(f"GSS{i}", [128, 16, 6], F32) for i in range(2)]
            FACC = [A.alloc(f"FACC{i}", [128, TT], F32) for i in range(2)]
            SFF = A.alloc("SFF", [128, NFC, 16, 2], F32)
            OFP = A.alloc("OFP", [128, NFC, 2], F32)
            P.fence(dying, ["ACTB", "YMO", "GSB0", "GSB1", "GSS0", "GSS1", "FACC0", "FACC1", "SFF", "OFP"])
            prenorm(l, 64, 48)
            if HAS_S:
                for j in range(NFC):
                    dma(SFF[:, j, :, :], st_ffn[l][j * 128:(j + 1) * 128, :].rearrange("p (s r) -> p s r", r=2),
                        writes=[("SFF", j)])
            for u in range(22):
                f0 = u * 256
                nf = min(256, DFF - f0)
                s, (wg, wv) = load_unit([(w_up[l][:, f0:f0 + nf], KC), (w_up[l][:, DFF + f0:DFF + f0 + nf], KC)])
                for m in range(nf // 128):
                    j = u * 2 + m
                    gb_ = GSB[j % 2]; gk = f"GSB{j % 2}"
                    gs_ = GSS[j % 2]; gsk = f"GSS{j % 2}"
                    fa_ = FACC[j % 2]; fk = f"FACC{j % 2}"
                    copy(gb_[:, 0:2], HFFN[:, l, j, :], ["HFFN"], [gk])
                    if HAS_S:
                        copy(gs_[:, :, 0:2], SFF[:, j, :, :], [("SFF", j)], [gsk])
                    for (O_, N_, subs_) in mblocks:
                        b1 = next_bank()
                        chain(PSB[b1][:, 0:N_], b1, [wg[:, k, m * 128:(m + 1) * 128] for k in range(KC)],
                              [H[:, k, O_:O_ + N_] for k in range(KC)],
                              reads=[("WS", s, 0), ("WS", s, 1)] + [("H", kk) for kk in range(KC)])
                        for (o_, n_, is_s) in subs_:
                            r0 = o_ - O_
                            if not is_s:
                                act(gb_[:, 2 + o_:2 + o_ + n_], PSB[b1][:, r0:r0 + n_], AF.Identity, [("ps", b1)], [gk])
                            else:
                                act(gs_[:, :, 2:6], s3(PSB[b1][:, r0:r0 + n_]), AF.Identity, [("ps", b1)], [gsk])
                    if first_super:
                        ts(gb_[:, 2:2 + HALO], gb_[:, 2:2 + HALO], HMASK[:, 0:1], None, ALU.mult, None,
                           [gk, "HMASK"], [gk])
                    wc = lambda tap: PPt[:, l, O_WFC + tap * NFC + j:O_WFC + tap * NFC + j + 1]
                    bcol = PPt[:, l, O_BFC + j:O_BFC + j + 1]
                    ts(fa_[:, 0:NP], gb_[:, 0:NP], wc(0), bcol, ALU.mult, ALU.add, [gk, "PP"], [fk])
                    for tap in (1, 2):
                        stt(fa_[:, 0:NP], gb_[:, tap:tap + NP], wc(tap), fa_[:, 0:NP], ALU.mult, ALU.add,
                            [gk, "PP", fk], [fk])
                    copy(HFFN[:, l, j, :], gb_[:, NP:NP + 2], [gk], ["HFFN"])
                    if last_super:
                        copy(OFP[:, j, :], gb_[:, NP:NP + 2], [gk], ["OFP"])
                    if HAS_S:
                        fs_ = s3(fa_[:, NP:NP + NS])
                        ts(fs_, gs_[:, :, 0:4], wc(0), bcol, ALU.mult, ALU.add, [gsk, "PP"], [fk])
                        for tap in (1, 2):
                            stt(fs_, gs_[:, :, tap:tap + 4], wc(tap), fs_, ALU.mult, ALU.add, [gsk, "PP", fk], [fk])
                        copy(SFF[:, j, :, :], gs_[:, :, 4:6], [gsk], [("SFF", j)])
                    act(fa_[:, 0:TT], fa_[:, 0:TT], AF.Gelu_apprx_tanh, [fk], [fk])
                    for (o_, n_, _sb) in mblocks:
                        b2 = next_bank()
                        chain(PSB[b2][:, 0:n_], b2, [wv[:, k, m * 128:(m + 1) * 128] for k in range(KC)],
                              [H[:, k, o_:o_ + n_] for k in range(KC)],
                              reads=[("WS", s, 0), ("WS", s, 1)] + [("H", kk) for kk in range(KC)])
                        tt(ACTB[:, j, o_:o_ + n_], PSB[b2][:, 0:n_], fa_[:, o_:o_ + n_], ALU.mult,
                           [("ps", b2), fk], [("ACTB", j)])
            if last_super:
                out_toks.append(dma(o_ffn_p[l].rearrange("(j p) r -> p j r", p=128), OFP[:], reads=["OFP"]))
            if HAS_S:
                out_toks.append(dma(o_ffn_s[l].rearrange("(j p) (s r) -> p j s r", p=128, r=2), SFF[:],
                                    reads=[("SFF", j) for j in range(NFC)]))
            KH = [(0, 22), (22, 21)]
            proj_out(l, lambda u, kp: (w_down[l][KH[kp][0] * 128:(KH[kp][0] + KH[kp][1]) * 128, u * 256:(u + 1) * 256],
                                       KH[kp][1], KH[kp][0]), 8,
                     lambda k, o_, n_: ACTB[:, k, o_:o_ + n_], [("ACTB", jj) for jj in range(NFC)], 2)
            postnorm(l, 80, YMO)
            dying = A.reset(m0)
            P.fence(dying, ["MIXACC", "VN", "U", "VF", "LNG", "LNB"])

        for k in range(KC):
            c_lo = max(P0, HALO)
            if c_lo < P0 + NP:
                out_toks.append(dma(yp[k * 128:(k + 1) * 128, c_lo - HALO:P0 + NP - HALO],
                                    X[:, k, c_lo - P0:NP], reads=[("X", k)]))
            if HAS_S:
                out_toks.append(dma(ys[k * 128:(k + 1) * 128, :], X[:, k, NP:NP + NS], reads=[("X", k)]))

    run = P.make_runner(sems, lanes, final_waits=out_toks)
    with nc.Block() as block:
        @block.sync
        def _(e):
            run("sp", e)

        @block.tensor
        def _(e):
            run("pe", e)

        @block.scalar
        def _(e):
            run("act", e)

        @block.vector
        def _(e):
            run("dve", e)

        @block.gpsimd
        def _(e):
            run("pool", e)
    es.close()
    return nc, A.peak


def _chunkT(v):
    return np.ascontiguousarray(v.reshape(-1, 128).T)


def kernel(x_prompt, x_sample, c_prompt, c_sample, state_conv, state_pool, state_ffn_conv,
           ada_w, ada_b, g_pre_mix, g_post_mix, g_pre_ffn, g_post_ffn, w_in, b_gate,
           ln_v_g, ln_v_b, w_spatial, b_spatial, w_a_out, w_dwconv, b_dwconv, ln_conv_g,
           ln_conv_b, w_b_out, w_pool_grp, pool_scale, w_o, w_up, w_ffn_conv, b_ffn_conv, w_down):
    f = lambda a: np.ascontiguousarray(np.asarray(a, dtype=np.float32))
    x_prompt, x_sample, c_prompt, c_sample = f(x_prompt), f(x_sample), f(c_prompt), f(c_sample)
    state_conv, state_pool, state_ffn_conv = f(state_conv), f(state_pool), f(state_ffn_conv)
    w_spatial = f(w_spatial); b_spatial = f(b_spatial)

    pp = np.zeros((DEPTH, 128, NPP), np.float32)
    for l in range(DEPTH):
        pp[l, :, O_GPM:O_GPM + 16] = _chunkT(f(g_pre_mix)[l])
        pp[l, :, O_GQM:O_GQM + 16] = _chunkT(f(g_post_mix)[l])
        pp[l, :, O_GPF:O_GPF + 16] = _chunkT(f(g_pre_ffn)[l])
        pp[l, :, O_GQF:O_GQF + 16] = _chunkT(f(g_post_ffn)[l])
        pp[l, :, O_BG:O_BG + 48] = _chunkT(f(b_gate)[l])
        pp[l, :, O_PS:O_PS + 16] = _chunkT(f(pool_scale)[l])
        pp[l, :, O_BDW:O_BDW + 8] = _chunkT(f(b_dwconv)[l])
        pp[l, :, O_LCG:O_LCG + 8] = _chunkT(f(ln_conv_g)[l])
        pp[l, :, O_LCB:O_LCB + 8] = _chunkT(f(ln_conv_b)[l])
        for tap in range(31):
            pp[l, :, O_WDW + tap * 8:O_WDW + tap * 8 + 8] = _chunkT(f(w_dwconv)[l, tap])
        for tap in range(3):
            pp[l, :, O_WFC + tap * NFC:O_WFC + (tap + 1) * NFC] = _chunkT(f(w_ffn_conv)[l, tap])
        pp[l, :, O_BFC:O_BFC + NFC] = _chunkT(f(b_ffn_conv)[l])
        pp[l, :, O_ADB:O_ADB + 96] = _chunkT(f(ada_b)[l])
    wsT = np.ascontiguousarray(w_spatial.transpose(0, 3, 1, 2)).reshape(DEPTH, 128, 8 * 128)
    bsp = np.ascontiguousarray(b_spatial.reshape(DEPTH, 8 * 128))
    w4 = w_spatial[:, :, 0:4, 0:4].transpose(0, 3, 1, 2)
    ws4 = np.ascontiguousarray(np.tile(w4.reshape(DEPTH, 1, 4, 32), (1, 16, 1, 1)).reshape(DEPTH, 64, 32))
    bs4 = np.ascontiguousarray(np.tile(b_spatial[:, :, None, 0:4], (1, 1, 16, 1)).reshape(DEPTH, 8 * 64))
    tril = np.triu(np.ones((128, 128), np.float32))
    bdm = np.zeros((64, 16, 4), np.float32)
    for sq in range(16):
        for s_ in range(4):
            bdm[sq * 4 + s_, sq, s_:] = 1.0
    bdm = bdm.reshape(64, 64)

    shared = dict(pp=pp, lnvg=f(ln_v_g), lnvb=f(ln_v_b), wsT=wsT, bsp=bsp, ws4=ws4, bs4=bs4, tril=tril,
                  bdmask=bdm, ada_w=f(ada_w), w_in=f(w_in), w_a_out=f(w_a_out), w_b_out=f(w_b_out),
                  w_pool=f(w_pool_grp), w_o=f(w_o), w_up=f(w_up), w_down=f(w_down))
    in_maps = []
    for c in range(NCORES):
        seq, half = c // 2, c % 2
        xpc = np.zeros((NPROMPT, D), np.float32)
        if half == 0:
            xpc[HALO:] = x_prompt[seq, 0:1024]
        else:
            xpc = x_prompt[seq, 1024 - HALO:2048]
        ss = slice(c * 16, (c + 1) * 16)
        cT = np.concatenate([c_prompt[seq][None, :], c_sample[ss]], axis=0).T
        pcnt = np.zeros((4, 16), np.float32)
        for gi, w in enumerate((2, 4, 8, 16)):
            for i in range(16):
                pos = half * 1024 + i
                pcnt[gi, i] = 1.0 / min(pos + 1, w)
        m = dict(shared)
        m.update(
            xp=np.ascontiguousarray(xpc.T), xs=np.ascontiguousarray(x_sample[ss].reshape(NS, D).T),
            cT=np.ascontiguousarray(cT), hmask=np.full((128, 1), float(half), np.float32),
            pcnt=np.ascontiguousarray(np.tile(pcnt.reshape(1, 64), (128, 1))),
            st_conv=np.ascontiguousarray(state_conv[:, ss].transpose(0, 3, 1, 2)).reshape(DEPTH, DA, 16 * 30),
            st_pool=np.ascontiguousarray(state_pool[:, ss].transpose(0, 3, 1, 2)).reshape(DEPTH, DA, 16 * 15),
            st_ffn=np.ascontiguousarray(state_ffn_conv[:, ss].transpose(0, 3, 1, 2)).reshape(DEPTH, DFF, 16 * 2),
        )
        in_maps.append(m)

    nc, _ = build_program()
    res = run_bass_kernel_spmd(nc, in_maps, core_ids=list(range(NCORES)))
    R = res.results

    y_prompt = np.zeros((4, 2048, D), np.float32)
    y_sample = np.zeros((128, 4, D), np.float32)
    conv_p = np.zeros((DEPTH, 4, 30, DA), np.float32); conv_s = np.zeros((DEPTH, 128, 30, DA), np.float32)
    pool_p = np.zeros((DEPTH, 4, 15, DA), np.float32); pool_s = np.zeros((DEPTH, 128, 15, DA), np.float32)
    ffn_p = np.zeros((DEPTH, 4, 2, DFF), np.float32); ffn_s = np.zeros((DEPTH, 128, 2, DFF), np.float32)
    v_s = np.zeros((DEPTH, 128, 4, DA), np.float32)
    for c in range(NCORES):
        seq, half = c // 2, c % 2
        ss = slice(c * 16, (c + 1) * 16)
        r = R[c]
        y_prompt[seq, half * 1024:(half + 1) * 1024] = r["yp"].T
        y_sample[ss] = r["ys"].T.reshape(16, 4, D)
        conv_s[:, ss] = r["o_conv_s"].reshape(DEPTH, DA, 16, 30).transpose(0, 2, 3, 1)
        pool_s[:, ss] = r["o_pool_s"].reshape(DEPTH, DA, 16, 15).transpose(0, 2, 3, 1)
        ffn_s[:, ss] = r["o_ffn_s"].reshape(DEPTH, DFF, 16, 2).transpose(0, 2, 3, 1)
        v_s[:, ss] = r["o_v_s"].reshape(DEPTH, 16, 4, DA)
        if half == 1:
            conv_p[:, seq] = r["o_conv_p"].transpose(0, 2, 1)
            pool_p[:, seq] = r["o_pool_p"].transpose(0, 2, 1)
            ffn_p[:, seq] = r["o_ffn_p"].transpose(0, 2, 1)
    return (y_prompt, y_sample, conv_p, conv_s, pool_p, pool_s, ffn_p, ffn_s, v_s)
```

```python
import numpy as np
import concourse.bass as bass
import concourse.mybir as mybir
from concourse.bass_utils import run_bass_kernel_spmd
from concourse.ap import AP as RawAP

F32 = mybir.dt.float32
BF16 = mybir.dt.bfloat16
AF = mybir.ActivationFunctionType
ALU = mybir.AluOpType

D = 2048; DA = 1024; DFF = 5504; NIN = 11264; DEPTH = 4
KC = 16; NFC = 43
EPS = 1e-6
NCORES = 8
HALO = 128
NPROMPT = 1152
NS = 64
SUPERS = [(0, 384, True), (384, 384, False), (768, 384, False)]
NPMAX = 384
TTMAX = NPMAX + NS
GATE0 = 2 * DA + 2 * DA + DA

O_GPM = 0; O_GQM = 16; O_GPF = 32; O_GQF = 48; O_BG = 64; O_PS = 112; O_BDW = 128
O_LCG = 136; O_LCB = 144; O_WDW = 152; O_WFC = 400; O_BFC = 529; O_ADB = 572; NPP = 668

ENGS = ("pe", "act", "dve", "pool", "sp")


class Plan:
    def __init__(self):
        self.ops = {e: [] for e in ENGS}
        self.last_write = {}
        self.readers = {}
        self.lane_count = {}
        self.lane_last = {}
        self.inherit = {}
        self.carry = {}

    @staticmethod
    def _merge(dst, tok):
        ch = tok[:2]
        old = dst.get(ch)
        if old is None or old[2] < tok[2]:
            dst[ch] = tok

    def _init_key(self, k):
        if k not in self.readers:
            name = k[0] if isinstance(k, tuple) else k
            self.readers[k] = dict(self.inherit.get(name, {}))

    def fence(self, dying, newnames):
        merged = {}
        for k, t in self.last_write.items():
            name = k[0] if isinstance(k, tuple) else k
            if name in dying:
                self._merge(merged, t)
        for k, rd in self.readers.items():
            name = k[0] if isinstance(k, tuple) else k
            if name in dying:
                for t in rd.values():
                    self._merge(merged, t)
        for t in merged.values():
            self._merge(self.carry, t)
        merged = self.carry
        for n in newnames:
            self.inherit[n] = dict(merged)
        for k in list(self.readers.keys()):
            name = k[0] if isinstance(k, tuple) else k
            if name in newnames:
                for t in merged.values():
                    self._merge(self.readers[k], t)

    def add(self, eng, emit, reads=(), writes=(), lane=None, serialize=True):
        idx = len(self.ops[eng])
        if lane is not None:
            cnt = self.lane_count.get(lane, 0) + 1
            self.lane_count[lane] = cnt
            tok = ("d", lane, cnt)
        else:
            tok = ("c", eng, idx)
        deps = set()
        if lane is not None and serialize and lane in self.lane_last:
            deps.add(self.lane_last[lane])
        for k in reads:
            self._init_key(k)
            t = self.last_write.get(k)
            if t is not None:
                deps.add(t)
        for k in writes:
            self._init_key(k)
            t = self.last_write.get(k)
            if t is not None:
                deps.add(t)
            for t in self.readers[k].values():
                deps.add(t)
        final = []
        for t in deps:
            if t == tok:
                continue
            if t[0] == "c" and lane is None and t[1] == eng:
                if eng == "pe":
                    continue
                is_raw = any(self.last_write.get(k) == t for k in reads)
                if not is_raw:
                    continue
            final.append(t)
            if t[0] == "c":
                self.ops[t[1]][t[2]]["inc"] = True
        self.ops[eng].append(dict(emit=emit, deps=final, inc=False, lane=lane))
        if lane is not None:
            self.lane_last[lane] = tok
        for k in reads:
            self._merge(self.readers[k], tok)
        for k in writes:
            self.last_write[k] = tok
            self.readers[k] = {}
        return tok

    def make_runner(self, sems, lane_sems, final_waits=()):
        counts = {}
        for e in ENGS:
            c = 0
            lst = []
            for op in self.ops[e]:
                if op["inc"] and op["lane"] is None:
                    c += 1
                lst.append(c)
            counts[e] = lst

        def tokval(t):
            if t[0] == "c":
                return sems[t[1]], counts[t[1]][t[2]]
            return lane_sems[t[1]], 16 * t[2]

        def run(e, engine):
            known = {}
            for op in self.ops[e]:
                for t in op["deps"]:
                    s, v = tokval(t)
                    if known.get(id(s), -1) >= v:
                        continue
                    known[id(s)] = v
                    engine.wait_ge(s, v)
                ins = op["emit"](engine)
                if op["lane"] is not None:
                    ins.then_inc(lane_sems[op["lane"]], 16)
                elif op["inc"]:
                    ins.then_inc(sems[e], 1)
            if e == "sp":
                for t in final_waits:
                    s, v = tokval(t)
                    engine.wait_ge(s, v)
        return run


class Arena:
    def __init__(self, nc):
        self.nc = nc
        self.off = (nc.sbuf_base + 63) // 64 * 64
        self.top = nc.sbuf_top
        self.live = []
        self.uid = 0
        self.peak = self.off

    def alloc(self, name, shape, dt):
        nb = int(np.prod(shape[1:])) * (4 if dt == F32 else 2)
        nb = (nb + 63) // 64 * 64
        o = self.off
        self.off += nb
        self.peak = max(self.peak, self.off)
        assert self.off <= self.top, f"SBUF overflow at {name}: {self.off} > {self.top}"
        self.uid += 1
        t = self.nc.alloc_sbuf_tensor_at(f"{name}_{self.uid}", shape, dt, offset=o)
        self.live.append(name)
        return t

    def mark(self):
        return (self.off, len(self.live))

    def reset(self, m):
        dying = set(self.live[m[1]:])
        self.off = m[0]
        del self.live[m[1]:]
        return dying


def build_program():
    nc = bass.Bass("TRN2", target_bir_lowering=False)

    def din(name, shape):
        return nc.dram_tensor(name, list(shape), F32, kind="ExternalInput").ap()

    def dout(name, shape):
        return nc.dram_tensor(name, list(shape), F32, kind="ExternalOutput").ap()

    xp = din("xp", [D, NPROMPT]); xs = din("xs", [D, NS]); cT = din("cT", [D, 17])
    hmask = din("hmask", [128, 1]); pcnt = din("pcnt", [128, 64]); tril = din("tril", [128, 128])
    bdmask = din("bdmask", [64, 64])
    st_conv = din("st_conv", [DEPTH, DA, 16 * 30]); st_pool = din("st_pool", [DEPTH, DA, 16 * 15])
    st_ffn = din("st_ffn", [DEPTH, DFF, 16 * 2])
    pp = din("pp", [DEPTH, 128, NPP])
    lnvg = din("lnvg", [DEPTH, DA]); lnvb = din("lnvb", [DEPTH, DA])
    wsT = din("wsT", [DEPTH, 128, 8 * 128]); bsp = din("bsp", [DEPTH, 8 * 128])
    ws4 = din("ws4", [DEPTH, 64, 32]); bs4 = din("bs4", [DEPTH, 8 * 64])
    ada_w = din("ada_w", [DEPTH, D, 6 * D]); w_in = din("w_in", [DEPTH, D, NIN])
    w_a_out = din("w_a_out", [DEPTH, DA, D]); w_b_out = din("w_b_out", [DEPTH, DA, D])
    w_pool = din("w_pool", [DEPTH, 4, 256, 512]); w_o = din("w_o", [DEPTH, D, D])
    w_up = din("w_up", [DEPTH, D, 2 * DFF]); w_down = din("w_down", [DEPTH, DFF, D])

    yp = dout("yp", [D, 1024]); ys = dout("ys", [D, NS])
    o_conv_p = dout("o_conv_p", [DEPTH, DA, 30]); o_conv_s = dout("o_conv_s", [DEPTH, DA, 16 * 30])
    o_pool_p = dout("o_pool_p", [DEPTH, DA, 15]); o_pool_s = dout("o_pool_s", [DEPTH, DA, 16 * 15])
    o_ffn_p = dout("o_ffn_p", [DEPTH, DFF, 2]); o_ffn_s = dout("o_ffn_s", [DEPTH, DFF, 16 * 2])
    o_v_s = dout("o_v_s", [DEPTH, NS, DA])

    modsc = nc.dram_tensor("modsc", [DEPTH, 128, 96 * 17], F32).ap()
    P = Plan()
    A = Arena(nc)
    out_toks = []

    X = A.alloc("X", [128, KC, TTMAX], F32)
    H = A.alloc("H", [128, KC, TTMAX], BF16)
    MODC = A.alloc("MODC", [128, 96, 17], F32)
    PPt = A.alloc("PP", [128, DEPTH, NPP], F32)
    NSLOT = 3
    SLOT_ELEMS = 8192
    WS = [A.alloc(f"WS{i}", [128, SLOT_ELEMS], BF16) for i in range(NSLOT)]
    RSTD = A.alloc("RSTD", [128, TTMAX], F32)
    NSQ = 3
    SQ = [A.alloc(f"SQ{i}", [128, 512], BF16) for i in range(NSQ)]
    NTMP = 2
    TMPF = [A.alloc(f"TMPF{i}", [128, 512], F32) for i in range(NTMP)]
    NSIG = 2
    SIG = [A.alloc(f"SIG{i}", [128, 512], F32) for i in range(NSIG)]
    HCONV = A.alloc("HCONV", [128, DEPTH, 8, 30], F32)
    HPOOL = A.alloc("HPOOL", [128, DEPTH, 8, 15], F32)
    HFFN = A.alloc("HFFN", [128, DEPTH, NFC, 2], F32)
    ONES = A.alloc("ONES", [128, 128], BF16)
    TRIL = A.alloc("TRIL", [128, 128], F32)
    BDM = A.alloc("BDM", [64, 64], F32)
    HMASK = A.alloc("HMASK", [128, 1], F32)
    PCNT = A.alloc("PCNT", [128, 64], F32)
    MV = A.alloc("MV", [128, 32], F32)
    ada_mark = A.mark()
    CT = A.alloc("CT", [128, KC, 17], F32)
    SC = A.alloc("SC", [128, KC, 17], BF16)

    from contextlib import ExitStack
    es = ExitStack()
    PSB = [es.enter_context(nc.psum_tensor(f"psb{i}", [128, 512], F32)) for i in range(8)]
    sems = {e: es.enter_context(nc.semaphore(f"s_{e}")) for e in ENGS}
    NMISC = 8
    lanes = {}
    for i in range(NSLOT):
        lanes[f"w{i}"] = es.enter_context(nc.semaphore(f"l_w{i}"))
    for i in range(NMISC):
        lanes[f"m{i}"] = es.enter_context(nc.semaphore(f"l_m{i}"))

    st = dict(misc=0, slot=0, bank=0, sq=0, tmp=0, sig=0)

    def misc_lane():
        st["misc"] = (st["misc"] + 1) % NMISC
        return f"m{st['misc']}"

    def dma(out, in_, reads=(), writes=(), eng="sp"):
        return P.add(eng, lambda e: e.dma_start(out=out, in_=in_), reads=reads, writes=writes,
                     lane=misc_lane())

    def next_bank():
        b = st["bank"]
        st["bank"] = (b + 1) % 6
        return b

    def next_sq():
        st["sq"] = (st["sq"] + 1) % NSQ
        return st["sq"]

    def next_tmp():
        st["tmp"] = (st["tmp"] + 1) % NTMP
        return st["tmp"]

    def next_sig():
        st["sig"] = (st["sig"] + 1) % NSIG
        return st["sig"]

    def act(out, in_, func, reads, writes, bias=0.0, scale=1.0):
        return P.add("act", lambda e: e.activation(out=out, in_=in_, func=func, bias=bias, scale=scale),
                     reads=reads, writes=writes)

    def tt(out, in0, in1, op, reads, writes, eng="dve"):
        return P.add(eng, lambda e: e.tensor_tensor(out=out, in0=in0, in1=in1, op=op),
                     reads=reads, writes=writes)

    def ts(out, in0, s1, s2, op0, op1, reads, writes, eng="dve"):
        if s2 is None:
            return P.add(eng, lambda e: e.tensor_scalar(out=out, in0=in0, scalar1=s1, scalar2=None, op0=op0),
                         reads=reads, writes=writes)
        return P.add(eng, lambda e: e.tensor_scalar(out=out, in0=in0, scalar1=s1, scalar2=s2, op0=op0, op1=op1),
                     reads=reads, writes=writes)

    def stt(out, in0, scalar, in1, op0, op1, reads, writes, eng="dve"):
        return P.add(eng, lambda e: e.scalar_tensor_tensor(out=out, in0=in0, scalar=scalar, in1=in1,
                                                            op0=op0, op1=op1), reads=reads, writes=writes)

    def copy(out, in_, reads, writes, eng="dve"):
        return P.add(eng, lambda e: e.tensor_copy(out, in_), reads=reads, writes=writes)

    def memset(ap, val, writes, eng="dve"):
        return P.add(eng, lambda e: e.memset(ap, val), writes=writes)

    def mm(ps_ap, lhsT, rhs, start, stop, reads, bank):
        return P.add("pe", lambda e: e.matmul(ps_ap, lhsT=lhsT, rhs=rhs, start=start, stop=stop),
                     reads=reads, writes=[("ps", bank)])

    def load_unit(pieces):
        s = st["slot"]
        st["slot"] = (s + 1) % NSLOT
        views = []
        off = 0
        for pi, (src, kc) in enumerate(pieces):
            ncols = src.shape[1]
            v = WS[s][:, off:off + kc * ncols].rearrange("p (k c) -> p k c", c=ncols)
            srcv = src.rearrange("(k p) c -> p k c", p=128)
            P.add("pool", lambda e, v=v, srcv=srcv: e.dma_start(out=v, in_=srcv), writes=[("WS", s, pi)],
                  lane=f"w{s}", serialize=False)
            views.append(v)
            off += kc * ncols
        assert off <= SLOT_ELEMS
        return s, views

    def chain(ps_ap, bank, lhs_list, rhs_list, reads, first=True, last=True):
        n = len(lhs_list)
        for k in range(n):
            mm(ps_ap, lhs_list[k], rhs_list[k], start=(first and k == 0), stop=(last and k == n - 1),
               reads=reads, bank=bank)

    dma(PPt[:], pp.rearrange("l p c -> p l c"), writes=["PP"])
    dma(TRIL[:], tril, writes=["TRIL"])
    dma(BDM[:], bdmask, writes=["BDM"])
    dma(HMASK[:], hmask, writes=["HMASK"])
    dma(PCNT[:], pcnt, writes=["PCNT"])
    dma(CT[:], cT.rearrange("(k p) j -> p k j", p=128), writes=["CT"])
    memset(ONES[:], 1.0, ["ONES"])
    memset(HCONV[:], 0.0, ["HCONV"])
    memset(HPOOL[:], 0.0, ["HPOOL"])
    memset(HFFN[:], 0.0, ["HFFN"])
    act(SC[:], CT[:], AF.Silu, ["CT"], ["SC"])

    for l in range(DEPTH):
        for u in range(24):
            s, (wv,) = load_unit([(ada_w[l][:, u * 512:(u + 1) * 512], KC)])
            for m in range(4):
                mi = u * 4 + m
                b = next_bank()
                chain(PSB[b][:, 0:17], b, [wv[:, k, m * 128:(m + 1) * 128] for k in range(KC)],
                      [SC[:, k, :] for k in range(KC)], reads=[("WS", s, 0), ("WS", s, 1), "SC"])
                act(MODC[:, mi, :], PSB[b][:, 0:17], AF.Identity, [("ps", b), "PP"], ["MODC"],
                    bias=PPt[:, l, O_ADB + mi:O_ADB + mi + 1])
        for (c0, goff) in ((16, O_GPM), (64, O_GPF)):
            ts(MODC[:, c0:c0 + 16, :], MODC[:, c0:c0 + 16, :], 1.0, None, ALU.add, None,
               ["MODC"], ["MODC"])
        for (c0, goff) in ((16, O_GPM), (64, O_GPF), (32, O_GQM), (80, O_GQF)):
            tt(MODC[:, c0:c0 + 16, :], MODC[:, c0:c0 + 16, :],
               PPt[:, l, goff:goff + 16].unsqueeze(2).to_broadcast([128, 16, 17]), ALU.mult,
               ["MODC", "PP"], ["MODC"])
        dma(modsc[l], MODC[:].rearrange("p a b -> p (a b)"), reads=["MODC"], writes=[("MODSC", l)])

    dying = A.reset(ada_mark)
    P.fence(dying, [])
    base_mark = A.mark()

    for si, (P0, NP, HAS_S) in enumerate(SUPERS):
        TT = NP + (NS if HAS_S else 0)
        first_super = (si == 0)
        last_super = (si == len(SUPERS) - 1)
        pblocks = []
        o = 0
        while o < NP:
            n = min(512, NP - o)
            pblocks.append((o, n))
            o += n
        blocks = [(o_, n_, False) for (o_, n_) in pblocks] + ([(NP, NS, True)] if HAS_S else [])
        mblocks = [[o_, n_, [(o_, n_, False)]] for (o_, n_) in pblocks]
        if HAS_S:
            if mblocks[-1][1] + NS <= 512:
                mblocks[-1][1] += NS
                mblocks[-1][2].append((NP, NS, True))
            else:
                mblocks.append([NP, NS, [(NP, NS, True)]])
        assert len(mblocks) <= 2
        ntile = NP // 128

        for k in range(KC):
            dma(X[:, k, 0:NP], xp[k * 128:(k + 1) * 128, P0:P0 + NP], writes=[("X", k)])
            if HAS_S:
                dma(X[:, k, NP:NP + NS], xs[k * 128:(k + 1) * 128, :], writes=[("X", k)])

        def s3(ap):
            return ap.rearrange("p (s t) -> p s t", t=4)

        def bc_s(ap16):
            return ap16.unsqueeze(2).to_broadcast([128, 16, 4])

        def rstd_from(bank, o_, n_, scale):
            act(RSTD[:, o_:o_ + n_], PSB[bank][:, 0:n_], AF.Sqrt, [("ps", bank)], ["RSTD"],
                bias=EPS, scale=scale)
            P.add("dve", lambda e: e.reciprocal(out=RSTD[:, o_:o_ + n_], in_=RSTD[:, o_:o_ + n_]),
                  reads=["RSTD"], writes=["RSTD"])

        def prenorm(l, a_off, b_off):
            for bi_, (O_, N_, subs_) in enumerate(mblocks):
                b = 6 + bi_
                for k in range(KC):
                    q = next_sq()
                    act(SQ[q][:, 0:N_], X[:, k, O_:O_ + N_], AF.Square, [("X", k)], [("SQ", q)])
                    mm(PSB[b][:, 0:N_], ONES[:], SQ[q][:, 0:N_], start=(k == 0), stop=(k == KC - 1),
                       reads=["ONES", ("SQ", q)], bank=b)
                rstd_from(b, O_, N_, 1.0 / D)
            for (o_, n_, is_s) in blocks:
                for k in range(KC):
                    t_ = next_tmp()
                    tt(TMPF[t_][:, 0:n_], X[:, k, o_:o_ + n_], RSTD[:, o_:o_ + n_], ALU.mult,
                       [("X", k), "RSTD"], [("TMPF", t_)])
                    if not is_s:
                        act(H[:, k, o_:o_ + n_], TMPF[t_][:, 0:n_], AF.Identity,
                            [("TMPF", t_), "MODC"], [("H", k)],
                            bias=MODC[:, b_off + k, 0:1], scale=MODC[:, a_off + k, 0:1])
                    else:
                        tt(s3(TMPF[t_][:, 0:n_]), s3(TMPF[t_][:, 0:n_]), bc_s(MODC[:, a_off + k, 1:17]),
                           ALU.mult, [("TMPF", t_), "MODC"], [("TMPF", t_)])
                        tt(s3(H[:, k, o_:o_ + n_]), s3(TMPF[t_][:, 0:n_]), bc_s(MODC[:, b_off + k, 1:17]),
                           ALU.add, [("TMPF", t_), "MODC"], [("H", k)])

        def postnorm(l, g_off, YMO):
            for bi, (o_, n_, is_s) in enumerate(blocks):
                for mi in range(KC):
                    t_ = next_tmp()
                    tt(TMPF[t_][:, 0:n_], YMO[:, mi, o_:o_ + n_], RSTD[:, o_:o_ + n_], ALU.mult,
                       [("YMO", mi), "RSTD"], [("TMPF", t_)])
                    if not is_s:
                        stt(X[:, mi, o_:o_ + n_], TMPF[t_][:, 0:n_], MODC[:, g_off + mi, 0:1],
                            X[:, mi, o_:o_ + n_], ALU.mult, ALU.add,
                            [("TMPF", t_), "MODC", ("X", mi)], [("X", mi)])
                    else:
                        tt(s3(TMPF[t_][:, 0:n_]), s3(TMPF[t_][:, 0:n_]), bc_s(MODC[:, g_off + mi, 1:17]),
                           ALU.mult, [("TMPF", t_), "MODC"], [("TMPF", t_)])
                        tt(X[:, mi, o_:o_ + n_], X[:, mi, o_:o_ + n_], TMPF[t_][:, 0:n_], ALU.add,
                           [("TMPF", t_), ("X", mi)], [("X", mi)])

        def out_stage(l, YMO, units, rhs_fn, nk_total_fn):
            pass

        for l in range(DEPTH):
            m0 = A.mark()
            dma(MODC[:].rearrange("p a b -> p (a b)"), modsc[l], reads=[("MODSC", l)], writes=["MODC"])
            MIXACC = A.alloc("MIXACC", [128, KC, TT], BF16)
            CACC = A.alloc("CACC", [128, 8, TT], F32)
            XBUF = [A.alloc(f"XBUF{i}", [128, 30 + NP], F32) for i in range(2)]
            XSC = [A.alloc(f"XSC{i}", [128, 16, 34], F32) for i in range(2)]
            CAC2 = A.alloc("CAC2", [128, NP], F32)
            P.fence(set(), ["MIXACC", "CACC", "XBUF0", "XBUF1", "XSC0", "XSC1", "CAC2"])
            prenorm(l, 16, 0)
            mA = A.mark()

            def gated_out(l, src_buf, src_key, kc_src, w_src_fn, gate_col0, first, scale_off=None,
                          rhs_chunk_fn=None, pre_unit=None):
                for u in range(8):
                    if pre_unit is not None:
                        pre_unit(u)
                    c0 = u * 256
                    wsrc, kcs = w_src_fn(c0)
                    s, (wy, wg) = load_unit([(wsrc, kcs),
                                              (w_in[l][:, gate_col0 + c0:gate_col0 + c0 + 256], KC)])
                    for m in range(2):
                        mi = u * 2 + m
                        for (o_, n_, _sb) in mblocks:
                            b1 = next_bank()
                            chain(PSB[b1][:, 0:n_], b1, [wy[:, k, m * 128:(m + 1) * 128] for k in range(kcs)],
                                  [rhs_chunk_fn(mi, k, o_, n_) for k in range(kcs)],
                                  reads=[("WS", s, 0), ("WS", s, 1)] + [(src_key, kk) for kk in range(8)])
                            b2 = next_bank()
                            chain(PSB[b2][:, 0:n_], b2, [wg[:, k, m * 128:(m + 1) * 128] for k in range(KC)],
                                  [H[:, k, o_:o_ + n_] for k in range(KC)],
                                  reads=[("WS", s, 0), ("WS", s, 1)] + [("H", kk) for kk in range(KC)])
                            g = next_sig()
                            act(SIG[g][:, 0:n_], PSB[b2][:, 0:n_], AF.Sigmoid, [("ps", b2), "PP"], [("SIG", g)],
                                bias=PPt[:, l, O_BG + (gate_col0 - GATE0) // 128 + mi:
                                         O_BG + (gate_col0 - GATE0) // 128 + mi + 1])
                            if first:
                                tt(MIXACC[:, mi, o_:o_ + n_], PSB[b1][:, 0:n_], SIG[g][:, 0:n_], ALU.mult,
                                   [("ps", b1), ("SIG", g)], [("MIXACC", mi)])
                            else:
                                t_ = next_tmp()
                                if scale_off is None:
                                    tt(TMPF[t_][:, 0:n_], PSB[b1][:, 0:n_], SIG[g][:, 0:n_], ALU.mult,
                                       [("ps", b1), ("SIG", g)], [("TMPF", t_)])
                                else:
                                    stt(TMPF[t_][:, 0:n_], PSB[b1][:, 0:n_],
                                        PPt[:, l, scale_off + mi:scale_off + mi + 1], SIG[g][:, 0:n_],
                                        ALU.mult, ALU.mult, [("ps", b1), ("SIG", g), "PP"], [("TMPF", t_)])
                                tt(MIXACC[:, mi, o_:o_ + n_], MIXACC[:, mi, o_:o_ + n_], TMPF[t_][:, 0:n_],
                                   ALU.add, [("MIXACC", mi), ("TMPF", t_)], [("MIXACC", mi)])

            NTT = ntile + (1 if HAS_S else 0)
            VN = A.alloc("VN", [128, NTT, DA], BF16)
            U = A.alloc("U", [128, 8, TT], BF16)
            mA1 = A.mark()
            VF = A.alloc("VF", [128, NTT, DA], F32)
            LNG = A.alloc("LNG", [128, DA], F32)
            LNB = A.alloc("LNB", [128, DA], F32)
            P.fence(set(), ["VN", "U", "VF", "LNG", "LNB"])
            dma(LNG[:], lnvg[l:l + 1, :].partition_broadcast(128), writes=["LNG"])
            dma(LNB[:], lnvb[l:l + 1, :].partition_broadcast(128), writes=["LNB"])
            tiles = [(t_ * 128, 128, t_) for t_ in range(ntile)] + ([(NP, NS, ntile)] if HAS_S else [])
            for uu in range(2):
                s, (wv,) = load_unit([(w_in[l][:, DA + uu * 512:DA + (uu + 1) * 512], KC)])
                for (o_, n_, ti) in tiles:
                    b = next_bank()
                    chain(PSB[b][0:n_, 0:512], b, [H[:, k, o_:o_ + n_] for k in range(KC)],
                          [wv[:, k, :] for k in range(KC)], reads=[("WS", s, 0), ("WS", s, 1)] + [("H", kk) for kk in range(KC)])
                    act(VF[0:n_, ti, uu * 512:(uu + 1) * 512], PSB[b][0:n_, 0:512], AF.Gelu_apprx_tanh,
                        [("ps", b)], [("VF", ti)])
            for (o_, n_, ti) in tiles:
                for hh in range(2):
                    P.add("dve", lambda e, n_=n_, ti=ti, hh=hh, VF=VF: e.bn_stats(
                        out=MV[0:n_, hh * 6:(hh + 1) * 6], in_=VF[0:n_, ti, hh * 512:(hh + 1) * 512]),
                        reads=[("VF", ti)], writes=["MV"])
                P.add("dve", lambda e, n_=n_: e.bn_aggr(out=MV[0:n_, 16:18], in_=MV[0:n_, 0:12]),
                      reads=["MV"], writes=["MV"])
                act(MV[0:n_, 18:19], MV[0:n_, 17:18], AF.Sqrt, ["MV"], ["MV"], bias=EPS, scale=1.0)
                P.add("dve", lambda e, n_=n_: e.reciprocal(out=MV[0:n_, 18:19], in_=MV[0:n_, 18:19]),
                      reads=["MV"], writes=["MV"])
                ts(VF[0:n_, ti, :], VF[0:n_, ti, :], MV[0:n_, 16:17], MV[0:n_, 18:19], ALU.subtract, ALU.mult,
                   [("VF", ti), "MV"], [("VF", ti)])
                tt(VF[0:n_, ti, :], VF[0:n_, ti, :], LNG[0:n_, :], ALU.mult, [("VF", ti), "LNG"], [("VF", ti)])
                if ti < ntile:
                    tt(VN[0:n_, ti, :], VF[0:n_, ti, :], LNB[0:n_, :], ALU.add, [("VF", ti), "LNB"], [("VN", ti)])
                else:
                    tt(VF[0:n_, ti, :], VF[0:n_, ti, :], LNB[0:n_, :], ALU.add, [("VF", ti), "LNB"], [("VF", ti)])
                    act(VN[0:n_, ti, :], VF[0:n_, ti, :], AF.Identity, [("VF", ti)], [("VN", ti)])
                    out_toks.append(dma(o_v_s[l], VF[0:NS, ti, :], reads=[("VF", ti)]))
            for uu in range(2):
                s, (wv,) = load_unit([(w_in[l][:, uu * 512:(uu + 1) * 512], KC)])
                for m in range(4):
                    mi = uu * 4 + m
                    for (o_, n_, _sb) in mblocks:
                        b = next_bank()
                        chain(PSB[b][:, 0:n_], b, [wv[:, k, m * 128:(m + 1) * 128] for k in range(KC)],
                              [H[:, k, o_:o_ + n_] for k in range(KC)],
                              reads=[("WS", s, 0), ("WS", s, 1)] + [("H", kk) for kk in range(KC)])
                        act(U[:, mi, o_:o_ + n_], PSB[b][:, 0:n_], AF.Gelu_apprx_tanh, [("ps", b)], [("U", mi)])
            dying = A.reset(mA1)
            WST = A.alloc("WST", [128, 8, 128], F32)
            WSB = A.alloc("WSB", [128, 8, 128], BF16)
            BSF = A.alloc("BSF", [1, 1024], F32)
            BSR = A.alloc("BSR", [1, 1024], BF16)
            WS4 = A.alloc("WS4", [64, 32], F32)
            BDB = A.alloc("BDB", [64, 8, 64], BF16)
            BS4F = A.alloc("BS4F", [1, 512], F32)
            BS4R = A.alloc("BS4R", [1, 512], BF16)
            PW = A.alloc("PW", [128, 16 * 4 * 31], F32) if HAS_S else None
            P.fence(dying, ["WST", "WSB", "BSF", "BSR", "WS4", "BDB", "BS4F", "BS4R", "PW"])
            dma(WST[:], wsT[l].rearrange("p (g t) -> p g t", t=128), writes=["WST"])
            tt(WSB[:], WST[:], TRIL[:].unsqueeze(1).to_broadcast([128, 8, 128]), ALU.mult,
               ["WST", "TRIL"], ["WSB"])
            dma(BSF[:], bsp[l:l + 1, :], writes=["BSF"])
            copy(BSR[:], BSF[:], ["BSF"], ["BSR"])
            if HAS_S:
                dma(WS4[:], ws4[l], writes=["WS4"])
                tt(BDB[:].rearrange("p g (s t) -> p g s t", t=4),
                   WS4[:].rearrange("p (g t) -> p g t", t=4).unsqueeze(2).to_broadcast([64, 8, 16, 4]),
                   BDM[:].rearrange("p (s t) -> p s t", t=4).unsqueeze(1).to_broadcast([64, 8, 16, 4]),
                   ALU.mult, ["WS4", "BDM"], ["BDB"])
                dma(BS4F[:], bs4[l:l + 1, :], writes=["BS4F"])
                copy(BS4R[:], BS4F[:], ["BS4F"], ["BS4R"])
            for (o_, n_, ti) in tiles:
                for g in range(8):
                    b = next_bank()
                    if ti < ntile:
                        mm(PSB[b][:, 0:128], VN[:, ti, g * 128:(g + 1) * 128], WSB[:, g, :], True, False,
                           [("VN", ti), "WSB"], b)
                        mm(PSB[b][:, 0:128], ONES[0:1, :], BSR[0:1, g * 128:(g + 1) * 128], False, True,
                           ["ONES", "BSR"], b)
                    else:
                        mm(PSB[b][:, 0:NS], VN[0:NS, ti, g * 128:(g + 1) * 128], BDB[:, g, :], True, False,
                           [("VN", ti), "BDB"], b)
                        mm(PSB[b][:, 0:NS], ONES[0:1, :], BS4R[0:1, g * NS:(g + 1) * NS], False, True,
                           ["ONES", "BS4R"], b)
                    tt(U[:, g, o_:o_ + n_], PSB[b][:, 0:n_], U[:, g, o_:o_ + n_], ALU.mult,
                       [("ps", b), ("U", g)], [("U", g)])

            def glu_unit(uo):
                c0 = uo * 256
                s, (wa, wb) = load_unit([(w_in[l][:, 2 * DA + c0:2 * DA + c0 + 256], KC),
                                         (w_in[l][:, 3 * DA + c0:3 * DA + c0 + 256], KC)])
                for m in range(2):
                    ci = uo * 2 + m
                    xb_ = XBUF[ci % 2]
                    xk = f"XBUF{ci % 2}"
                    xs_ = XSC[ci % 2]
                    sk = f"XSC{ci % 2}"
                    copy(xb_[:, 0:30], HCONV[:, l, ci, :], ["HCONV"], [xk])
                    if HAS_S:
                        dma(xs_[:, :, 0:30], st_conv[l][ci * 128:(ci + 1) * 128, :].rearrange("p (s r) -> p s r", r=30),
                            writes=[sk])
                    for (O_, N_, subs_) in mblocks:
                        b1 = next_bank()
                        chain(PSB[b1][:, 0:N_], b1, [wa[:, k, m * 128:(m + 1) * 128] for k in range(KC)],
                              [H[:, k, O_:O_ + N_] for k in range(KC)],
                              reads=[("WS", s, 0), ("WS", s, 1)] + [("H", kk) for kk in range(KC)])
                        b2 = next_bank()
                        chain(PSB[b2][:, 0:N_], b2, [wb[:, k, m * 128:(m + 1) * 128] for k in range(KC)],
                              [H[:, k, O_:O_ + N_] for k in range(KC)],
                              reads=[("WS", s, 0), ("WS", s, 1)] + [("H", kk) for kk in range(KC)])
                        g = next_sig()
                        act(SIG[g][:, 0:N_], PSB[b2][:, 0:N_], AF.Sigmoid, [("ps", b2)], [("SIG", g)])
                        for (o_, n_, is_s) in subs_:
                            r0 = o_ - O_
                            if not is_s:
                                tt(xb_[:, 30 + o_:30 + o_ + n_], PSB[b1][:, r0:r0 + n_], SIG[g][:, r0:r0 + n_], ALU.mult,
                                   [("ps", b1), ("SIG", g)], [xk])
                            else:
                                tt(xs_[:, :, 30:34], s3(PSB[b1][:, r0:r0 + n_]), s3(SIG[g][:, r0:r0 + n_]), ALU.mult,
                                   [("ps", b1), ("SIG", g)], [sk])
                    if first_super:
                        ts(xb_[:, 30:30 + HALO], xb_[:, 30:30 + HALO], HMASK[:, 0:1], None, ALU.mult, None,
                           [xk, "HMASK"], [xk])
                    wcol = lambda tap: PPt[:, l, O_WDW + tap * 8 + ci:O_WDW + tap * 8 + ci + 1]
                    ts(CACC[:, ci, 0:NP], xb_[:, 0:NP], wcol(0), PPt[:, l, O_BDW + ci:O_BDW + ci + 1],
                       ALU.mult, ALU.add, [xk, "PP"], [("CACC", ci)])
                    ts(CAC2[:, 0:NP], xb_[:, 1:1 + NP], wcol(1), None, ALU.mult, None, [xk, "PP"], ["CAC2"])
                    for tap in range(2, 31):
                        if tap % 2 == 0:
                            stt(CACC[:, ci, 0:NP], xb_[:, tap:tap + NP], wcol(tap), CACC[:, ci, 0:NP], ALU.mult,
                                ALU.add, [xk, "PP", ("CACC", ci)], [("CACC", ci)])
                        else:
                            stt(CAC2[:, 0:NP], xb_[:, tap:tap + NP], wcol(tap), CAC2[:, 0:NP], ALU.mult,
                                ALU.add, [xk, "PP", "CAC2"], ["CAC2"])
                    tt(CACC[:, ci, 0:NP], CACC[:, ci, 0:NP], CAC2[:, 0:NP], ALU.add, [("CACC", ci), "CAC2"],
                       [("CACC", ci)])
                    copy(HCONV[:, l, ci, :], xb_[:, NP:NP + 30], [xk], ["HCONV"])
                    if last_super:
                        out_toks.append(dma(o_conv_p[l][ci * 128:(ci + 1) * 128, :], xb_[:, NP:NP + 30], reads=[xk]))
                    if HAS_S:
                        cs_ = s3(CACC[:, ci, NP:NP + NS])
                        w0_ = xs_[:, :, 0:4]
                        win_ = RawAP(w0_.tensor, w0_.offset, [[w0_.ap[0][0], 128], [34, 16], [1, 4], [1, 31]])
                        c0_ = wcol(0)
                        wbc_ = RawAP(c0_.tensor, c0_.offset, [[c0_.ap[0][0], 128], [0, 16], [0, 4], [8, 31]])
                        pw4_ = PW[:, :].rearrange("p (s t k) -> p s t k", t=4, k=31)
                        tt(pw4_, win_, wbc_, ALU.mult, [sk, "PP"], ["PW"])
                        P.add("dve", lambda e, cs_=cs_, pw4_=pw4_: e.tensor_reduce(
                            out=cs_, in_=pw4_, axis=mybir.AxisListType.X, op=ALU.add),
                            reads=["PW"], writes=[("CACC", ci)])
                        ts(cs_, cs_, PPt[:, l, O_BDW + ci:O_BDW + ci + 1], None, ALU.add, None,
                           [("CACC", ci), "PP"], [("CACC", ci)])
                        out_toks.append(dma(o_conv_s[l][ci * 128:(ci + 1) * 128, :].rearrange("p (s r) -> p s r", r=30),
                                            xs_[:, :, 4:34], reads=[sk]))
            gated_out(l, U, "U", 8, lambda c0: (w_a_out[l][:, c0:c0 + 256], 8), GATE0, True,
                      rhs_chunk_fn=lambda mi, k, o_, n_: U[:, k, o_:o_ + n_],
                      pre_unit=lambda u: glu_unit(u // 2) if u % 2 == 0 else None)
            dying = A.reset(mA)
            YBIN = A.alloc("YBIN", [128, 8, TT], BF16)
            MEAN = A.alloc("MEAN", [128, 512], F32)
            VAR = A.alloc("VAR", [128, 512], F32)
            P.fence(dying, ["YBIN", "MEAN", "VAR"])
            for (o_, n_, _sb) in mblocks:
                for ci in range(8):
                    q = next_sq()
                    act(SQ[q][:, 0:n_], CACC[:, ci, o_:o_ + n_], AF.Square, [("CACC", ci)], [("SQ", q)])
                    mm(PSB[7][:, 0:n_], ONES[:], SQ[q][:, 0:n_], ci == 0, ci == 7, ["ONES", ("SQ", q)], 7)
                    q2 = next_sq()
                    copy(SQ[q2][:, 0:n_], CACC[:, ci, o_:o_ + n_], [("CACC", ci)], [("SQ", q2)])
                    mm(PSB[6][:, 0:n_], ONES[:], SQ[q2][:, 0:n_], ci == 0, ci == 7, ["ONES", ("SQ", q2)], 6)
                ts(MEAN[:, 0:n_], PSB[6][:, 0:n_], 1.0 / DA, None, ALU.mult, None, [("ps", 6)], ["MEAN"])
                tt(VAR[:, 0:n_], MEAN[:, 0:n_], MEAN[:, 0:n_], ALU.mult, ["MEAN"], ["VAR"])
                stt(VAR[:, 0:n_], PSB[7][:, 0:n_], 1.0 / DA, VAR[:, 0:n_], ALU.mult, ALU.subtract,
                    [("ps", 7), "VAR"], ["VAR"])
                ts(VAR[:, 0:n_], VAR[:, 0:n_], 0.0, None, ALU.max, None, ["VAR"], ["VAR"])
                act(VAR[:, 0:n_], VAR[:, 0:n_], AF.Sqrt, ["VAR"], ["VAR"], bias=EPS, scale=1.0)
                P.add("dve", lambda e, n_=n_, VAR=VAR: e.reciprocal(out=VAR[:, 0:n_], in_=VAR[:, 0:n_]),
                      reads=["VAR"], writes=["VAR"])
                for ci in range(8):
                    t_ = next_tmp()
                    tt(TMPF[t_][:, 0:n_], CACC[:, ci, o_:o_ + n_], MEAN[:, 0:n_], ALU.subtract,
                       [("CACC", ci), "MEAN"], [("TMPF", t_)])
                    tt(TMPF[t_][:, 0:n_], TMPF[t_][:, 0:n_], VAR[:, 0:n_], ALU.mult, [("TMPF", t_), "VAR"],
                       [("TMPF", t_)])
                    act(YBIN[:, ci, o_:o_ + n_], TMPF[t_][:, 0:n_], AF.Silu, [("TMPF", t_), "PP"], [("YBIN", ci)],
                        bias=PPt[:, l, O_LCB + ci:O_LCB + ci + 1], scale=PPt[:, l, O_LCG + ci:O_LCG + ci + 1])
            gated_out(l, YBIN, "YBIN", 8, lambda c0: (w_b_out[l][:, c0:c0 + 256], 8), GATE0 + D, False,
                      rhs_chunk_fn=lambda mi, k, o_, n_: YBIN[:, k, o_:o_ + n_])

            dying = A.reset(mA)
            LP = 15 + NP
            PBUF = A.alloc("PBUF", [128, 8, LP], F32)
            POOLED = A.alloc("POOLED", [128, 8, TT], BF16)
            PT = [A.alloc(f"PT{i}", [128, 2, LP], F32) for i in range(2)]
            PSC = A.alloc("PSC", [128, 8, 16, 19], F32)
            PTS = [A.alloc(f"PTS{i}", [128, 2, 16, 19], F32) for i in range(2)]
            T16 = A.alloc("T16", [128, 2, 16], F32)
            P.fence(dying, ["PBUF", "POOLED", "PT0", "PT1", "PSC", "PTS0", "PTS1", "T16"])
            for ci in range(8):
                copy(PBUF[:, ci, 0:15], HPOOL[:, l, ci, :], ["HPOOL"], [("PBUF", ci)])
            if HAS_S:
                for ci in range(8):
                    dma(PSC[:, ci, :, 0:15], st_pool[l][ci * 128:(ci + 1) * 128, :].rearrange("p (s r) -> p s r", r=15),
                        writes=[("PSC", ci)])
            for uu in range(2):
                s, (wv,) = load_unit([(w_in[l][:, 4 * DA + uu * 512:4 * DA + (uu + 1) * 512], KC)])
                for m in range(4):
                    ci = uu * 4 + m
                    for (O_, N_, subs_) in mblocks:
                        b = next_bank()
                        chain(PSB[b][:, 0:N_], b, [wv[:, k, m * 128:(m + 1) * 128] for k in range(KC)],
                              [H[:, k, O_:O_ + N_] for k in range(KC)],
                              reads=[("WS", s, 0), ("WS", s, 1)] + [("H", kk) for kk in range(KC)])
                        for (o_, n_, is_s) in subs_:
                            r0 = o_ - O_
                            if not is_s:
                                act(PBUF[:, ci, 15 + o_:15 + o_ + n_], PSB[b][:, r0:r0 + n_], AF.Identity, [("ps", b)],
                                    [("PBUF", ci)])
                            else:
                                act(PSC[:, ci, :, 15:19], s3(PSB[b][:, r0:r0 + n_]), AF.Identity, [("ps", b)],
                                    [("PSC", ci)])
                    if first_super:
                        ts(PBUF[:, ci, 15:15 + HALO], PBUF[:, ci, 15:15 + HALO], HMASK[:, 0:1], None, ALU.mult, None,
                           [("PBUF", ci), "HMASK"], [("PBUF", ci)])
                    copy(HPOOL[:, l, ci, :], PBUF[:, ci, NP:NP + 15], [("PBUF", ci)], ["HPOOL"])
                    if last_super:
                        out_toks.append(dma(o_pool_p[l][ci * 128:(ci + 1) * 128, :], PBUF[:, ci, NP:NP + 15],
                                            reads=[("PBUF", ci)]))
                    if HAS_S:
                        out_toks.append(dma(o_pool_s[l][ci * 128:(ci + 1) * 128, :].rearrange("p (s r) -> p s r", r=15),
                                            PSC[:, ci, :, 4:19], reads=[("PSC", ci)]))
            for gi in range(4):
                w = 2 << gi
                c2 = slice(2 * gi, 2 * gi + 2)
                rk = [("PBUF", 2 * gi), ("PBUF", 2 * gi + 1)]
                src = PBUF[:, c2, :]
                srck = rk
                sh = 1
                lo = 0
                for stp in range(gi + 1):
                    dst = PT[stp % 2]
                    dk = [f"PT{stp % 2}"]
                    nlo = lo + sh
                    tt(dst[:, :, nlo:LP], src[:, :, nlo:LP], src[:, :, lo:LP - sh], ALU.add, srck, dk)
                    src = dst; srck = dk; lo = nlo; sh *= 2
                stt(POOLED[:, c2, 0:NP], src[:, :, 15:15 + NP], 1.0 / w, PBUF[:, c2, 15:15 + NP], ALU.mult,
                    ALU.subtract, srck + rk, [("POOLED", 2 * gi), ("POOLED", 2 * gi + 1)])
                if first_super:
                    a0 = 15 + HALO
                    tt(T16[:], src[:, :, a0:a0 + 16],
                       PCNT[:, gi * 16:(gi + 1) * 16].unsqueeze(1).to_broadcast([128, 2, 16]), ALU.mult,
                       srck + ["PCNT"], ["T16"])
                    tt(POOLED[:, c2, HALO:HALO + 16], T16[:], PBUF[:, c2, a0:a0 + 16], ALU.subtract,
                       ["T16"] + rk, [("POOLED", 2 * gi), ("POOLED", 2 * gi + 1)])
                if HAS_S:
                    rks = [("PSC", 2 * gi), ("PSC", 2 * gi + 1)]
                    src = PSC[:, c2, :, :]
                    srck = rks
                    sh = 1
                    lo = 0
                    for stp in range(gi + 1):
                        dst = PTS[stp % 2]
                        dk = [f"PTS{stp % 2}"]
                        nlo = lo + sh
                        tt(dst[:, :, :, nlo:19], src[:, :, :, nlo:19], src[:, :, :, lo:19 - sh], ALU.add, srck, dk)
                        src = dst; srck = dk; lo = nlo; sh *= 2
                    stt(POOLED[:, c2, NP:NP + NS].rearrange("p c (s t) -> p c s t", t=4), src[:, :, :, 15:19],
                        1.0 / w, PSC[:, c2, :, 15:19], ALU.mult, ALU.subtract, srck + rks,
                        [("POOLED", 2 * gi), ("POOLED", 2 * gi + 1)])
            gated_out(l, POOLED, "POOLED", 2,
                      lambda c0: (w_pool[l][c0 // 512][:, (c0 % 512):(c0 % 512) + 256], 2), GATE0 + 2 * D, False,
                      scale_off=O_PS,
                      rhs_chunk_fn=lambda mi, k, o_, n_: POOLED[:, 2 * (mi // 4) + k, o_:o_ + n_])

            dying = A.reset(mA)
            YMO = A.alloc("YMO", [128, KC, TT], F32)
            P.fence(dying, ["YMO"])

            def proj_out(l, units_fn, nunits, rhs_fn, rhs_keys, kparts):
                for u in range(nunits):
                    banks = {}
                    for kp in range(kparts):
                        src, kc_, koff = units_fn(u, kp)
                        s, (wv,) = load_unit([(src, kc_)])
                        for m in range(2):
                            for bi, (o_, n_, _sb) in enumerate(mblocks):
                                if kp == 0:
                                    banks[(m, bi)] = next_bank()
                                b = banks[(m, bi)]
                                chain(PSB[b][:, 0:n_], b, [wv[:, k, m * 128:(m + 1) * 128] for k in range(kc_)],
                                      [rhs_fn(koff + k, o_, n_) for k in range(kc_)],
                                      reads=[("WS", s, 0), ("WS", s, 1)] + rhs_keys, first=(kp == 0), last=(kp == kparts - 1))
                    for m in range(2):
                        mi = u * 2 + m
                        for bi, (o_, n_, _sb) in enumerate(mblocks):
                            b = banks[(m, bi)]
                            act(YMO[:, mi, o_:o_ + n_], PSB[b][:, 0:n_], AF.Identity, [("ps", b)], [("YMO", mi)])
                            q = next_sq()
                            act(SQ[q][:, 0:n_], PSB[b][:, 0:n_], AF.Square, [("ps", b)], [("SQ", q)])
                            sb_ = 6 + bi
                            mm(PSB[sb_][:, 0:n_], ONES[:],
                               SQ[q][:, 0:n_], mi == 0, mi == KC - 1, ["ONES", ("SQ", q)], sb_)
                for bi, (o_, n_, _sb) in enumerate(mblocks):
                    sb_ = 6 + bi
                    c0_ = 0
                    act(RSTD[:, o_:o_ + n_], PSB[sb_][:, c0_:c0_ + n_], AF.Sqrt, [("ps", sb_)], ["RSTD"],
                        bias=EPS, scale=1.0 / D)
                    P.add("dve", lambda e, o_=o_, n_=n_: e.reciprocal(out=RSTD[:, o_:o_ + n_], in_=RSTD[:, o_:o_ + n_]),
                          reads=["RSTD"], writes=["RSTD"])

            proj_out(l, lambda u, kp: (w_o[l][:, u * 256:(u + 1) * 256], KC, 0), 8,
                     lambda k, o_, n_: MIXACC[:, k, o_:o_ + n_], [("MIXACC", kk) for kk in range(KC)], 1)
            postnorm(l, 32, YMO)

            dying = A.reset(m0)
            ACTB = A.alloc("ACTB", [128, NFC, TT], BF16)
            YMO = A.alloc("YMO", [128, KC, TT], F32)
            GSB = [A.alloc(f"GSB{i}", [128, 2 + NP], F32) for i in range(2)]
            GSS = [A.alloc(f"GSS{i}", [128, 16, 6], F32) for i in range(2)]
            FACC = [A.alloc(f"FACC{i}", [128, TT], F32) for i in range(2)]
            SFF = A.alloc("SFF", [128, NFC, 16, 2], F32)
            OFP = A.alloc("OFP", [128, NFC, 2], F32)
            P.fence(dying, ["ACTB", "YMO", "GSB0", "GSB1", "GSS0", "GSS1", "FACC0", "FACC1", "SFF", "OFP"])
            prenorm(l, 64, 48)
            if HAS_S:
                for j in range(NFC):
                    dma(SFF[:, j, :, :], st_ffn[l][j * 128:(j + 1) * 128, :].rearrange("p (s r) -> p s r", r=2),
                        writes=[("SFF", j)])
            for u in range(22):
                f0 = u * 256
                nf = min(256, DFF - f0)
                s, (wg, wv) = load_unit([(w_up[l][:, f0:f0 + nf], KC), (w_up[l][:, DFF + f0:DFF + f0 + nf], KC)])
                for m in range(nf // 128):
                    j = u * 2 + m
                    gb_ = GSB[j % 2]; gk = f"GSB{j % 2}"
                    gs_ = GSS[j % 2]; gsk = f"GSS{j % 2}"
                    fa_ = FACC[j % 2]; fk = f"FACC{j % 2}"
                    copy(gb_[:, 0:2], HFFN[:, l, j, :], ["HFFN"], [gk])
                    if HAS_S:
                        copy(gs_[:, :, 0:2], SFF[:, j, :, :], [("SFF", j)], [gsk])
                    for (O_, N_, subs_) in mblocks:
                        b1 = next_bank()
                        chain(PSB[b1][:, 0:N_], b1, [wg[:, k, m * 128:(m + 1) * 128] for k in range(KC)],
                              [H[:, k, O_:O_ + N_] for k in range(KC)],
                              reads=[("WS", s, 0), ("WS", s, 1)] + [("H", kk) for kk in range(KC)])
                        for (o_, n_, is_s) in subs_:
                            r0 = o_ - O_
                            if not is_s:
                                act(gb_[:, 2 + o_:2 + o_ + n_], PSB[b1][:, r0:r0 + n_], AF.Identity, [("ps", b1)], [gk])
                            else:
                                act(gs_[:, :, 2:6], s3(PSB[b1][:, r0:r0 + n_]), AF.Identity, [("ps", b1)], [gsk])
                    if first_super:
                        ts(gb_[:, 2:2 + HALO], gb_[:, 2:2 + HALO], HMASK[:, 0:1], None, ALU.mult, None,
                           [gk, "HMASK"], [gk])
                    wc = lambda tap: PPt[:, l, O_WFC + tap * NFC + j:O_WFC + tap * NFC + j + 1]
                    bcol = PPt[:, l, O_BFC + j:O_BFC + j + 1]
                    ts(fa_[:, 0:NP], gb_[:, 0:NP], wc(0), bcol, ALU.mult, ALU.add, [gk, "PP"], [fk])
                    for tap in (1, 2):
                        stt(fa_[:, 0:NP], gb_[:, tap:tap + NP], wc(tap), fa_[:, 0:NP], ALU.mult, ALU.add,
                            [gk, "PP", fk], [fk])
                    copy(HFFN[:, l, j, :], gb_[:, NP:NP + 2], [gk], ["HFFN"])
                    if last_super:
                        copy(OFP[:, j, :], gb_[:, NP:NP + 2], [gk], ["OFP"])
                    if HAS_S:
                        fs_ = s3(fa_[:, NP:NP + NS])
                        ts(fs_, gs_[:, :, 0:4], wc(0), bcol, ALU.mult, ALU.add, [gsk, "PP"], [fk])
                        for tap in (1, 2):
                            stt(fs_, gs_[:, :, tap:tap + 4], wc(tap), fs_, ALU.mult, ALU.add, [gsk, "PP", fk], [fk])
                        copy(SFF[:, j, :, :], gs_[:, :, 4:6], [gsk], [("SFF", j)])
                    act(fa_[:, 0:TT], fa_[:, 0:TT], AF.Gelu_apprx_tanh, [fk], [fk])
                    for (o_, n_, _sb) in mblocks:
                        b2 = next_bank()
                        chain(PSB[b2][:, 0:n_], b2, [wv[:, k, m * 128:(m + 1) * 128] for k in range(KC)],
                              [H[:, k, o_:o_ + n_] for k in range(KC)],
                              reads=[("WS", s, 0), ("WS", s, 1)] + [("H", kk) for kk in range(KC)])
                        tt(ACTB[:, j, o_:o_ + n_], PSB[b2][:, 0:n_], fa_[:, o_:o_ + n_], ALU.mult,
                           [("ps", b2), fk], [("ACTB", j)])
            if last_super:
                out_toks.append(dma(o_ffn_p[l].rearrange("(j p) r -> p j r", p=128), OFP[:], reads=["OFP"]))
            if HAS_S:
                out_toks.append(dma(o_ffn_s[l].rearrange("(j p) (s r) -> p j s r", p=128, r=2), SFF[:],
                                    reads=[("SFF", j) for j in range(NFC)]))
            KH = [(0, 22), (22, 21)]
            proj_out(l, lambda u, kp: (w_down[l][KH[kp][0] * 128:(KH[kp][0] + KH[kp][1]) * 128, u * 256:(u + 1) * 256],
                                       KH[kp][1], KH[kp][0]), 8,
                     lambda k, o_, n_: ACTB[:, k, o_:o_ + n_], [("ACTB", jj) for jj in range(NFC)], 2)
            postnorm(l, 80, YMO)
            dying = A.reset(m0)
            P.fence(dying, ["MIXACC", "VN", "U", "VF", "LNG", "LNB"])

        for k in range(KC):
            c_lo = max(P0, HALO)
            if c_lo < P0 + NP:
                out_toks.append(dma(yp[k * 128:(k + 1) * 128, c_lo - HALO:P0 + NP - HALO],
                                    X[:, k, c_lo - P0:NP], reads=[("X", k)]))
            if HAS_S:
                out_toks.append(dma(ys[k * 128:(k + 1) * 128, :], X[:, k, NP:NP + NS], reads=[("X", k)]))

    run = P.make_runner(sems, lanes, final_waits=out_toks)
    with nc.Block() as block:
        @block.sync
        def _(e):
            run("sp", e)

        @block.tensor
        def _(e):
            run("pe", e)

        @block.scalar
        def _(e):
            run("act", e)

        @block.vector
        def _(e):
            run("dve", e)

        @block.gpsimd
        def _(e):
            run("pool", e)
    es.close()
    return nc, A.peak


def _chunkT(v):
    return np.ascontiguousarray(v.reshape(-1, 128).T)


def kernel(x_prompt, x_sample, c_prompt, c_sample, state_conv, state_pool, state_ffn_conv,
           ada_w, ada_b, g_pre_mix, g_post_mix, g_pre_ffn, g_post_ffn, w_in, b_gate,
           ln_v_g, ln_v_b, w_spatial, b_spatial, w_a_out, w_dwconv, b_dwconv, ln_conv_g,
           ln_conv_b, w_b_out, w_pool_grp, pool_scale, w_o, w_up, w_ffn_conv, b_ffn_conv, w_down):
    f = lambda a: np.ascontiguousarray(np.asarray(a, dtype=np.float32))
    x_prompt, x_sample, c_prompt, c_sample = f(x_prompt), f(x_sample), f(c_prompt), f(c_sample)
    state_conv, state_pool, state_ffn_conv = f(state_conv), f(state_pool), f(state_ffn_conv)
    w_spatial = f(w_spatial); b_spatial = f(b_spatial)

    pp = np.zeros((DEPTH, 128, NPP), np.float32)
    for l in range(DEPTH):
        pp[l, :, O_GPM:O_GPM + 16] = _chunkT(f(g_pre_mix)[l])
        pp[l, :, O_GQM:O_GQM + 16] = _chunkT(f(g_post_mix)[l])
        pp[l, :, O_GPF:O_GPF + 16] = _chunkT(f(g_pre_ffn)[l])
        pp[l, :, O_GQF:O_GQF + 16] = _chunkT(f(g_post_ffn)[l])
        pp[l, :, O_BG:O_BG + 48] = _chunkT(f(b_gate)[l])
        pp[l, :, O_PS:O_PS + 16] = _chunkT(f(pool_scale)[l])
        pp[l, :, O_BDW:O_BDW + 8] = _chunkT(f(b_dwconv)[l])
        pp[l, :, O_LCG:O_LCG + 8] = _chunkT(f(ln_conv_g)[l])
        pp[l, :, O_LCB:O_LCB + 8] = _chunkT(f(ln_conv_b)[l])
        for tap in range(31):
            pp[l, :, O_WDW + tap * 8:O_WDW + tap * 8 + 8] = _chunkT(f(w_dwconv)[l, tap])
        for tap in range(3):
            pp[l, :, O_WFC + tap * NFC:O_WFC + (tap + 1) * NFC] = _chunkT(f(w_ffn_conv)[l, tap])
        pp[l, :, O_BFC:O_BFC + NFC] = _chunkT(f(b_ffn_conv)[l])
        pp[l, :, O_ADB:O_ADB + 96] = _chunkT(f(ada_b)[l])
    wsT = np.ascontiguousarray(w_spatial.transpose(0, 3, 1, 2)).reshape(DEPTH, 128, 8 * 128)
    bsp = np.ascontiguousarray(b_spatial.reshape(DEPTH, 8 * 128))
    w4 = w_spatial[:, :, 0:4, 0:4].transpose(0, 3, 1, 2)
    ws4 = np.ascontiguousarray(np.tile(w4.reshape(DEPTH, 1, 4, 32), (1, 16, 1, 1)).reshape(DEPTH, 64, 32))
    bs4 = np.ascontiguousarray(np.tile(b_spatial[:, :, None, 0:4], (1, 1, 16, 1)).reshape(DEPTH, 8 * 64))
    tril = np.triu(np.ones((128, 128), np.float32))
    bdm = np.zeros((64, 16, 4), np.float32)
    for sq in range(16):
        for s_ in range(4):
            bdm[sq * 4 + s_, sq, s_:] = 1.0
    bdm = bdm.reshape(64, 64)

    shared = dict(pp=pp, lnvg=f(ln_v_g), lnvb=f(ln_v_b), wsT=wsT, bsp=bsp, ws4=ws4, bs4=bs4, tril=tril,
                  bdmask=bdm, ada_w=f(ada_w), w_in=f(w_in), w_a_out=f(w_a_out), w_b_out=f(w_b_out),
                  w_pool=f(w_pool_grp), w_o=f(w_o), w_up=f(w_up), w_down=f(w_down))
    in_maps = []
    for c in range(NCORES):
        seq, half = c // 2, c % 2
        xpc = np.zeros((NPROMPT, D), np.float32)
        if half == 0:
            xpc[HALO:] = x_prompt[seq, 0:1024]
        else:
            xpc = x_prompt[seq, 1024 - HALO:2048]
        ss = slice(c * 16, (c + 1) * 16)
        cT = np.concatenate([c_prompt[seq][None, :], c_sample[ss]], axis=0).T
        pcnt = np.zeros((4, 16), np.float32)
        for gi, w in enumerate((2, 4, 8, 16)):
            for i in range(16):
                pos = half * 1024 + i
                pcnt[gi, i] = 1.0 / min(pos + 1, w)
        m = dict(shared)
        m.update(
            xp=np.ascontiguousarray(xpc.T), xs=np.ascontiguousarray(x_sample[ss].reshape(NS, D).T),
            cT=np.ascontiguousarray(cT), hmask=np.full((128, 1), float(half), np.float32),
            pcnt=np.ascontiguousarray(np.tile(pcnt.reshape(1, 64), (128, 1))),
            st_conv=np.ascontiguousarray(state_conv[:, ss].transpose(0, 3, 1, 2)).reshape(DEPTH, DA, 16 * 30),
            st_pool=np.ascontiguousarray(state_pool[:, ss].transpose(0, 3, 1, 2)).reshape(DEPTH, DA, 16 * 15),
            st_ffn=np.ascontiguousarray(state_ffn_conv[:, ss].transpose(0, 3, 1, 2)).reshape(DEPTH, DFF, 16 * 2),
        )
        in_maps.append(m)

    nc, _ = build_program()
    res = run_bass_kernel_spmd(nc, in_maps, core_ids=list(range(NCORES)))
    R = res.results

    y_prompt = np.zeros((4, 2048, D), np.float32)
    y_sample = np.zeros((128, 4, D), np.float32)
    conv_p = np.zeros((DEPTH, 4, 30, DA), np.float32); conv_s = np.zeros((DEPTH, 128, 30, DA), np.float32)
    pool_p = np.zeros((DEPTH, 4, 15, DA), np.float32); pool_s = np.zeros((DEPTH, 128, 15, DA), np.float32)
    ffn_p = np.zeros((DEPTH, 4, 2, DFF), np.float32); ffn_s = np.zeros((DEPTH, 128, 2, DFF), np.float32)
    v_s = np.zeros((DEPTH, 128, 4, DA), np.float32)
    for c in range(NCORES):
        seq, half = c // 2, c % 2
        ss = slice(c * 16, (c + 1) * 16)
        r = R[c]
        y_prompt[seq, half * 1024:(half + 1) * 1024] = r["yp"].T
        y_sample[ss] = r["ys"].T.reshape(16, 4, D)
        conv_s[:, ss] = r["o_conv_s"].reshape(DEPTH, DA, 16, 30).transpose(0, 2, 3, 1)
        pool_s[:, ss] = r["o_pool_s"].reshape(DEPTH, DA, 16, 15).transpose(0, 2, 3, 1)
        ffn_s[:, ss] = r["o_ffn_s"].reshape(DEPTH, DFF, 16, 2).transpose(0, 2, 3, 1)
        v_s[:, ss] = r["o_v_s"].reshape(DEPTH, 16, 4, DA)
        if half == 1:
            conv_p[:, seq] = r["o_conv_p"].transpose(0, 2, 1)
            pool_p[:, seq] = r["o_pool_p"].transpose(0, 2, 1)
            ffn_p[:, seq] = r["o_ffn_p"].transpose(0, 2, 1)
    return (y_prompt, y_sample, conv_p, conv_s, pool_p, pool_s, ffn_p, ffn_s, v_s)
```
